# Optimizing a Trainium2 kernel written in Bass

```python
import math
import jax, jax.numpy as jnp
from jax import lax
import numpy as np

D_MODEL = 4096
BATCH = 2
SEQ = 8192
DEPTH = 2

CHUNK = 64
EPS = 1e-6
MASK_VALUE = -1e30
TINY = 1e-30

MLA_HEADS = 16
MLA_Q_RANK = 768
MLA_KV_RANK = 512
MLA_NOPE = 128
MLA_ROPE = 64
MLA_V = 128
ROPE_THETA = 10000.0
Q_BLOCK = 128

HG_HEADS = 8
HG_KDIM = 128
HG_VDIM = 128
HG_CHUNK = 16

CA_HEADS = 8
CA_HEAD_DIM = 128
CA_LEFT_CHUNKS = 8
CA_REL_CLIP = 256

MLA_WIDTH = MLA_HEADS * MLA_V
HG_WIDTH = HG_HEADS * HG_VDIM
CA_WIDTH = CA_HEADS * CA_HEAD_DIM
MIX_WIDTH = MLA_WIDTH + HG_WIDTH + CA_WIDTH
D_FF = ((8 * D_MODEL + 3 * 256 - 1) // (3 * 256)) * 256

IN_SIZES = (MLA_Q_RANK, MLA_KV_RANK, MLA_ROPE,
            HG_HEADS * HG_KDIM, HG_HEADS * HG_KDIM, HG_WIDTH, HG_WIDTH,
            CA_WIDTH, CA_WIDTH, CA_WIDTH)
D_IN = sum(IN_SIZES)
IN_OFFSETS = tuple(int(o) for o in np.cumsum(IN_SIZES)[:-1])

kernel_name = "hybrid_mla_hgrn2_chunkattn_block"


def rms_norm(x, g):
    xf = x.astype(jnp.float32)
    y = xf * lax.rsqrt(jnp.mean(xf * xf, axis=-1, keepdims=True) + EPS)
    return (y * g.astype(jnp.float32)).astype(x.dtype)


def rope_tables(positions):
    half = MLA_ROPE // 2
    inv_freq = jnp.exp(-math.log(ROPE_THETA) * 2.0 * jnp.arange(half, dtype=jnp.float32) / MLA_ROPE)
    ang = positions.astype(jnp.float32)[..., None] * inv_freq
    return jnp.cos(ang)[:, :, None, :], jnp.sin(ang)[:, :, None, :]


def apply_rope(t, cos, sin):
    half = t.shape[-1] // 2
    tf = t.astype(jnp.float32)
    t1, t2 = tf[..., :half], tf[..., half:]
    return jnp.concatenate([t1 * cos - t2 * sin, t1 * sin + t2 * cos], axis=-1).astype(t.dtype)


def mla_mixer(c_q, c_kv, k_rope, q_norm, kv_norm, w_uq, w_ukv, out_norm, cos, sin):
    B, S, _ = c_q.shape
    q = (rms_norm(c_q, q_norm) @ w_uq).reshape(B, S, MLA_HEADS, MLA_NOPE + MLA_ROPE)
    kv = (rms_norm(c_kv, kv_norm) @ w_ukv).reshape(B, S, MLA_HEADS, MLA_NOPE + MLA_V)
    q_nope = q[..., :MLA_NOPE]
    q_pe = apply_rope(q[..., MLA_NOPE:], cos, sin)
    k_nope, v = kv[..., :MLA_NOPE], kv[..., MLA_NOPE:]
    k_pe = apply_rope(k_rope[:, :, None, :], cos, sin)[:, :, 0, :]
    scale = (MLA_NOPE + MLA_ROPE) ** -0.5
    nb = S // Q_BLOCK
    qn_blocks = q_nope.reshape(B, nb, Q_BLOCK, MLA_HEADS, MLA_NOPE).transpose(1, 0, 2, 3, 4)
    qp_blocks = q_pe.reshape(B, nb, Q_BLOCK, MLA_HEADS, MLA_ROPE).transpose(1, 0, 2, 3, 4)
    key_chunk = jnp.arange(S) // CHUNK

    def one_block(args):
        qn, qp, blk = args
        s = (jnp.einsum('bqhd,bkhd->bhqk', qn, k_nope)
             + jnp.einsum('bqhr,bkr->bhqk', qp, k_pe)).astype(jnp.float32) * scale
        q_chunk = (blk * Q_BLOCK + jnp.arange(Q_BLOCK)) // CHUNK
        mask = key_chunk[None, :] <= q_chunk[:, None]
        s = jnp.where(mask[None, None], s, MASK_VALUE)
        p = jax.nn.softmax(s, axis=-1).astype(v.dtype)
        return jnp.einsum('bhqk,bkhd->bqhd', p, v)

    o = lax.map(one_block, (qn_blocks, qp_blocks, jnp.arange(nb)))
    o = o.transpose(1, 0, 2, 3, 4).reshape(B, S, MLA_WIDTH)
    return rms_norm(o, out_norm)


def hgrn2_mixer(q, f_pre, i, g, lb, out_norm):
    B, S, _ = q.shape
    L = HG_CHUNK
    nc = S // L
    dtype = q.dtype
    q = jax.nn.silu(q).astype(jnp.float32)
    fp = f_pre.astype(jnp.float32)
    f = lb + (1.0 - lb) * jax.nn.sigmoid(fp)
    log_f = jnp.log(jnp.maximum(f, TINY))
    k = (1.0 - lb) * jax.nn.sigmoid(-fp)

    def heads(t, d):
        return t.reshape(B, nc, L, HG_HEADS, d).transpose(0, 3, 1, 2, 4)

    qh, kh, lfh = heads(q, HG_KDIM), heads(k, HG_KDIM), heads(log_f, HG_KDIM)
    vh = heads(i.astype(jnp.float32), HG_VDIM)
    b = jnp.cumsum(lfh, axis=3)
    causal = jnp.tril(jnp.ones((L, L), dtype=bool))[:, :, None]
    diff = b[..., :, None, :] - b[..., None, :, :]
    decay = jnp.where(causal, jnp.exp(jnp.where(causal, diff, 0.0)), 0.0)
    A = jnp.sum(qh[..., :, None, :] * kh[..., None, :, :] * decay, axis=-1)
    o_intra = jnp.einsum('bhnij,bhnjv->bhniv', A, vh)

    b_last = b[..., -1:, :]
    k_dec = kh * jnp.exp(b_last - b)
    q_dec = qh * jnp.exp(b)
    chunk_decay = jnp.exp(b_last[..., 0, :])

    def step(state, xs):
        qd, kd, vc, cd = xs
        out = jnp.einsum('bhlk,bhkv->bhlv', qd, state)
        state = cd[..., None] * state + jnp.einsum('bhlk,bhlv->bhkv', kd, vc)
        return state, out

    xs = (jnp.moveaxis(q_dec, 2, 0), jnp.moveaxis(k_dec, 2, 0),
          jnp.moveaxis(vh, 2, 0), jnp.moveaxis(chunk_decay, 2, 0))
    state0 = jnp.zeros((B, HG_HEADS, HG_KDIM, HG_VDIM), jnp.float32)
    _, o_inter = lax.scan(step, state0, xs)
    o = o_intra + jnp.moveaxis(o_inter, 0, 2)
    o = o.transpose(0, 2, 3, 1, 4).reshape(B, S, HG_HEADS, HG_VDIM).astype(dtype)
    o = rms_norm(o, out_norm.reshape(HG_HEADS, HG_VDIM)).reshape(B, S, HG_WIDTH)
    return o * jax.nn.silu(g)


def chunk_attn_mixer(q, k, v, rel_bias, out_norm):
    B, S, _ = q.shape
    nc = S // CHUNK
    W = CA_LEFT_CHUNKS + 1

    def heads(t):
        return t.reshape(B, nc, CHUNK, CA_HEADS, CA_HEAD_DIM)

    qc, kc, vc = heads(q), heads(k), heads(v)
    pad = ((0, 0), (CA_LEFT_CHUNKS, 0), (0, 0), (0, 0), (0, 0))
    kp, vp = jnp.pad(kc, pad), jnp.pad(vc, pad)
    k_band = jnp.concatenate([kp[:, w:w + nc] for w in range(W)], axis=2)
    v_band = jnp.concatenate([vp[:, w:w + nc] for w in range(W)], axis=2)
    s = jnp.einsum('bnqhd,bnkhd->bhnqk', qc, k_band).astype(jnp.float32) * (CA_HEAD_DIM ** -0.5)
    a = jnp.arange(CHUNK)
    kidx = jnp.arange(W * CHUNK)
    dist = (CA_LEFT_CHUNKS * CHUNK + a[:, None]) - kidx[None, :]
    bucket = jnp.clip(dist, -CA_REL_CLIP, CA_REL_CLIP) + CA_REL_CLIP
    bias = rel_bias[:, bucket].astype(jnp.float32)
    key_chunk = jnp.arange(nc)[:, None] - CA_LEFT_CHUNKS + kidx[None, :] // CHUNK
    valid = key_chunk >= 0
    s = s + bias[None, :, None]
    s = jnp.where(valid[None, None, :, None, :], s, MASK_VALUE)
    p = jax.nn.softmax(s, axis=-1).astype(v.dtype)
    o = jnp.einsum('bhnqk,bnkhd->bnqhd', p, v_band).reshape(B, S, CA_WIDTH)
    return rms_norm(o, out_norm)


def setup_inputs(seed: int = 0) -> dict:
    key = jax.random.key(seed)
    ks = jax.random.split(key, 24)
    f32 = jnp.float32

    def w(k, shape, fan_in):
        return jax.random.normal(k, shape, f32) * (fan_in ** -0.5)

    def gain(k, shape):
        return 1.0 + 0.05 * jax.random.normal(k, shape, f32)

    x = jax.random.normal(ks[0], (BATCH, SEQ, D_MODEL), f32)
    offset = jax.random.randint(ks[1], (BATCH, 1), 0, 4096, dtype=jnp.int32)
    positions = (offset + jnp.arange(SEQ, dtype=jnp.int32)[None, :]).astype(jnp.int32)
    return {
        "x": x,
        "positions": positions,
        "attn_pre_norm": gain(ks[2], (DEPTH, D_MODEL)),
        "attn_post_norm": gain(ks[3], (DEPTH, D_MODEL)),
        "w_in": w(ks[4], (DEPTH, D_MODEL, D_IN), D_MODEL),
        "mla_q_norm": gain(ks[5], (DEPTH, MLA_Q_RANK)),
        "mla_kv_norm": gain(ks[6], (DEPTH, MLA_KV_RANK)),
        "w_uq": w(ks[7], (DEPTH, MLA_Q_RANK, MLA_HEADS * (MLA_NOPE + MLA_ROPE)), MLA_Q_RANK),
        "w_ukv": w(ks[8], (DEPTH, MLA_KV_RANK, MLA_HEADS * (MLA_NOPE + MLA_V)), MLA_KV_RANK),
        "mla_out_norm": gain(ks[9], (DEPTH, MLA_WIDTH)),
        "hg_lower_bounds": jax.random.normal(ks[10], (DEPTH, HG_HEADS * HG_KDIM), f32),
        "hg_out_norm": gain(ks[11], (DEPTH, HG_WIDTH)),
        "ca_rel_bias": 0.5 * jax.random.normal(ks[12], (DEPTH, CA_HEADS, 2 * CA_REL_CLIP + 1), f32),
        "ca_out_norm": gain(ks[13], (DEPTH, CA_WIDTH)),
        "w_out": w(ks[14], (DEPTH, MIX_WIDTH, D_MODEL), MIX_WIDTH),
        "ffn_pre_norm": gain(ks[15], (DEPTH, D_MODEL)),
        "ffn_post_norm": gain(ks[16], (DEPTH, D_MODEL)),
        "w_gate": w(ks[17], (DEPTH, D_MODEL, D_FF), D_MODEL),
        "w_up": w(ks[18], (DEPTH, D_MODEL, D_FF), D_MODEL),
        "w_down": w(ks[19], (DEPTH, D_FF, D_MODEL), D_FF),
    }


def reference(x, positions, attn_pre_norm, attn_post_norm, w_in, mla_q_norm, mla_kv_norm,
              w_uq, w_ukv, mla_out_norm, hg_lower_bounds, hg_out_norm, ca_rel_bias,
              ca_out_norm, w_out, ffn_pre_norm, ffn_post_norm, w_gate, w_up, w_down):
    cos, sin = rope_tables(positions)
    p = jax.nn.softmax(hg_lower_bounds.astype(jnp.float32), axis=0)
    lower_bounds = jnp.cumsum(p, axis=0) - p[0]
    for l in range(DEPTH):
        h = rms_norm(x, attn_pre_norm[l]) @ w_in[l]
        cq, ckv, kr, hq, hf, hi, hg, aq, ak, av = jnp.split(h, IN_OFFSETS, axis=-1)
        o_mla = mla_mixer(cq, ckv, kr, mla_q_norm[l], mla_kv_norm[l], w_uq[l], w_ukv[l],
                          mla_out_norm[l], cos, sin)
        o_hg = hgrn2_mixer(hq, hf, hi, hg, lower_bounds[l], hg_out_norm[l])
        o_ca = chunk_attn_mixer(aq, ak, av, ca_rel_bias[l], ca_out_norm[l])
        y = jnp.concatenate([o_mla, o_hg.astype(o_mla.dtype), o_ca], axis=-1) @ w_out[l]
        x = x + rms_norm(y, attn_post_norm[l])
        hf2 = rms_norm(x, ffn_pre_norm[l])
        y = (jax.nn.silu(hf2 @ w_gate[l]) * (hf2 @ w_up[l])) @ w_down[l]
        x = x + rms_norm(y, ffn_post_norm[l])
    return x
```

```python
import math
import numpy as np
import concourse.bass as bass
import concourse.mybir as mybir
from concourse.bass_utils import run_bass_kernel_spmd
from contextlib import ExitStack

F32 = mybir.dt.float32
BF16 = mybir.dt.bfloat16
I32 = mybir.dt.int32
AF = mybir.ActivationFunctionType
ALU = mybir.AluOpType

NCORES = 8
DEBUG = {}
D_MODEL = 4096
BATCH = 2
SEQ = 8192
DEPTH = 2
EPS = 1e-6
TINY = 1e-30
MLA_HEADS = 16
QR = 768
KVR = 512
ROPE = 64
D_FF = 11008
D_IN = 8512
NFM = 768 + 512 + 128 + 1024 + 1024 + 1024 + 1024
NTM = 3072
TWO_PI = 2.0 * math.pi


class Buf:
    __slots__ = ("w", "r", "name")

    def __init__(self, name=""):
        self.w = None
        self.r = []
        self.name = name


class TileH:
    def __init__(self, t, name):
        self.t = t
        self.b = Buf(name)


class Sched:
    ENG = ("pe", "act", "dve", "pool", "sp")

    def __init__(self, nc, stack, n_dma_sems=48):
        self.nc = nc
        self.prog = {e: [] for e in self.ENG}
        self.sem = {e: stack.enter_context(nc.semaphore("s_" + e)) for e in self.ENG}
        self.cnt = {e: 0 for e in self.ENG}
        self.seen = {e: {} for e in self.ENG}
        self.dsem = [stack.enter_context(nc.semaphore("d%d" % i)) for i in range(n_dma_sems)]
        self.dtot = [0] * n_dma_sems
        nsp = (2 * n_dma_sems) // 3
        self.dpool = {"sp": list(range(0, nsp)), "pool": list(range(nsp, n_dma_sems))}
        self.dnext = {"sp": 0, "pool": 0}
        self.ninst = 0

    def _waits(self, e, reads, writes, extra=()):
        d = {}

        def add(ev):
            if ev is None:
                return
            k = ev[0]
            if k not in d or d[k][1] < ev[1]:
                d[k] = ev
        for b in reads:
            add(b.w)
        for b in writes:
            add(b.w)
            for ev in b.r:
                add(ev)
        for ev in extra:
            add(ev)
        waits = []
        for k, ev in d.items():
            if e == "pe" and k is self.sem["pe"]:
                continue
            if self.seen[e].get(k, 0) < ev[1]:
                self.seen[e][k] = ev[1]
                waits.append(ev)
        return waits

    def _commit(self, ev, reads, writes):
        for b in writes:
            b.w = ev
            b.r = []
        for b in reads:
            b.r.append(ev)
            if len(b.r) > 64:
                m = {}
                for x in b.r:
                    if x[0] not in m or m[x[0]][1] < x[1]:
                        m[x[0]] = x
                b.r = list(m.values())

    def op(self, e, fn, reads=(), writes=(), inc=True):
        waits = self._waits(e, reads, writes)
        if inc:
            self.cnt[e] += 1
            ev = (self.sem[e], self.cnt[e])
            self.prog[e].append((waits, fn, self.sem[e], 1))
            self._commit(ev, reads, writes)
        else:
            self.prog[e].append((waits, fn, None, 0))
        self.ninst += 1

    def dma(self, q, out, in_, reads=(), writes=()):
        lst = self.dpool[q]
        k = lst[self.dnext[q] % len(lst)]
        self.dnext[q] += 1
        extra = []
        if self.dtot[k] > 0:
            extra.append((self.dsem[k], self.dtot[k]))
        waits = self._waits(q, reads, writes, extra)
        self.dtot[k] += 16
        ev = (self.dsem[k], self.dtot[k])
        self.prog[q].append((waits, lambda eng, o=out, i=in_: eng.dma_start(out=o, in_=i), self.dsem[k], 16))
        self._commit(ev, reads, writes)
        self.ninst += 1

    def barrier(self):
        evs = [(self.sem[e], self.cnt[e]) for e in self.ENG if self.cnt[e] > 0]
        evs += [(self.dsem[k], t) for k, t in enumerate(self.dtot) if t > 0]
        for e in self.ENG:
            waits = []
            for ev in evs:
                if self.seen[e].get(ev[0], 0) < ev[1]:
                    self.seen[e][ev[0]] = ev[1]
                    waits.append(ev)
            if waits:
                self.prog[e].append((waits, None, None, 0))

    def finish(self):
        self.barrier()

    def emit(self):
        progs = self.prog

        def run(eng, lst):
            for waits, fn, sem, inc in lst:
                for (s, v) in waits:
                    eng.wait_ge(s, v)
                if fn is not None:
                    ins = fn(eng)
                    if sem is not None:
                        ins.then_inc(sem, inc)

        with self.nc.Block() as block:
            @block.sync
            def _(eng):
                run(eng, progs["sp"])

            @block.tensor
            def _(eng):
                run(eng, progs["pe"])

            @block.scalar
            def _(eng):
                run(eng, progs["act"])

            @block.vector
            def _(eng):
                run(eng, progs["dve"])

            @block.gpsimd
            def _(eng):
                run(eng, progs["pool"])


_UID = [0]


class Ctx:
    def __init__(self, nc, st):
        self.nc = nc
        self.st = st

    def sb(self, name, shape, dt):
        _UID[0] += 1
        return TileH(self.st.enter_context(self.nc.sbuf_tensor("%s_%d" % (name, _UID[0]), list(shape), dt)), name)

    def ps(self, name, shape, dt=F32):
        _UID[0] += 1
        return TileH(self.st.enter_context(self.nc.psum_tensor("%s_%d" % (name, _UID[0]), list(shape), dt)), name)


class RR:
    def __init__(self, tiles):
        self.tiles = tiles
        self.i = 0

    def next(self):
        t = self.tiles[self.i % len(self.tiles)]
        self.i += 1
        return t


def rstd(S, out_ap, out_b, in_ap, in_b, inv_n):
    S.op("act", lambda e: e.activation(out_ap, in_ap, AF.Sqrt, bias=EPS, scale=inv_n), reads=[in_b], writes=[out_b])
    S.op("dve", lambda e: e.reciprocal(out_ap, out_ap), reads=[out_b], writes=[out_b])


def evac(S, flip, out_ap, in_ap, reads, writes):
    flip[0] ^= 1
    if flip[0]:
        S.op("act", lambda e: e.activation(out_ap, in_ap, AF.Copy), reads=reads, writes=writes)
    else:
        S.op("dve", lambda e: e.tensor_copy(out_ap, in_ap), reads=reads, writes=writes)


class Front:
    def __init__(self, S, cx, D, ident, gpc, ps_t, nbuf=2):
        self.S, self.D = S, D
        self.xs = RR([cx.sb("xs", [128, D], F32) for _ in range(nbuf)])
        self.junk = cx.sb("junk", [128, 1024], BF16)
        self.ssq = cx.sb("ssq", [128, 4], F32)
        self.r = cx.sb("r", [128, 1], F32)
        self.diag = cx.sb("diag", [128, 128], F32)
        self.ident, self.gpc, self.ps_t = ident, gpc, ps_t
        self.flip = [0]

    def run(self, x_rows, s, xnT, groups=None):
        S, D = self.S, self.D
        xs, junk, ssq, r, diag = self.xs.next(), self.junk, self.ssq, self.r, self.diag
        ident, gpc = self.ident, self.gpc
        S.dma("sp", xs.t[:, :], x_rows, writes=[xs.b])
        if groups is None:
            groups = [(0, D, True)]
        for (lo, hi, do_norm) in groups:
            w = hi - lo
            if do_norm:
                npc = w // 1024
                for q in range(npc):
                    S.op("act", lambda e, a=lo + q * 1024, q=q: e.activation(junk.t[:, :], xs.t[:, a:a + 1024], AF.Square, accum_out=ssq.t[:, q:q + 1]),
                         reads=[xs.b], writes=[junk.b, ssq.b])
                if npc > 1:
                    S.op("dve", lambda e, npc=npc: e.tensor_reduce(ssq.t[:, 0:1], ssq.t[:, 0:npc], mybir.AxisListType.X, ALU.add), reads=[ssq.b], writes=[ssq.b])
                rstd(S, r.t[:, 0:1], r.b, ssq.t[:, 0:1], ssq.b, 1.0 / w)
                S.op("dve", lambda e: e.tensor_scalar(diag.t[:, :], ident.t[:, :], r.t[:, 0:1], None, ALU.mult),
                     reads=[ident.b, r.b], writes=[diag.b])
                dg = diag
            else:
                dg = ident
            for cg in range(lo // 512, hi // 512):
                pt = self.ps_t.next()
                for k in range(4):
                    c = cg * 4 + k
                    S.op("pe", lambda e, c=c, k=k, pt=pt, dg=dg: e.matmul(pt.t[:, k * 128:(k + 1) * 128], xs.t[:, c * 128:(c + 1) * 128], dg.t[:, :], start=True, stop=True),
                         reads=[xs.b, dg.b], writes=[pt.b], inc=(k == 3))
                self.flip[0] ^= 1
                for k in range(4):
                    c = cg * 4 + k
                    if self.flip[0]:
                        S.op("act", lambda e, c=c, k=k, pt=pt: e.activation(xnT.t[:, c, s * 128:(s + 1) * 128], pt.t[:, k * 128:(k + 1) * 128], AF.Copy, scale=gpc.t[:, c:c + 1]),
                             reads=[pt.b, gpc.b], writes=[xnT.b])
                    else:
                        S.op("dve", lambda e, c=c, k=k, pt=pt: e.tensor_scalar(xnT.t[:, c, s * 128:(s + 1) * 128], pt.t[:, k * 128:(k + 1) * 128], gpc.t[:, c:c + 1], None, ALU.mult),
                             reads=[pt.b, gpc.b], writes=[xnT.b])


def mm_fm(S, wq, nchunk, wch_rr, w_dram, col, xnT, TT, ps_rr, ncols=128):
    wch = wch_rr.next()
    S.dma(wq, wch.t[:, 0:nchunk, 0:ncols], w_dram[:, col:col + ncols].rearrange("(c p) n -> p c n", p=128), writes=[wch.b])
    pt = ps_rr.next()
    for c in range(nchunk):
        S.op("pe", lambda e, c=c, pt=pt, wch=wch: e.matmul(pt.t[0:ncols, 0:TT], wch.t[:, c, 0:ncols], xnT.t[:, c, 0:TT], start=(c == 0), stop=(c == nchunk - 1)),
             reads=[wch.b, xnT.b], writes=[pt.b], inc=(c == nchunk - 1))
    return pt


def phase_A(nc, S, T, x, gbc_d, wfm, wtm, ident_d, hT, htm, nfm=NFM, ntm=NTM, D=D_MODEL):
    TT = 512
    NC = D // 128
    with ExitStack() as st:
        cx = Ctx(nc, st)
        ident = cx.sb("ident", [128, 128], F32)
        gbc = cx.sb("gpc", [128, NC], F32)
        S.dma("sp", ident.t[:, :], ident_d, writes=[ident.b])
        S.dma("sp", gbc.t[:, :], gbc_d, writes=[gbc.b])
        ps_t = RR([cx.ps("ps_t", [128, 512]) for _ in range(2)])
        ps_m = RR([cx.ps("ps_m", [128, 512]) for _ in range(4)])
        fr = Front(S, cx, D, ident, gbc, ps_t)
        xnT = cx.sb("xnT", [128, NC, TT], BF16)
        wch = RR([cx.sb("wch", [128, NC, 256], BF16) for _ in range(3)])
        stg = RR([cx.sb("stg", [128, 512], F32) for _ in range(4)])
        flip = [0]
        for tt in range(T // TT):
            for s in range(4):
                fr.run(x[tt * TT + s * 128: tt * TT + (s + 1) * 128, :], s, xnT)
            for j in range(nfm // 128):
                pt = mm_fm(S, "pool", NC, wch, wfm, j * 128, xnT, TT, ps_m)
                sg = stg.next()
                evac(S, flip, sg.t[:, :], pt.t[:, :], [pt.b], [sg.b])
                S.dma("sp", hT[j * 128:(j + 1) * 128, tt * TT:(tt + 1) * TT], sg.t[:, :], reads=[sg.b])
            for j in range(ntm // 256):
                w = wch.next()
                S.dma("pool", w.t[:, :, :], wtm[:, j * 256:(j + 1) * 256].rearrange("(c p) n -> p c n", p=128), writes=[w.b])
                for s in range(4):
                    pt = ps_m.next()
                    for c in range(NC):
                        S.op("pe", lambda e, c=c, pt=pt, w=w, s=s: e.matmul(pt.t[:, 0:256], xnT.t[:, c, s * 128:(s + 1) * 128], w.t[:, c, :], start=(c == 0), stop=(c == NC - 1)),
                             reads=[w.b, xnT.b], writes=[pt.b], inc=(c == NC - 1))
                    sg = stg.next()
                    evac(S, flip, sg.t[:, 0:256], pt.t[:, 0:256], [pt.b], [sg.b])
                    S.dma("sp", htm[tt * TT + s * 128: tt * TT + (s + 1) * 128, j * 256:(j + 1) * 256], sg.t[:, 0:256], reads=[sg.b])
        S.barrier()


def build_A(T):
    nc = bass.Bass("TRN2", target_bir_lowering=False)
    x = nc.dram_tensor("x", [T, D_MODEL], F32, kind="ExternalInput").ap()
    gbc = nc.dram_tensor("gpc", [128, D_MODEL // 128], F32, kind="ExternalInput").ap()
    wfm = nc.dram_tensor("wfm", [D_MODEL, NFM], F32, kind="ExternalInput").ap()
    wtm = nc.dram_tensor("wtm", [D_MODEL, NTM], F32, kind="ExternalInput").ap()
    ident = nc.dram_tensor("ident", [128, 128], F32, kind="ExternalInput").ap()
    hT = nc.dram_tensor("hT", [NFM, T], F32, kind="ExternalOutput").ap()
    htm = nc.dram_tensor("htm", [T, NTM], F32, kind="ExternalOutput").ap()
    with ExitStack() as st:
        S = Sched(nc, st)
        phase_A(nc, S, T, x, gbc, wfm, wtm, ident, hT, htm)
        S.finish()
        S.emit()
    return nc


ROPE_PERM = np.concatenate([np.arange(32, 64), np.arange(0, 32)])


def split_w_in(w_in_l):
    o = np.cumsum([0, 768, 512, 64, 1024, 1024, 1024, 1024, 1024, 1024, 1024])
    cq, ckv, kr, hq, hf, hi, hg, aq, ak, av = [np.arange(o[i], o[i + 1]) for i in range(10)]
    fm_cols = np.concatenate([cq, ckv, kr, kr[ROPE_PERM], hq, hf, aq, ak])
    tm_cols = np.concatenate([hi, hg, av])
    return np.ascontiguousarray(w_in_l[:, fm_cols]), np.ascontiguousarray(w_in_l[:, tm_cols])


def bcast128(v):
    return np.ascontiguousarray(np.broadcast_to(np.asarray(v, np.float32)[None, :], (128, v.shape[0])))


IDENT = np.eye(128, dtype=np.float32)


C1_2PI = 6.28125
C2_2PI = TWO_PI - 6.28125
MLA_SCALE = 192.0 ** -0.5
CA_SCALE = 128.0 ** -0.5
VW = 132


def phase_B_mla(nc, S, nb, SL, cqT, ckvT, krT, pos64, wuq, wukv, gq_d, gkv_d, rc_d, out, ocol0):
    TT = 512
    NT = SL // TT
    NKB = SL // 128
    with ExitStack() as st:
        cx = Ctx(nc, st)
        ones = cx.sb("ones", [128, 128], BF16)
        S.op("pool", lambda e: e.memset(ones.t[:, :], 1.0), writes=[ones.b])
        gq = cx.sb("gq", [128, 6], F32)
        gkv = cx.sb("gkv", [128, 4], F32)
        rc = cx.sb("rc", [64, 2], F32)
        S.dma("sp", gq.t[:, :], gq_d, writes=[gq.b])
        S.dma("sp", gkv.t[:, :], gkv_d, writes=[gkv.b])
        S.dma("sp", rc.t[:, :], rc_d, writes=[rc.b])
        wq = cx.sb("wq", [128, 6, 512], BF16)
        wkv = cx.sb("wkv", [128, 4, 512], BF16)
        S.dma("pool", wq.t[:, :, :], wuq.rearrange("(c p) n -> p c n", p=128), writes=[wq.b])
        S.dma("pool", wkv.t[:, :, :], wukv.rearrange("(c p) n -> p c n", p=128), writes=[wkv.b])
        KT = [cx.sb("KT%d" % h, [128, SL], BF16) for h in range(2)]
        Kpe = cx.sb("Kpe", [64, SL], BF16)
        V = cx.sb("V", [128, NKB, 2, VW], BF16)
        KTb = [[Buf() for _ in range(NT)] for _ in range(2)]
        Kpeb = [Buf() for _ in range(NT)]
        Vb = [Buf() for _ in range(NT)]
        S.op("pool", lambda e: e.memset(V.t[:, :, :, :], 1.0), writes=Vb)
        cq = cx.sb("cq", [128, 6, TT], F32)
        ckv = cx.sb("ckv", [128, 4, TT], F32)
        sqq = cx.sb("sqq", [128, 6, TT], BF16)
        sqk = cx.sb("sqk", [128, 4, TT], BF16)
        kr = cx.sb("kr", [64, TT], F32)
        krp = cx.sb("krp", [64, TT], F32)
        posi = cx.sb("posi", [64, TT], I32)
        ang = cx.sb("ang", [64, TT], F32)
        ki = cx.sb("ki", [64, TT], I32)
        kf = cx.sb("kf", [64, TT], F32)
        yy = cx.sb("yy", [64, TT], F32)
        ya = cx.sb("ya", [64, TT], F32)
        Ct = cx.sb("Ct", [64, TT], F32)
        St = cx.sb("St", [64, TT], F32)
        t1 = cx.sb("t1", [64, TT], F32)
        t2 = cx.sb("t2", [64, TT], F32)
        cqn = cx.sb("cqn", [128, 6, TT], BF16)
        ckvn = cx.sb("ckvn", [128, 4, TT], BF16)
        rbq = cx.sb("rbq", [128, TT], F32)
        rbk = cx.sb("rbk", [128, TT], F32)
        qn = [[cx.sb("qn%d" % h, [128, TT], BF16) for h in range(2)] for _ in range(2)]
        qpe = [[cx.sb("qpe%d" % h, [64, TT], BF16) for h in range(2)] for _ in range(2)]
        pT = RR([cx.sb("pT", [128, TT], BF16) for _ in range(3)])
        osb = RR([cx.sb("osb", [128, 128], F32) for _ in range(2)])
        rec = cx.sb("rec", [128, 1], F32)
        ps_s = RR([cx.ps("ps_s", [128, 512]) for _ in range(2)])
        ps_o = [cx.ps("ps_o", [128, 512]) for _ in range(4)]
        ps_p = RR([cx.ps("ps_p", [128, 512]) for _ in range(2)])
        flip = [0]

        def rope(src_a, src_b, dst_ap, dst_b, src_reads):
            S.op("dve", lambda e: e.tensor_tensor(t1.t[:, :], src_a, Ct.t[:, :], ALU.mult), reads=src_reads + [Ct.b], writes=[t1.b])
            S.op("dve", lambda e: e.tensor_tensor(t2.t[:, :], src_b, St.t[:, :], ALU.mult), reads=src_reads + [St.b], writes=[t2.b])
            S.op("dve", lambda e: e.tensor_tensor(dst_ap, t1.t[:, :], t2.t[:, :], ALU.add), reads=[t1.b, t2.b], writes=[dst_b])

        def prologue(b, i):
            t0 = b * SL + i * TT
            c0 = i * TT
            par = i % 2
            S.dma("sp", cq.t[:, :, :], cqT[:, t0:t0 + TT].rearrange("(c p) t -> p c t", p=128), writes=[cq.b])
            S.dma("sp", ckv.t[:, :, :], ckvT[:, t0:t0 + TT].rearrange("(c p) t -> p c t", p=128), writes=[ckv.b])
            S.dma("sp", kr.t[:, :], krT[0:64, t0:t0 + TT], writes=[kr.b])
            S.dma("sp", krp.t[:, :], krT[64:128, t0:t0 + TT], writes=[krp.b])
            S.dma("sp", posi.t[:, :], pos64[:, t0:t0 + TT], writes=[posi.b])
            S.op("act", lambda e: e.activation(sqq.t[:, :, :], cq.t[:, :, :], AF.Square), reads=[cq.b], writes=[sqq.b])
            S.op("act", lambda e: e.activation(sqk.t[:, :, :], ckv.t[:, :, :], AF.Square), reads=[ckv.b], writes=[sqk.b])
            S.op("dve", lambda e: e.tensor_copy(ang.t[:, :], posi.t[:, :]), reads=[posi.b], writes=[ang.b])
            S.op("dve", lambda e: e.tensor_scalar(ang.t[:, :], ang.t[:, :], rc.t[:, 0:1], None, ALU.mult), reads=[ang.b, rc.b], writes=[ang.b])
            S.op("dve", lambda e: e.tensor_scalar(ki.t[:, :], ang.t[:, :], 1.0 / TWO_PI, None, ALU.mult), reads=[ang.b], writes=[ki.b])
            S.op("dve", lambda e: e.tensor_copy(kf.t[:, :], ki.t[:, :]), reads=[ki.b], writes=[kf.b])
            S.op("dve", lambda e: e.scalar_tensor_tensor(yy.t[:, :], kf.t[:, :], -C1_2PI, ang.t[:, :], ALU.mult, ALU.add), reads=[kf.b, ang.b], writes=[yy.b])
            S.op("dve", lambda e: e.scalar_tensor_tensor(yy.t[:, :], kf.t[:, :], -C2_2PI, yy.t[:, :], ALU.mult, ALU.add), reads=[kf.b, yy.b], writes=[yy.b])
            S.op("dve", lambda e: e.tensor_scalar(yy.t[:, :], yy.t[:, :], -math.pi, math.pi, ALU.max, ALU.min), reads=[yy.b], writes=[yy.b])
            S.op("dve", lambda e: e.scalar_tensor_tensor(ya.t[:, :], yy.t[:, :], -1.0, yy.t[:, :], ALU.mult, ALU.max), reads=[yy.b], writes=[ya.b])
            S.op("act", lambda e: e.activation(St.t[:, :], yy.t[:, :], AF.Sin), reads=[yy.b], writes=[St.b])
            S.op("act", lambda e: e.activation(Ct.t[:, :], ya.t[:, :], AF.Sin, bias=math.pi / 2, scale=-1.0), reads=[ya.b], writes=[Ct.b])
            S.op("dve", lambda e: e.tensor_scalar(St.t[:, :], St.t[:, :], rc.t[:, 1:2], None, ALU.mult), reads=[St.b, rc.b], writes=[St.b])
            yield
            psq = ps_p.next()
            for c in range(6):
                S.op("pe", lambda e, c=c: e.matmul(psq.t[:, :], ones.t[:, :], sqq.t[:, c, :], start=(c == 0), stop=(c == 5)),
                     reads=[ones.b, sqq.b], writes=[psq.b], inc=(c == 5))
            psk = ps_p.next()
            for c in range(4):
                S.op("pe", lambda e, c=c: e.matmul(psk.t[:, :], ones.t[:, :], sqk.t[:, c, :], start=(c == 0), stop=(c == 3)),
                     reads=[ones.b, sqk.b], writes=[psk.b], inc=(c == 3))
            yield
            rstd(S, rbq.t[:, :], rbq.b, psq.t[:, :], psq.b, 1.0 / 768)
            rstd(S, rbk.t[:, :], rbk.b, psk.t[:, :], psk.b, 1.0 / 512)
            for c in range(4):
                S.op("dve", lambda e, c=c: e.scalar_tensor_tensor(ckvn.t[:, c, :], ckv.t[:, c, :], gkv.t[:, c:c + 1], rbk.t[:, :], ALU.mult, ALU.mult),
                     reads=[ckv.b, gkv.b, rbk.b], writes=[ckvn.b])
            for c in range(6):
                S.op("dve", lambda e, c=c: e.scalar_tensor_tensor(cqn.t[:, c, :], cq.t[:, c, :], gq.t[:, c:c + 1], rbq.t[:, :], ALU.mult, ALU.mult),
                     reads=[cq.b, gq.b, rbq.b], writes=[cqn.b])
            rope(kr.t[:, :], krp.t[:, :], Kpe.t[:, c0:c0 + TT], Kpeb[i], [kr.b, krp.b])
            yield
            for h in range(2):
                ps = ps_p.next()
                for c in range(4):
                    S.op("pe", lambda e, c=c, ps=ps, h=h: e.matmul(ps.t[:, :], wkv.t[:, c, h * 128:(h + 1) * 128], ckvn.t[:, c, :], start=(c == 0), stop=(c == 3)),
                         reads=[wkv.b, ckvn.b], writes=[ps.b], inc=(c == 3))
                evac(S, flip, KT[h].t[:, c0:c0 + TT], ps.t[:, :], [ps.b], [KTb[h][i]])
            for s in range(4):
                ps = ps_p.next()
                for c in range(4):
                    S.op("pe", lambda e, c=c, ps=ps, s=s: e.matmul(ps.t[:, 0:256], ckvn.t[:, c, s * 128:(s + 1) * 128], wkv.t[:, c, 256:512], start=(c == 0), stop=(c == 3)),
                         reads=[wkv.b, ckvn.b], writes=[ps.b], inc=(c == 3))
                evac(S, flip, V.t[:, i * 4 + s, :, 0:128], ps.t[:, 0:256].rearrange("p (h v) -> p h v", h=2), [ps.b], [Vb[i]])
                if s == 1:
                    yield
            yield
            for h in range(2):
                ps = ps_p.next()
                for c in range(6):
                    S.op("pe", lambda e, c=c, ps=ps, h=h: e.matmul(ps.t[:, :], wq.t[:, c, h * 256:h * 256 + 128], cqn.t[:, c, :], start=(c == 0), stop=(c == 5)),
                         reads=[wq.b, cqn.b], writes=[ps.b], inc=(c == 5))
                evac(S, flip, qn[par][h].t[:, :], ps.t[:, :], [ps.b], [qn[par][h].b])
                yield
                psa = ps_p.next()
                for c in range(6):
                    S.op("pe", lambda e, c=c, ps=psa, h=h: e.matmul(ps.t[0:64, :], wq.t[:, c, h * 256 + 128:h * 256 + 192], cqn.t[:, c, :], start=(c == 0), stop=(c == 5)),
                         reads=[wq.b, cqn.b], writes=[psa.b], inc=(c == 5))
                psb = ps_p.next()
                for c in range(6):
                    S.op("pe", lambda e, c=c, ps=psb, h=h: e.matmul(ps.t[0:64, :], wq.t[:, c, h * 256 + 192:h * 256 + 256], cqn.t[:, c, :], start=(c == 0), stop=(c == 5)),
                         reads=[wq.b, cqn.b], writes=[psb.b], inc=(c == 5))
                rope(psa.t[0:64, :], psb.t[0:64, :], qpe[par][h].t[:, :], qpe[par][h].b, [psa.b, psb.b])
                yield

        def attention(b, i, gen):
            t0 = b * SL + i * TT
            par = i % 2
            nkb = 4 * i + 4
            total = 2 * nkb
            NSTAGE = 10
            every = max(1, total // NSTAGE)
            stepno = 0
            for h in range(2):
                def qk(kb, h=h):
                    off = kb - 4 * i
                    qs = max(0, off) * 128
                    N = TT - qs
                    ps = ps_s.next()
                    S.op("pe", lambda e, ps=ps, kb=kb, qs=qs, N=N: e.matmul(ps.t[:, 0:N], KT[h].t[:, kb * 128:(kb + 1) * 128], qn[par][h].t[:, qs:TT], start=True, stop=False),
                         reads=[KTb[h][kb // 4], qn[par][h].b], writes=[ps.b], inc=False)
                    S.op("pe", lambda e, ps=ps, kb=kb, qs=qs, N=N: e.matmul(ps.t[:, 0:N], Kpe.t[:, kb * 128:(kb + 1) * 128], qpe[par][h].t[:, qs:TT], start=False, stop=True),
                         reads=[Kpeb[kb // 4], qpe[par][h].b], writes=[ps.b])
                    return ps, off, qs, N

                cur = qk(0)
                for kb in range(nkb):
                    nxt = qk(kb + 1) if kb + 1 < nkb else None
                    ps, off, qs, N = cur
                    p = pT.next()
                    S.op("act", lambda e, ps=ps, p=p, N=N: e.activation(p.t[:, 0:N], ps.t[:, 0:N], AF.Exp, scale=MLA_SCALE), reads=[ps.b], writes=[p.b])
                    if off >= 0:
                        S.op("pool", lambda e, p=p: e.memset(p.t[64:128, 0:64], 0.0), writes=[p.b])
                    js = list(range(max(0, off), 4))
                    for j in js:
                        po = ps_o[j]
                        lastj = (j == js[-1])
                        S.op("pe", lambda e, p=p, po=po, j=j, qs=qs, kb=kb, h=h: e.matmul(po.t[:, 0:129], p.t[:, j * 128 - qs:j * 128 - qs + 128], V.t[:, kb, h, 0:129], start=(kb == 0), stop=(kb == 4 * i + j)),
                             reads=[p.b, Vb[kb // 4]], writes=([ps_o[jj].b for jj in js] if (lastj or j == js[0]) else []), inc=lastj)
                    cur = nxt
                    stepno += 1
                    if gen is not None and stepno % every == 0:
                        next(gen, None)
                for j in range(4):
                    po = ps_o[j]
                    S.op("dve", lambda e, po=po: e.reciprocal(rec.t[:, 0:1], po.t[:, 128:129]), reads=[po.b], writes=[rec.b])
                    ob = osb.next()
                    S.op("dve", lambda e, po=po, ob=ob: e.tensor_scalar(ob.t[:, :], po.t[:, 0:128], rec.t[:, 0:1], None, ALU.mult), reads=[po.b, rec.b], writes=[ob.b])
                    S.dma("sp", out[t0 + j * 128:t0 + (j + 1) * 128, ocol0 + h * 128:ocol0 + (h + 1) * 128], ob.t[:, :], reads=[ob.b])
            if gen is not None:
                for _ in gen:
                    pass

        for b in range(nb):
            for _ in prologue(b, 0):
                pass
            for i in range(NT):
                gen = prologue(b, i + 1) if i + 1 < NT else None
                attention(b, i, gen)
        S.barrier()


def rope_consts():
    inv = np.exp(-math.log(10000.0) * 2.0 * np.arange(32, dtype=np.float32) / 64).astype(np.float32)
    rc = np.zeros((64, 2), np.float32)
    rc[:, 0] = np.concatenate([inv, inv])
    rc[:, 1] = np.concatenate([-np.ones(32, np.float32), np.ones(32, np.float32)])
    return rc


def mla_weights_for_core(w_uq_l, w_ukv_l, heads):
    qcols, kcols, vcols = [], [], []
    for h in heads:
        base = h * 192
        rope = base + 128 + np.arange(64)
        qcols.append(np.concatenate([base + np.arange(128), rope, rope[ROPE_PERM]]))
        kcols.append(h * 256 + np.arange(128))
        vcols.append(h * 256 + 128 + np.arange(128))
    wuq = np.ascontiguousarray(w_uq_l[:, np.concatenate(qcols)])
    wukv = np.ascontiguousarray(w_ukv_l[:, np.concatenate(kcols + vcols)])
    return wuq, wukv


def gain_pc(g):
    return np.ascontiguousarray(np.asarray(g, np.float32).reshape(-1, 128).T)


def phase_B_ca(nc, S, nb, SL, aqT, akT, av, biasT_d, out, ocol):
    NKB = SL // 128
    with ExitStack() as st:
        cx = Ctx(nc, st)
        EB = cx.sb("EB", [128, 640], F32)
        S.dma("sp", EB.t[:, :], biasT_d, writes=[EB.b])
        S.op("act", lambda e: e.activation(EB.t[:, :], EB.t[:, :], AF.Exp), reads=[EB.b], writes=[EB.b])
        S.op("pool", lambda e: e.memset(EB.t[0:64, 576:640], 0.0), writes=[EB.b])
        S.op("pool", lambda e: e.memset(EB.t[64:128, 0:64], 0.0), writes=[EB.b])
        QT = cx.sb("QT", [128, SL], BF16)
        KT = cx.sb("KT", [128, SL], BF16)
        V = cx.sb("V", [128, NKB, VW], BF16)
        e32 = RR([cx.sb("e32", [128, 128], F32) for _ in range(2)])
        pb = RR([cx.sb("pb", [128, 128], BF16) for _ in range(2)])
        osb = RR([cx.sb("osb", [128, 128], F32) for _ in range(2)])
        rec = cx.sb("rec", [128, 1], F32)
        ps_s = RR([cx.ps("ps_s", [128, 512]) for _ in range(2)])
        ps_o = RR([cx.ps("ps_o", [128, 512]) for _ in range(2)])
        for b in range(nb):
            tb = b * SL
            S.op("pool", lambda e: e.memset(V.t[:, :, :], 1.0), writes=[V.b])
            S.dma("pool", QT.t[:, :], aqT[:, tb:tb + SL], writes=[QT.b])
            S.dma("pool", KT.t[:, :], akT[:, tb:tb + SL], writes=[KT.b])
            for g in range(0, NKB, 16):
                n = min(16, NKB - g)
                S.dma("pool", V.t[:, g:g + n, 0:128], av[tb + g * 128:tb + (g + n) * 128, :].rearrange("(n p) d -> p n d", p=128), writes=[V.b])
            steps = [(m, kb) for m in range(NKB) for kb in range(max(0, m - 4), m + 1)]

            def qk(m, kb):
                ps = ps_s.next()
                S.op("pe", lambda e, ps=ps, kb=kb, m=m: e.matmul(ps.t[:, 0:128], KT.t[:, kb * 128:(kb + 1) * 128], QT.t[:, m * 128:(m + 1) * 128], start=True, stop=True),
                     reads=[KT.b, QT.b], writes=[ps.b])
                return ps

            cur = qk(*steps[0])
            acc = None
            for si, (m, kb) in enumerate(steps):
                nxt = qk(*steps[si + 1]) if si + 1 < len(steps) else None
                first = max(0, m - 4)
                if kb == first:
                    acc = ps_o.next()
                ps = cur
                ee = e32.next()
                S.op("act", lambda e, ps=ps, ee=ee: e.activation(ee.t[:, :], ps.t[:, 0:128], AF.Exp, scale=CA_SCALE), reads=[ps.b], writes=[ee.b])
                p = pb.next()
                S.op("dve", lambda e, ee=ee, p=p, kb=kb, m=m: e.tensor_tensor(p.t[:, :], ee.t[:, :], EB.t[:, (m - kb) * 128:(m - kb + 1) * 128], ALU.mult),
                     reads=[ee.b, EB.b], writes=[p.b])
                S.op("pe", lambda e, p=p, acc=acc, kb=kb, m=m, first=first: e.matmul(acc.t[:, 0:129], p.t[:, :], V.t[:, kb, 0:129], start=(kb == first), stop=(kb == m)),
                     reads=[p.b, V.b], writes=[acc.b])
                if kb == m:
                    S.op("dve", lambda e, acc=acc: e.reciprocal(rec.t[:, 0:1], acc.t[:, 128:129]), reads=[acc.b], writes=[rec.b])
                    ob = osb.next()
                    S.op("dve", lambda e, acc=acc, ob=ob: e.tensor_scalar(ob.t[:, :], acc.t[:, 0:128], rec.t[:, 0:1], None, ALU.mult), reads=[acc.b, rec.b], writes=[ob.b])
                    S.dma("sp", out[tb + m * 128:tb + (m + 1) * 128, ocol:ocol + 128], ob.t[:, :], reads=[ob.b])
                cur = nxt
        S.barrier()


HG_BIG = 4.7e18


def phase_B_hg(nc, S, nb, SL, layer, hqT, hfT, hi, hgate, lbraw_d, gout_d, tri_d, ident_d, out, ocol):
    TT = 512
    L = 64
    NCH = TT // L
    NT = SL // TT
    with ExitStack() as st:
        cx = Ctx(nc, st)
        tri = cx.sb("tri", [128, 128], F32)
        gout = cx.sb("gout", [128, 128], F32)
        idf = cx.sb("idf", [128, 128], F32)
        idb = cx.sb("idb", [128, 128], BF16)
        lbr = cx.sb("lbr", [128, DEPTH], F32)
        S.dma("sp", tri.t[:, :], tri_d, writes=[tri.b])
        S.dma("sp", gout.t[:, :], gout_d, writes=[gout.b])
        S.dma("sp", idf.t[:, :], ident_d, writes=[idf.b])
        S.dma("sp", lbr.t[:, :], lbraw_d, writes=[lbr.b])
        S.op("dve", lambda e: e.tensor_copy(idb.t[:, :], idf.t[:, :]), reads=[idf.b], writes=[idb.b])
        sm = cx.sb("sm", [128, 8], F32)
        S.op("dve", lambda e: e.tensor_tensor(sm.t[:, 0:1], lbr.t[:, 1:2], lbr.t[:, 0:1], ALU.subtract), reads=[lbr.b], writes=[sm.b])
        S.op("act", lambda e: e.activation(sm.t[:, 1:2], sm.t[:, 0:1], AF.Sigmoid), reads=[sm.b], writes=[sm.b])
        S.op("act", lambda e: e.activation(sm.t[:, 2:3], sm.t[:, 0:1], AF.Sigmoid, scale=-1.0), reads=[sm.b], writes=[sm.b])
        if layer == 0:
            S.op("dve", lambda e: e.tensor_tensor(sm.t[:, 4:5], sm.t[:, 2:3], sm.t[:, 2:3], ALU.subtract), reads=[sm.b], writes=[sm.b])
        else:
            S.op("dve", lambda e: e.tensor_tensor(sm.t[:, 3:4], sm.t[:, 2:3], sm.t[:, 1:2], ALU.add), reads=[sm.b], writes=[sm.b])
            S.op("dve", lambda e: e.tensor_tensor(sm.t[:, 4:5], sm.t[:, 3:4], sm.t[:, 2:3], ALU.subtract), reads=[sm.b], writes=[sm.b])
        S.op("dve", lambda e: e.tensor_scalar(sm.t[:, 5:6], sm.t[:, 4:5], -1.0, 1.0, ALU.mult, ALU.add), reads=[sm.b], writes=[sm.b])
        lb_ap = sm.t[:, 4:5]
        oml_ap = sm.t[:, 5:6]
        mask = cx.sb("mask", [128, TT], F32)
        S.op("pool", lambda e: e.memset(mask.t[:, :], 1.0), writes=[mask.b])
        for k in range(NCH):
            S.op("pool", lambda e, k=k: e.memset(mask.t[:, k * L:k * L + 1], 0.0), writes=[mask.b])
        class Set:
            pass
        sets = []
        for b in range(nb):
            B = Set()
            for nm in ("hq", "hf", "sg", "sgn", "ff", "bb", "eq", "ek", "sq"):
                setattr(B, nm, cx.sb(nm, [128, TT], F32))
            B.qT = cx.sb("qT", [128, TT], BF16)
            B.kT = cx.sb("kT", [128, TT], BF16)
            B.ktm = cx.sb("ktm", [L, NCH, 128], BF16)
            B.vtm = cx.sb("vtm", [L, NCH, 128], BF16)
            B.gtm = cx.sb("gtm", [L, NCH, 128], F32)
            B.sc = cx.sb("sc", [128, 4 * NCH], F32)
            B.AT = RR([cx.sb("AT", [L, L], BF16) for _ in range(2)])
            B.state = cx.sb("state", [128, 128], F32)
            B.tmp = cx.sb("tmp", [128, 128], F32)
            B.stbf = cx.sb("stbf", [128, 128], BF16)
            B.junk = cx.sb("junk", [L, 128], F32)
            B.ssq = cx.sb("ssq", [L, 1], F32)
            B.rr = cx.sb("rr", [L, 1], F32)
            B.o1 = cx.sb("o1", [L, 128], F32)
            B.osb = RR([cx.sb("osb", [L, 128], F32) for _ in range(2)])
            B.ps_a = cx.ps("ps_a", [128, 512])
            B.ps_o = cx.ps("ps_o", [128, 512])
            B.ps_d = cx.ps("ps_d", [128, 512])
            B.ps_tr = cx.ps("ps_tr", [128, 1024], BF16)
            B.flip = [b]
            sets.append(B)

        def prep(b, i):
            B = sets[b]
            hq, hf, sg, sgn, ff, bb, eq, ek, sq, qT, kT, ktm, vtm, gtm, sc = B.hq, B.hf, B.sg, B.sgn, B.ff, B.bb, B.eq, B.ek, B.sq, B.qT, B.kT, B.ktm, B.vtm, B.gtm, B.sc
            t0 = b * SL + i * TT
            S.dma("sp", hq.t[:, :], hqT[:, t0:t0 + TT], writes=[hq.b])
            S.dma("sp", hf.t[:, :], hfT[:, t0:t0 + TT], writes=[hf.b])
            S.dma("pool", vtm.t[:, :, :], hi[t0:t0 + TT, :].rearrange("(n p) d -> p n d", p=L), writes=[vtm.b])
            S.dma("sp", gtm.t[:, :, :], hgate[t0:t0 + TT, :].rearrange("(n p) d -> p n d", p=L), writes=[gtm.b])
            S.op("act", lambda e: e.activation(sg.t[:, :], hf.t[:, :], AF.Sigmoid), reads=[hf.b], writes=[sg.b])
            S.op("act", lambda e: e.activation(sgn.t[:, :], hf.t[:, :], AF.Sigmoid, scale=-1.0), reads=[hf.b], writes=[sgn.b])
            S.op("act", lambda e: e.activation(sq.t[:, :], hq.t[:, :], AF.Silu), reads=[hq.b], writes=[sq.b])
            S.op("act", lambda e: e.activation(gtm.t[:, :, :], gtm.t[:, :, :], AF.Silu), reads=[gtm.b], writes=[gtm.b])
            S.op("dve", lambda e: e.tensor_scalar(ff.t[:, :], sg.t[:, :], oml_ap, lb_ap, ALU.mult, ALU.add), reads=[sg.b, sm.b], writes=[ff.b])
            S.op("dve", lambda e: e.tensor_scalar_max(ff.t[:, :], ff.t[:, :], TINY), reads=[ff.b], writes=[ff.b])
            S.op("act", lambda e: e.activation(ff.t[:, :], ff.t[:, :], AF.Ln), reads=[ff.b], writes=[ff.b])
            S.op("dve", lambda e: e.tensor_tensor_scan(bb.t[:, :], mask.t[:, :], ff.t[:, :], 0.0, ALU.mult, ALU.add), reads=[mask.b, ff.b], writes=[bb.b])
            S.op("dve", lambda e: e.tensor_scalar(sgn.t[:, :], sgn.t[:, :], oml_ap, None, ALU.mult), reads=[sgn.b, sm.b], writes=[sgn.b])
            for n in range(NCH):
                c0 = n * L
                mid = bb.t[:, c0 + L // 2 - 1:c0 + L // 2]
                last = bb.t[:, c0 + L - 1:c0 + L]
                S.op("dve", lambda e, n=n, mid=mid: e.tensor_scalar(sc.t[:, n:n + 1], mid, -1.0, None, ALU.mult), reads=[bb.b], writes=[sc.b])
                S.op("act", lambda e, n=n, c0=c0: e.activation(eq.t[:, c0:c0 + L], bb.t[:, c0:c0 + L], AF.Exp, bias=sc.t[:, n:n + 1]), reads=[bb.b, sc.b], writes=[eq.b])
                S.op("act", lambda e, n=n, c0=c0, mid=mid: e.activation(ek.t[:, c0:c0 + L], bb.t[:, c0:c0 + L], AF.Exp, bias=mid, scale=-1.0), reads=[bb.b], writes=[ek.b])
                S.op("act", lambda e, n=n, mid=mid: e.activation(sc.t[:, NCH + n:NCH + n + 1], mid, AF.Exp), reads=[bb.b], writes=[sc.b])
                S.op("act", lambda e, n=n, last=last: e.activation(sc.t[:, 2 * NCH + n:2 * NCH + n + 1], last, AF.Exp, bias=sc.t[:, n:n + 1]), reads=[bb.b, sc.b], writes=[sc.b])
                S.op("act", lambda e, n=n, last=last: e.activation(sc.t[:, 3 * NCH + n:3 * NCH + n + 1], last, AF.Exp), reads=[bb.b], writes=[sc.b])
            S.op("dve", lambda e: e.scalar_tensor_tensor(qT.t[:, :], eq.t[:, :], HG_BIG, sq.t[:, :], ALU.min, ALU.mult), reads=[sq.b, eq.b], writes=[qT.b])
            S.op("dve", lambda e: e.scalar_tensor_tensor(kT.t[:, :], ek.t[:, :], HG_BIG, sgn.t[:, :], ALU.min, ALU.mult), reads=[sgn.b, ek.b], writes=[kT.b])
            ps_tr = B.ps_tr
            for n in range(NCH):
                S.op("pe", lambda e, n=n: e.transpose(ps_tr.t[0:L, n * 128:(n + 1) * 128], kT.t[:, n * L:(n + 1) * L], idb.t[:, :]),
                     reads=[kT.b, idb.b], writes=[ps_tr.b], inc=(n == NCH - 1))
            evac(S, B.flip, ktm.t[:, :, :], ps_tr.t[0:L, :].rearrange("p (n k) -> p n k", n=NCH), [ps_tr.b], [ktm.b])

        def chunk(b, i, n):
            B = sets[b]
            qT, kT, ktm, vtm, gtm, sc, state, tmp, stbf, junk, ssq, rr, o1 = B.qT, B.kT, B.ktm, B.vtm, B.gtm, B.sc, B.state, B.tmp, B.stbf, B.junk, B.ssq, B.rr, B.o1
            t0 = b * SL + i * TT
            c0 = n * L
            pa = B.ps_a
            S.op("pe", lambda e, c0=c0: e.matmul(pa.t[0:L, 0:L], kT.t[:, c0:c0 + L], qT.t[:, c0:c0 + L], start=True, stop=True),
                 reads=[kT.b, qT.b], writes=[pa.b])
            at = B.AT.next()
            S.op("dve", lambda e, at=at: e.tensor_tensor(at.t[:, :], pa.t[0:L, 0:L], tri.t[0:L, 0:L], ALU.mult), reads=[pa.b, tri.b], writes=[at.b])
            S.op("dve", lambda e, n=n: e.tensor_scalar(stbf.t[:, :], state.t[:, :], sc.t[:, NCH + n:NCH + n + 1], None, ALU.mult), reads=[state.b, sc.b], writes=[stbf.b])
            po = B.ps_o
            S.op("pe", lambda e, at=at, n=n: e.matmul(po.t[0:L, 0:128], at.t[:, :], vtm.t[:, n, :], start=True, stop=False),
                 reads=[at.b, vtm.b], writes=[po.b], inc=False)
            S.op("pe", lambda e, c0=c0: e.matmul(po.t[0:L, 0:128], qT.t[:, c0:c0 + L], stbf.t[:, :], start=False, stop=True),
                 reads=[qT.b, stbf.b], writes=[po.b])
            ps_d = B.ps_d
            S.op("pe", lambda e, n=n: e.matmul(ps_d.t[:, 0:128], ktm.t[:, n, :], vtm.t[:, n, :], start=True, stop=True),
                 reads=[ktm.b, vtm.b], writes=[ps_d.b])
            S.op("dve", lambda e, n=n: e.tensor_scalar(tmp.t[:, :], state.t[:, :], sc.t[:, 3 * NCH + n:3 * NCH + n + 1], None, ALU.mult), reads=[state.b, sc.b], writes=[tmp.b])
            S.op("dve", lambda e, n=n: e.scalar_tensor_tensor(state.t[:, :], ps_d.t[:, 0:128], sc.t[:, 2 * NCH + n:2 * NCH + n + 1], tmp.t[:, :], ALU.mult, ALU.add),
                 reads=[ps_d.b, sc.b, tmp.b], writes=[state.b])
            S.op("act", lambda e: e.activation(junk.t[:, :], po.t[0:L, 0:128], AF.Square, accum_out=ssq.t[:, 0:1]), reads=[po.b], writes=[junk.b, ssq.b])
            rstd(S, rr.t[:, 0:1], rr.b, ssq.t[:, 0:1], ssq.b, 1.0 / 128)
            S.op("dve", lambda e: e.scalar_tensor_tensor(o1.t[:, :], po.t[0:L, 0:128], rr.t[:, 0:1], gout.t[0:L, :], ALU.mult, ALU.mult),
                 reads=[po.b, rr.b, gout.b], writes=[o1.b])
            ob = B.osb.next()
            S.op("dve", lambda e, ob=ob, n=n: e.tensor_tensor(ob.t[:, :], o1.t[:, :], gtm.t[:, n, :], ALU.mult), reads=[o1.b, gtm.b], writes=[ob.b])
            S.dma("sp", out[t0 + c0:t0 + c0 + L, ocol:ocol + 128], ob.t[:, :], reads=[ob.b])

        for b in range(nb):
            S.op("pool", lambda e, b=b: e.memset(sets[b].state.t[:, :], 0.0), writes=[sets[b].state.b])
        for i in range(NT):
            for b in range(nb):
                prep(b, i)
            for n in range(NCH):
                for b in range(nb):
                    chunk(b, i, n)
        S.barrier()


def ca_bias_tile(rel_bias_h):
    ki = np.arange(128)[:, None]
    qi = np.arange(640)[None, :]
    return np.ascontiguousarray(rel_bias_h[np.clip(qi - ki, -256, 256) + 256].astype(np.float32))


TRI = np.ascontiguousarray((np.arange(128)[:, None] <= np.arange(128)[None, :]).astype(np.float32))


def mm_tm_stream(S, lhsT, nK, w_dram, wch_rr, ps_rr, stg_rr, junk, yscr, row0, ssqp, yb):
    G = 8
    ngrp = (nK + G - 1) // G
    for j in range(DEBUG.get("nj", 8)):
        acc = [ps_rr.next() for _ in range(4)]
        for g in range(ngrp):
            k0 = g * G
            kn = min(G, nK - k0)
            w = wch_rr.next()
            wv = w.t[:, :].rearrange("p (c n) -> p c n", n=512)
            S.dma("pool", wv[:, 0:kn, :], w_dram[k0 * 128:(k0 + kn) * 128, j * 512:(j + 1) * 512].rearrange("(c p) n -> p c n", p=128), writes=[w.b])
            for s in range(4):
                for k in range(kn):
                    S.op("pe", lambda e, a=acc[s], s=s, k=k, k0=k0, wv=wv, g=g, kn=kn: e.matmul(a.t[:, :], lhsT.t[:, k0 + k, s * 128:(s + 1) * 128], wv[:, k, :], start=(g == 0 and k == 0), stop=(g == ngrp - 1 and k == kn - 1)),
                         reads=[lhsT.b, w.b], writes=[acc[s].b], inc=(k == kn - 1))
        for s in range(4):
            sg = stg_rr.next()
            S.op("dve", lambda e, sg=sg, a=acc[s]: e.tensor_copy(sg.t[:, :], a.t[:, :]), reads=[acc[s].b], writes=[sg.b])
            if not DEBUG.get("nosq"):
                S.op("act", lambda e, sg=sg, s=s, j=j: e.activation(junk.t[:, 0:512], sg.t[:, :], AF.Square, accum_out=ssqp.t[:, s * 8 + j:s * 8 + j + 1]),
                     reads=[sg.b], writes=[junk.b, ssqp.b])
            if not DEBUG.get("nostore"):
                S.dma("sp", yscr[row0 + s * 128:row0 + (s + 1) * 128, j * 512:(j + 1) * 512], sg.t[:, :], reads=[sg.b], writes=[yb[s][j]])


def tail(S, yscr, row0, resid, outp, ssqp, yb, gpost, tl_rr, rt, pe_id):
    for s in range(4):
        r0 = row0 + s * 128
        S.op("dve", lambda e, s=s: e.tensor_reduce(rt.t[:, 0:1], ssqp.t[:, s * 8:(s + 1) * 8], mybir.AxisListType.X, ALU.add), reads=[ssqp.b], writes=[rt.b])
        rstd(S, rt.t[:, 0:1], rt.b, rt.t[:, 0:1], rt.b, 1.0 / D_MODEL)
        for j in range(8):
            ybk = tl_rr.next()
            xbk = tl_rr.next()
            S.dma("sp", ybk.t[:, :], yscr[r0:r0 + 128, j * 512:(j + 1) * 512], reads=[yb[s][j]], writes=[ybk.b])
            S.dma("sp", xbk.t[:, :], resid[r0:r0 + 128, j * 512:(j + 1) * 512], writes=[xbk.b])
            S.op("dve", lambda e, ybk=ybk, j=j: e.scalar_tensor_tensor(ybk.t[:, :], ybk.t[:, :], rt.t[:, 0:1], gpost.t[:, j * 512:(j + 1) * 512], ALU.mult, ALU.mult),
                 reads=[ybk.b, rt.b, gpost.b], writes=[ybk.b])
            S.op("pool", lambda e, ybk=ybk, xbk=xbk: e.tensor_tensor(xbk.t[:, :], ybk.t[:, :], xbk.t[:, :], ALU.add), reads=[ybk.b, xbk.b], writes=[xbk.b])
            S.dma("sp", outp[r0:r0 + 128, j * 512:(j + 1) * 512], xbk.t[:, :], reads=[xbk.b])


def phase_C(nc, S, T, o, x, gpc_d, w_out, gpost_d, ident_d, yscr, x1):
    TT = 512
    NC = D_MODEL // 128
    groups = [(0, 2048, True), (2048, 3072, False), (3072, 4096, True)]
    with ExitStack() as st:
        cx = Ctx(nc, st)
        ident = cx.sb("ident", [128, 128], F32)
        gpc = cx.sb("gpc", [128, NC], F32)
        gpost = cx.sb("gpost", [128, D_MODEL], F32)
        S.dma("sp", ident.t[:, :], ident_d, writes=[ident.b])
        S.dma("sp", gpc.t[:, :], gpc_d, writes=[gpc.b])
        S.dma("sp", gpost.t[:, :], gpost_d, writes=[gpost.b])
        ps_t = RR([cx.ps("ps_t", [128, 512]) for _ in range(2)])
        ps_m = RR([cx.ps("ps_m", [128, 512]) for _ in range(6)])
        fr = Front(S, cx, D_MODEL, ident, gpc, ps_t, nbuf=2)
        oT = cx.sb("oT", [128, NC, TT], BF16)
        wch = RR([cx.sb("wch", [128, 4096], BF16) for _ in range(3)])
        stg = RR([cx.sb("stg", [128, 512], F32) for _ in range(2)])
        tl = RR([cx.sb("tl", [128, 512], F32) for _ in range(4)])
        ssqp = cx.sb("ssqp", [128, 32], F32)
        rt = cx.sb("rt", [128, 1], F32)
        for tt in range(T // TT):
            yb = [[Buf() for _ in range(8)] for _ in range(4)]
            for s in range(4):
                fr.run(o[tt * TT + s * 128: tt * TT + (s + 1) * 128, :], s, oT, groups=(None if DEBUG.get("nogroups") else groups))
            mm_tm_stream(S, oT, NC, w_out, wch, ps_m, stg, fr.junk, yscr, tt * TT, ssqp, yb)
            if not DEBUG.get("notail"):
                tail(S, yscr, tt * TT, x, x1, ssqp, yb, gpost, tl, rt, None)
        S.barrier()


def phase_D(nc, S, T, x1, gpc_d, wg, wu, wd, gpost_d, ident_d, yscr, x2, dff=D_FF):
    TT = 512
    NC = D_MODEL // 128
    NF = dff // 128
    with ExitStack() as st:
        cx = Ctx(nc, st)
        ident = cx.sb("ident", [128, 128], F32)
        gpc = cx.sb("gpc", [128, NC], F32)
        gpost = cx.sb("gpost", [128, D_MODEL], F32)
        S.dma("sp", ident.t[:, :], ident_d, writes=[ident.b])
        S.dma("sp", gpc.t[:, :], gpc_d, writes=[gpc.b])
        S.dma("sp", gpost.t[:, :], gpost_d, writes=[gpost.b])
        ps_t = RR([cx.ps("ps_t", [128, 512]) for _ in range(2)])
        ps_m = RR([cx.ps("ps_m", [128, 512]) for _ in range(6)])
        fr = Front(S, cx, D_MODEL, ident, gpc, ps_t, nbuf=1)
        hT = cx.sb("hT", [128, NC, TT], BF16)
        actT = cx.sb("actT", [128, NF, TT], BF16)
        wch = RR([cx.sb("wch", [128, 4096], BF16) for _ in range(3)])
        stg = RR([cx.sb("stg", [128, 512], F32) for _ in range(2)])
        sgl = RR([cx.sb("sgl", [128, 512], F32) for _ in range(2)])
        tl = RR([cx.sb("tl", [128, 512], F32) for _ in range(4)])
        ssqp = cx.sb("ssqp", [128, 32], F32)
        rt = cx.sb("rt", [128, 1], F32)
        for tt in range(T // TT):
            yb = [[Buf() for _ in range(8)] for _ in range(4)]
            for s in range(4):
                fr.run(x1[tt * TT + s * 128: tt * TT + (s + 1) * 128, :], s, hT)
            for f in range(NF):
                pp = []
                for wmat in (wg, wu):
                    w = wch.next()
                    wv = w.t[:, :].rearrange("p (c n) -> p c n", n=128)
                    S.dma("pool", wv[:, :, :], wmat[:, f * 128:(f + 1) * 128].rearrange("(c p) n -> p c n", p=128), writes=[w.b])
                    pt = ps_m.next()
                    for c in range(NC):
                        S.op("pe", lambda e, c=c, pt=pt, wv=wv: e.matmul(pt.t[:, :], wv[:, c, :], hT.t[:, c, :], start=(c == 0), stop=(c == NC - 1)),
                             reads=[w.b, hT.b], writes=[pt.b], inc=(c == NC - 1))
                    pp.append(pt)
                sg = sgl.next()
                S.op("act", lambda e, sg=sg, pg=pp[0]: e.activation(sg.t[:, :], pg.t[:, :], AF.Silu), reads=[pp[0].b], writes=[sg.b])
                S.op("dve", lambda e, sg=sg, pu=pp[1], f=f: e.tensor_tensor(actT.t[:, f, :], sg.t[:, :], pu.t[:, :], ALU.mult), reads=[sg.b, pp[1].b], writes=[actT.b])
            mm_tm_stream(S, actT, NF, wd, wch, ps_m, stg, fr.junk, yscr, tt * TT, ssqp, yb)
            tail(S, yscr, tt * TT, x1, x2, ssqp, yb, gpost, tl, rt, None)
        S.barrier()


def build_B(layer, nb=BATCH, SL=SEQ):
    NTOK = nb * SL
    nc = bass.Bass("TRN2", target_bir_lowering=False)
    d = lambda n, s, t=F32: nc.dram_tensor(n, s, t, kind="ExternalInput").ap()
    lat = d("lat", [1408, NTOK])
    pos64 = d("pos64", [64, NTOK], I32)
    wuq = d("wuq", [768, 512])
    wukv = d("wukv", [512, 512])
    gq = d("gq", [128, 6])
    gkv = d("gkv", [128, 4])
    rc = d("rc", [64, 2])
    hqf = d("hqf", [256, NTOK])
    hig = d("hig", [NTOK, 256])
    lbr = d("lbr", [128, DEPTH])
    gout = d("gout", [128, 128])
    tri = d("tri", [128, 128])
    ident = d("ident", [128, 128])
    aqk = d("aqk", [256, NTOK])
    av = d("av", [NTOK, 128])
    biasT = d("biasT", [128, 640])
    out = nc.dram_tensor("out", [NTOK, 512], F32, kind="ExternalOutput").ap()
    with ExitStack() as st:
        S = Sched(nc, st)
        only = DEBUG.get("only")
        if only in (None, "mla"):
            phase_B_mla(nc, S, nb, SL, lat[0:768, :], lat[768:1280, :], lat[1280:1408, :], pos64, wuq, wukv, gq, gkv, rc, out, 0)
        if only in (None, "hg"):
            phase_B_hg(nc, S, nb, SL, layer, hqf[0:128, :], hqf[128:256, :], hig[:, 0:128], hig[:, 128:256], lbr, gout, tri, ident, out, 256)
        if only in (None, "ca"):
            phase_B_ca(nc, S, nb, SL, aqk[0:128, :], aqk[128:256, :], av, biasT, out, 384)
        S.finish()
        S.emit()
    return nc


def build_CD(T, with_A):
    nc = bass.Bass("TRN2", target_bir_lowering=False)
    d = lambda n, s, t=F32: nc.dram_tensor(n, s, t, kind="ExternalInput").ap()
    o = d("o", [T, D_MODEL])
    x = d("x", [T, D_MODEL])
    gpc_o = d("gpc_o", [128, 32])
    w_out = d("w_out", [D_MODEL, D_MODEL])
    gpost_a = d("gpost_a", [128, D_MODEL])
    gpc_f = d("gpc_f", [128, 32])
    wg = d("wg", [D_MODEL, D_FF])
    wu = d("wu", [D_MODEL, D_FF])
    wd = d("wd", [D_FF, D_MODEL])
    gpost_f = d("gpost_f", [128, D_MODEL])
    ident = d("ident", [128, 128])
    yscr = nc.dram_tensor("yscr", [T, D_MODEL], F32, kind="Internal").ap()
    x1 = nc.dram_tensor("x1", [T, D_MODEL], F32, kind="Internal").ap()
    x2 = nc.dram_tensor("x2", [T, D_MODEL], F32, kind="ExternalOutput").ap()
    if with_A:
        gpc_n = d("gpc_n", [128, 32])
        wfm = d("wfm", [D_MODEL, NFM])
        wtm = d("wtm", [D_MODEL, NTM])
        hT = nc.dram_tensor("hT", [NFM, T], F32, kind="ExternalOutput").ap()
        htm = nc.dram_tensor("htm", [T, NTM], F32, kind="ExternalOutput").ap()
    with ExitStack() as st:
        S = Sched(nc, st)
        phase_C(nc, S, T, o, x, gpc_o, w_out, gpost_a, ident, yscr, x1)
        phase_D(nc, S, T, x1, gpc_f, wg, wu, wd, gpost_f, ident, yscr, x2)
        if with_A:
            phase_A(nc, S, T, x2, gpc_n, wfm, wtm, ident, hT, htm)
        S.finish()
        S.emit()
    return nc


def _run(nc, in_maps):
    res = run_bass_kernel_spmd(nc, in_maps, core_ids=list(range(NCORES)))
    return res.results


def kernel(x, positions, attn_pre_norm, attn_post_norm, w_in, mla_q_norm, mla_kv_norm, w_uq, w_ukv,
           mla_out_norm, hg_lower_bounds, hg_out_norm, ca_rel_bias, ca_out_norm, w_out, ffn_pre_norm,
           ffn_post_norm, w_gate, w_up, w_down):
    f32 = lambda a: np.ascontiguousarray(np.asarray(a, dtype=np.float32))
    x = f32(x)
    NTOK = BATCH * SEQ
    T = NTOK // NCORES
    xs = x.reshape(NTOK, D_MODEL)
    pos64 = np.ascontiguousarray(np.broadcast_to(np.asarray(positions, np.int32).reshape(1, NTOK), (64, NTOK)))
    rc = rope_consts()
    ones1024 = np.ones(1024, np.float32)
    wsplit = [split_w_in(f32(w_in[l])) for l in range(DEPTH)]
    xcur = [np.ascontiguousarray(xs[c * T:(c + 1) * T]) for c in range(NCORES)]

    ncA = build_A(T)
    resA = _run(ncA, [{"x": xcur[c], "gpc": gain_pc(attn_pre_norm[0]), "wfm": wsplit[0][0], "wtm": wsplit[0][1], "ident": IDENT}
                      for c in range(NCORES)])
    hT_parts = [r["hT"] for r in resA]
    htm_parts = [r["htm"] for r in resA]
    for l in range(DEPTH):
        hT_all = np.concatenate(hT_parts, axis=1)
        htm_all = np.concatenate(htm_parts, axis=0)
        del hT_parts, htm_parts
        lat = np.ascontiguousarray(hT_all[0:1408])
        ncB = build_B(l)
        in_maps = []
        for c in range(NCORES):
            wuq_c, wukv_c = mla_weights_for_core(f32(w_uq[l]), f32(w_ukv[l]), [2 * c, 2 * c + 1])
            hs = slice(c * 128, (c + 1) * 128)
            in_maps.append({
                "lat": lat, "pos64": pos64, "wuq": wuq_c, "wukv": wukv_c,
                "gq": gain_pc(mla_q_norm[l]), "gkv": gain_pc(mla_kv_norm[l]), "rc": rc,
                "hqf": np.ascontiguousarray(np.concatenate([hT_all[1408 + c * 128:1408 + (c + 1) * 128], hT_all[2432 + c * 128:2432 + (c + 1) * 128]], 0)),
                "hig": np.ascontiguousarray(np.concatenate([htm_all[:, hs], htm_all[:, 1024 + c * 128:1024 + (c + 1) * 128]], 1)),
                "lbr": np.ascontiguousarray(f32(hg_lower_bounds)[:, hs].T),
                "gout": bcast128(f32(hg_out_norm[l])[hs]), "tri": TRI, "ident": IDENT,
                "aqk": np.ascontiguousarray(np.concatenate([hT_all[3456 + c * 128:3456 + (c + 1) * 128], hT_all[4480 + c * 128:4480 + (c + 1) * 128]], 0)),
                "av": np.ascontiguousarray(htm_all[:, 2048 + c * 128:2048 + (c + 1) * 128]),
                "biasT": ca_bias_tile(f32(ca_rel_bias[l])[c]),
            })
        del hT_all, htm_all
        resB = _run(ncB, in_maps)
        del in_maps, lat
        o_all = np.empty((NTOK, D_MODEL), np.float32)
        for c in range(NCORES):
            oc = resB[c]["out"]
            o_all[:, 2 * c * 128:(2 * c + 2) * 128] = oc[:, 0:256]
            o_all[:, 2048 + c * 128:2048 + (c + 1) * 128] = oc[:, 256:384]
            o_all[:, 3072 + c * 128:3072 + (c + 1) * 128] = oc[:, 384:512]
        del resB
        last = (l == DEPTH - 1)
        ncCD = build_CD(T, with_A=not last)
        gpc_o = gain_pc(np.concatenate([f32(mla_out_norm[l]), ones1024, f32(ca_out_norm[l])]))
        base = {"gpc_o": gpc_o, "w_out": f32(w_out[l]), "gpost_a": bcast128(f32(attn_post_norm[l])),
                "gpc_f": gain_pc(ffn_pre_norm[l]), "wg": f32(w_gate[l]), "wu": f32(w_up[l]), "wd": f32(w_down[l]),
                "gpost_f": bcast128(f32(ffn_post_norm[l])), "ident": IDENT}
        if not last:
            base.update({"gpc_n": gain_pc(attn_pre_norm[l + 1]), "wfm": wsplit[l + 1][0], "wtm": wsplit[l + 1][1]})
        in_maps = []
        for c in range(NCORES):
            m = dict(base)
            m["o"] = np.ascontiguousarray(o_all[c * T:(c + 1) * T])
            m["x"] = xcur[c]
            in_maps.append(m)
        del o_all
        resC = _run(ncCD, in_maps)
        del in_maps
        xcur = [r["x2"] for r in resC]
        if not last:
            hT_parts = [r["hT"] for r in resC]
            htm_parts = [r["htm"] for r in resC]
        del resC
    out = np.concatenate(xcur, axis=0).reshape(BATCH, SEQ, D_MODEL).astype(np.float32)
    return out
```

```python
import math
import numpy as np
import concourse.bass as bass
import concourse.mybir as mybir
from concourse.bass_utils import run_bass_kernel_spmd
from contextlib import ExitStack

F32 = mybir.dt.float32
BF16 = mybir.dt.bfloat16
I32 = mybir.dt.int32
AF = mybir.ActivationFunctionType
ALU = mybir.AluOpType

NCORES = 8
DEBUG = {}
D_MODEL = 4096
BATCH = 2
SEQ = 8192
DEPTH = 2
EPS = 1e-6
TINY = 1e-30
MLA_HEADS = 16
QR = 768
KVR = 512
ROPE = 64
D_FF = 11008
D_IN = 8512
NFM = 768 + 512 + 128 + 1024 + 1024 + 1024 + 1024
NTM = 3072
TWO_PI = 2.0 * math.pi


class Buf:
    __slots__ = ("w", "r", "name")

    def __init__(self, name=""):
        self.w = None
        self.r = []
        self.name = name


class TileH:
    def __init__(self, t, name):
        self.t = t
        self.b = Buf(name)


class Sched:
    ENG = ("pe", "act", "dve", "pool", "sp")

    def __init__(self, nc, stack, n_dma_sems=48):
        self.nc = nc
        self.prog = {e: [] for e in self.ENG}
        self.sem = {e: stack.enter_context(nc.semaphore("s_" + e)) for e in self.ENG}
        self.cnt = {e: 0 for e in self.ENG}
        self.seen = {e: {} for e in self.ENG}
        self.dsem = [stack.enter_context(nc.semaphore("d%d" % i)) for i in range(n_dma_sems)]
        self.dtot = [0] * n_dma_sems
        nsp = (2 * n_dma_sems) // 3
        self.dpool = {"sp": list(range(0, nsp)), "pool": list(range(nsp, n_dma_sems))}
        self.dnext = {"sp": 0, "pool": 0}
        self.ninst = 0

    def _waits(self, e, reads, writes, extra=()):
        d = {}

        def add(ev):
            if ev is None:
                return
            k = ev[0]
            if k not in d or d[k][1] < ev[1]:
                d[k] = ev
        for b in reads:
            add(b.w)
        for b in writes:
            add(b.w)
            for ev in b.r:
                add(ev)
        for ev in extra:
            add(ev)
        waits = []
        for k, ev in d.items():
            if e == "pe" and k is self.sem["pe"]:
                continue
            if self.seen[e].get(k, 0) < ev[1]:
                self.seen[e][k] = ev[1]
                waits.append(ev)
        return waits

    def _commit(self, ev, reads, writes):
        for b in writes:
            b.w = ev
            b.r = []
        for b in reads:
            b.r.append(ev)
            if len(b.r) > 64:
                m = {}
                for x in b.r:
                    if x[0] not in m or m[x[0]][1] < x[1]:
                        m[x[0]] = x
                b.r = list(m.values())

    def op(self, e, fn, reads=(), writes=(), inc=True):
        waits = self._waits(e, reads, writes)
        if inc:
            self.cnt[e] += 1
            ev = (self.sem[e], self.cnt[e])
            self.prog[e].append((waits, fn, self.sem[e], 1))
            self._commit(ev, reads, writes)
        else:
            self.prog[e].append((waits, fn, None, 0))
        self.ninst += 1

    def dma(self, q, out, in_, reads=(), writes=()):
        lst = self.dpool[q]
        k = lst[self.dnext[q] % len(lst)]
        self.dnext[q] += 1
        extra = []
        if self.dtot[k] > 0:
            extra.append((self.dsem[k], self.dtot[k]))
        waits = self._waits(q, reads, writes, extra)
        self.dtot[k] += 16
        ev = (self.dsem[k], self.dtot[k])
        self.prog[q].append((waits, lambda eng, o=out, i=in_: eng.dma_start(out=o, in_=i), self.dsem[k], 16))
        self._commit(ev, reads, writes)
        self.ninst += 1

    def barrier(self):
        evs = [(self.sem[e], self.cnt[e]) for e in self.ENG if self.cnt[e] > 0]
        evs += [(self.dsem[k], t) for k, t in enumerate(self.dtot) if t > 0]
        for e in self.ENG:
            waits = []
            for ev in evs:
                if self.seen[e].get(ev[0], 0) < ev[1]:
                    self.seen[e][ev[0]] = ev[1]
                    waits.append(ev)
            if waits:
                self.prog[e].append((waits, None, None, 0))

    def finish(self):
        self.barrier()

    def emit(self):
        progs = self.prog

        def run(eng, lst):
            for waits, fn, sem, inc in lst:
                for (s, v) in waits:
                    eng.wait_ge(s, v)
                if fn is not None:
                    ins = fn(eng)
                    if sem is not None:
                        ins.then_inc(sem, inc)

        with self.nc.Block() as block:
            @block.sync
            def _(eng):
                run(eng, progs["sp"])

            @block.tensor
            def _(eng):
                run(eng, progs["pe"])

            @block.scalar
            def _(eng):
                run(eng, progs["act"])

            @block.vector
            def _(eng):
                run(eng, progs["dve"])

            @block.gpsimd
            def _(eng):
                run(eng, progs["pool"])


_UID = [0]


class Ctx:
    def __init__(self, nc, st):
        self.nc = nc
        self.st = st

    def sb(self, name, shape, dt):
        _UID[0] += 1
        return TileH(self.st.enter_context(self.nc.sbuf_tensor("%s_%d" % (name, _UID[0]), list(shape), dt)), name)

    def ps(self, name, shape, dt=F32):
        _UID[0] += 1
        return TileH(self.st.enter_context(self.nc.psum_tensor("%s_%d" % (name, _UID[0]), list(shape), dt)), name)


class RR:
    def __init__(self, tiles):
        self.tiles = tiles
        self.i = 0

    def next(self):
        t = self.tiles[self.i % len(self.tiles)]
        self.i += 1
        return t


def rstd(S, out_ap, out_b, in_ap, in_b, inv_n):
    S.op("act", lambda e: e.activation(out_ap, in_ap, AF.Sqrt, bias=EPS, scale=inv_n), reads=[in_b], writes=[out_b])
    S.op("dve", lambda e: e.reciprocal(out_ap, out_ap), reads=[out_b], writes=[out_b])


def evac_dve(S, out_ap, in_ap, reads, writes):
    S.op("dve", lambda e: e.tensor_copy(out_ap, in_ap), reads=reads, writes=writes)


def evac(S, flip, out_ap, in_ap, reads, writes):
    flip[0] ^= 1
    if flip[0]:
        S.op("act", lambda e: e.activation(out_ap, in_ap, AF.Copy), reads=reads, writes=writes)
    else:
        S.op("dve", lambda e: e.tensor_copy(out_ap, in_ap), reads=reads, writes=writes)


class Front:
    def __init__(self, S, cx, D, ident, gpc, ps_t, nbuf=2):
        self.S, self.D = S, D
        self.xs = RR([cx.sb("xs", [128, D], F32) for _ in range(nbuf)])
        self.junk = cx.sb("junk", [128, 1024], BF16)
        self.ssq = cx.sb("ssq", [128, 4], F32)
        self.r = cx.sb("r", [128, 1], F32)
        self.diag = cx.sb("diag", [128, 128], F32)
        self.ident, self.gpc, self.ps_t = ident, gpc, ps_t
        self.flip = [0]

    def run(self, x_rows, s, xnT, groups=None):
        for _ in self.run_gen(x_rows, s, xnT, groups):
            pass

    def run_gen(self, x_rows, s, xnT, groups=None):
        S, D = self.S, self.D
        xs, junk, ssq, r, diag = self.xs.next(), self.junk, self.ssq, self.r, self.diag
        ident, gpc = self.ident, self.gpc
        S.dma("sp", xs.t[:, :], x_rows, writes=[xs.b])
        yield
        if groups is None:
            groups = [(0, D, True)]
        for (lo, hi, do_norm) in groups:
            w = hi - lo
            if do_norm:
                npc = w // 1024
                for q in range(npc):
                    S.op("act", lambda e, a=lo + q * 1024, q=q: e.activation(junk.t[:, :], xs.t[:, a:a + 1024], AF.Square, accum_out=ssq.t[:, q:q + 1]),
                         reads=[xs.b], writes=[junk.b, ssq.b])
                if npc > 1:
                    S.op("dve", lambda e, npc=npc: e.tensor_reduce(ssq.t[:, 0:1], ssq.t[:, 0:npc], mybir.AxisListType.X, ALU.add), reads=[ssq.b], writes=[ssq.b])
                rstd(S, r.t[:, 0:1], r.b, ssq.t[:, 0:1], ssq.b, 1.0 / w)
                S.op("dve", lambda e: e.tensor_scalar(diag.t[:, :], ident.t[:, :], r.t[:, 0:1], None, ALU.mult),
                     reads=[ident.b, r.b], writes=[diag.b])
                dg = diag
            else:
                dg = ident
            for cg in range(lo // 512, hi // 512):
                pt = self.ps_t.next()
                for k in range(4):
                    c = cg * 4 + k
                    S.op("pe", lambda e, c=c, k=k, pt=pt, dg=dg: e.matmul(pt.t[:, k * 128:(k + 1) * 128], xs.t[:, c * 128:(c + 1) * 128], dg.t[:, :], start=True, stop=True),
                         reads=[xs.b, dg.b], writes=[pt.b], inc=(k == 3))
                self.flip[0] ^= 1
                for k in range(4):
                    c = cg * 4 + k
                    if self.flip[0]:
                        S.op("act", lambda e, c=c, k=k, pt=pt: e.activation(xnT.t[:, c, s * 128:(s + 1) * 128], pt.t[:, k * 128:(k + 1) * 128], AF.Copy, scale=gpc.t[:, c:c + 1]),
                             reads=[pt.b, gpc.b], writes=[xnT.b])
                    else:
                        S.op("dve", lambda e, c=c, k=k, pt=pt: e.tensor_scalar(xnT.t[:, c, s * 128:(s + 1) * 128], pt.t[:, k * 128:(k + 1) * 128], gpc.t[:, c:c + 1], None, ALU.mult),
                             reads=[pt.b, gpc.b], writes=[xnT.b])
                yield


def mm_fm(S, wq, nchunk, wch_rr, w_dram, col, xnT, TT, ps_rr, ncols=128):
    wch = wch_rr.next()
    S.dma(wq, wch.t[:, 0:nchunk, 0:ncols], w_dram[:, col:col + ncols].rearrange("(c p) n -> p c n", p=128), writes=[wch.b])
    pt = ps_rr.next()
    for c in range(nchunk):
        S.op("pe", lambda e, c=c, pt=pt, wch=wch: e.matmul(pt.t[0:ncols, 0:TT], wch.t[:, c, 0:ncols], xnT.t[:, c, 0:TT], start=(c == 0), stop=(c == nchunk - 1)),
             reads=[wch.b, xnT.b], writes=[pt.b], inc=(c == nchunk - 1))
    return pt


def phase_A(nc, S, T, x, gbc_d, wfm, wtm, ident_d, hT, htm, nfm=NFM, ntm=NTM, D=D_MODEL):
    TT = 1024 if T % 1024 == 0 else 512
    NH = TT // 512
    NC = D // 128
    with ExitStack() as st:
        cx = Ctx(nc, st)
        ident = cx.sb("ident", [128, 128], F32)
        gbc = cx.sb("gpc", [128, NC], F32)
        S.dma("sp", ident.t[:, :], ident_d, writes=[ident.b])
        S.dma("sp", gbc.t[:, :], gbc_d, writes=[gbc.b])
        ps_t = RR([cx.ps("ps_t", [128, 512]) for _ in range(2)])
        ps_m = RR([cx.ps("ps_m", [128, 512]) for _ in range(6)])
        fr = Front(S, cx, D, ident, gbc, ps_t)
        xnT = cx.sb("xnT", [128, NC, TT], BF16)
        wch = RR([cx.sb("wch", [128, NC, 256], BF16) for _ in range(3)])
        stg = RR([cx.sb("stg", [128, 512], F32) for _ in range(4)])
        flip = [0]
        for tt in range(T // TT):
            for s in range(TT // 128):
                fr.run(x[tt * TT + s * 128: tt * TT + (s + 1) * 128, :], s, xnT)
            for j in range(nfm // 128):
                w = wch.next()
                S.dma("pool", w.t[:, :, 0:128], wfm[:, j * 128:(j + 1) * 128].rearrange("(c p) n -> p c n", p=128), writes=[w.b])
                for hh in range(NH):
                    pt = ps_m.next()
                    for c in range(NC):
                        S.op("pe", lambda e, c=c, pt=pt, w=w, hh=hh: e.matmul(pt.t[:, :], w.t[:, c, 0:128], xnT.t[:, c, hh * 512:(hh + 1) * 512], start=(c == 0), stop=(c == NC - 1)),
                             reads=[w.b, xnT.b], writes=[pt.b], inc=(c == NC - 1))
                    sg = stg.next()
                    evac(S, flip, sg.t[:, :], pt.t[:, :], [pt.b], [sg.b])
                    S.dma("sp", hT[j * 128:(j + 1) * 128, tt * TT + hh * 512:tt * TT + (hh + 1) * 512], sg.t[:, :], reads=[sg.b])
            for j in range(ntm // 256):
                w = wch.next()
                S.dma("pool", w.t[:, :, :], wtm[:, j * 256:(j + 1) * 256].rearrange("(c p) n -> p c n", p=128), writes=[w.b])
                for s in range(TT // 128):
                    pt = ps_m.next()
                    for c in range(NC):
                        S.op("pe", lambda e, c=c, pt=pt, w=w, s=s: e.matmul(pt.t[:, 0:256], xnT.t[:, c, s * 128:(s + 1) * 128], w.t[:, c, :], start=(c == 0), stop=(c == NC - 1)),
                             reads=[w.b, xnT.b], writes=[pt.b], inc=(c == NC - 1))
                    sg = stg.next()
                    evac(S, flip, sg.t[:, 0:256], pt.t[:, 0:256], [pt.b], [sg.b])
                    S.dma("sp", htm[tt * TT + s * 128: tt * TT + (s + 1) * 128, j * 256:(j + 1) * 256], sg.t[:, 0:256], reads=[sg.b])
        S.barrier()


def build_A(T):
    nc = bass.Bass("TRN2", target_bir_lowering=False)
    x = nc.dram_tensor("x", [T, D_MODEL], F32, kind="ExternalInput").ap()
    gbc = nc.dram_tensor("gpc", [128, D_MODEL // 128], F32, kind="ExternalInput").ap()
    wfm = nc.dram_tensor("wfm", [D_MODEL, NFM], F32, kind="ExternalInput").ap()
    wtm = nc.dram_tensor("wtm", [D_MODEL, NTM], F32, kind="ExternalInput").ap()
    ident = nc.dram_tensor("ident", [128, 128], F32, kind="ExternalInput").ap()
    hT = nc.dram_tensor("hT", [NFM, T], F32, kind="ExternalOutput").ap()
    htm = nc.dram_tensor("htm", [T, NTM], F32, kind="ExternalOutput").ap()
    with ExitStack() as st:
        S = Sched(nc, st)
        phase_A(nc, S, T, x, gbc, wfm, wtm, ident, hT, htm)
        S.finish()
        S.emit()
    return nc


ROPE_PERM = np.concatenate([np.arange(32, 64), np.arange(0, 32)])


def split_w_in(w_in_l):
    o = np.cumsum([0, 768, 512, 64, 1024, 1024, 1024, 1024, 1024, 1024, 1024])
    cq, ckv, kr, hq, hf, hi, hg, aq, ak, av = [np.arange(o[i], o[i + 1]) for i in range(10)]
    fm_cols = np.concatenate([cq, ckv, kr, kr[ROPE_PERM], hq, hf, aq, ak])
    tm_cols = np.concatenate([hi, hg, av])
    return np.ascontiguousarray(w_in_l[:, fm_cols]), np.ascontiguousarray(w_in_l[:, tm_cols])


def bcast128(v):
    return np.ascontiguousarray(np.broadcast_to(np.asarray(v, np.float32)[None, :], (128, v.shape[0])))


IDENT = np.eye(128, dtype=np.float32)


C1_2PI = 6.28125
C2_2PI = TWO_PI - 6.28125
MLA_SCALE = 192.0 ** -0.5
CA_SCALE = 128.0 ** -0.5
VW = 132


def phase_B_mla(nc, S, nb, SL, cqT, ckvT, krT, pos64, wuq, wukv, gq_d, gkv_d, rc_d, out, ocol0):
    TT = 512
    NT = SL // TT
    NKB = SL // 128
    with ExitStack() as st:
        cx = Ctx(nc, st)
        ones = cx.sb("ones", [128, 128], BF16)
        S.op("pool", lambda e: e.memset(ones.t[:, :], 1.0), writes=[ones.b])
        gq = cx.sb("gq", [128, 6], F32)
        gkv = cx.sb("gkv", [128, 4], F32)
        rc = cx.sb("rc", [64, 2], F32)
        S.dma("sp", gq.t[:, :], gq_d, writes=[gq.b])
        S.dma("sp", gkv.t[:, :], gkv_d, writes=[gkv.b])
        S.dma("sp", rc.t[:, :], rc_d, writes=[rc.b])
        wq = cx.sb("wq", [128, 6, 512], BF16)
        wkv = cx.sb("wkv", [128, 4, 512], BF16)
        S.dma("pool", wq.t[:, :, :], wuq.rearrange("(c p) n -> p c n", p=128), writes=[wq.b])
        S.dma("pool", wkv.t[:, :, :], wukv.rearrange("(c p) n -> p c n", p=128), writes=[wkv.b])
        KT = [cx.sb("KT%d" % h, [128, SL], BF16) for h in range(2)]
        Kpe = cx.sb("Kpe", [64, SL], BF16)
        V = cx.sb("V", [128, NKB, 2, VW], BF16)
        KTb = [[Buf() for _ in range(NT)] for _ in range(2)]
        Kpeb = [Buf() for _ in range(NT)]
        Vb = [Buf() for _ in range(NT)]
        S.op("pool", lambda e: e.memset(V.t[:, :, :, :], 1.0), writes=Vb)
        cq = cx.sb("cq", [128, 6, TT], F32)
        ckv = cx.sb("ckv", [128, 4, TT], F32)
        sqq = cx.sb("sqq", [128, 6, TT], BF16)
        sqk = cx.sb("sqk", [128, 4, TT], BF16)
        kr = cx.sb("kr", [64, TT], F32)
        krp = cx.sb("krp", [64, TT], F32)
        posi = cx.sb("posi", [64, TT], I32)
        ang = cx.sb("ang", [64, TT], F32)
        ki = cx.sb("ki", [64, TT], I32)
        kf = cx.sb("kf", [64, TT], F32)
        yy = cx.sb("yy", [64, TT], F32)
        ya = cx.sb("ya", [64, TT], F32)
        Ct = cx.sb("Ct", [64, TT], F32)
        St = cx.sb("St", [64, TT], F32)
        t1 = cx.sb("t1", [64, TT], F32)
        t2 = cx.sb("t2", [64, TT], F32)
        cqn = cx.sb("cqn", [128, 6, TT], BF16)
        ckvn = cx.sb("ckvn", [128, 4, TT], BF16)
        rbq = cx.sb("rbq", [128, TT], F32)
        rbk = cx.sb("rbk", [128, TT], F32)
        epsb = cx.sb("epsb", [128, 1], F32)
        S.op("pool", lambda e: e.memset(epsb.t[:, :], EPS), writes=[epsb.b])
        qn = [[cx.sb("qn%d" % h, [128, TT], BF16) for h in range(2)] for _ in range(2)]
        qpe = [[cx.sb("qpe%d" % h, [64, TT], BF16) for h in range(2)] for _ in range(2)]
        pT = RR([cx.sb("pT", [128, TT], BF16) for _ in range(3)])
        osb = RR([cx.sb("osb", [128, 128], F32) for _ in range(2)])
        rec = cx.sb("rec", [128, 1], F32)
        ps_s = RR([cx.ps("ps_s", [128, 512]) for _ in range(2)])
        ps_o = [cx.ps("ps_o", [128, 512]) for _ in range(4)]
        ps_p = RR([cx.ps("ps_p", [128, 512]) for _ in range(2)])
        flip = [0]

        def rope(src_a, src_b, dst_ap, dst_b, src_reads):
            S.op("dve", lambda e: e.tensor_tensor(t1.t[:, :], src_a, Ct.t[:, :], ALU.mult), reads=src_reads + [Ct.b], writes=[t1.b])
            S.op("dve", lambda e: e.tensor_tensor(t2.t[:, :], src_b, St.t[:, :], ALU.mult), reads=src_reads + [St.b], writes=[t2.b])
            S.op("dve", lambda e: e.tensor_tensor(dst_ap, t1.t[:, :], t2.t[:, :], ALU.add), reads=[t1.b, t2.b], writes=[dst_b])

        def prologue(b, i):
            t0 = b * SL + i * TT
            c0 = i * TT
            par = i % 2
            S.dma("sp", cq.t[:, :, :], cqT[:, t0:t0 + TT].rearrange("(c p) t -> p c t", p=128), writes=[cq.b])
            S.dma("sp", ckv.t[:, :, :], ckvT[:, t0:t0 + TT].rearrange("(c p) t -> p c t", p=128), writes=[ckv.b])
            S.dma("sp", kr.t[:, :], krT[0:64, t0:t0 + TT], writes=[kr.b])
            S.dma("sp", krp.t[:, :], krT[64:128, t0:t0 + TT], writes=[krp.b])
            S.dma("sp", posi.t[:, :], pos64[:, t0:t0 + TT], writes=[posi.b])
            yield
            S.op("pool", lambda e: e.tensor_tensor(sqq.t[:, :, :], cq.t[:, :, :], cq.t[:, :, :], ALU.mult), reads=[cq.b], writes=[sqq.b])
            S.op("pool", lambda e: e.tensor_tensor(sqk.t[:, :, :], ckv.t[:, :, :], ckv.t[:, :, :], ALU.mult), reads=[ckv.b], writes=[sqk.b])
            S.op("dve", lambda e: e.tensor_copy(ang.t[:, :], posi.t[:, :]), reads=[posi.b], writes=[ang.b])
            S.op("dve", lambda e: e.tensor_scalar(ang.t[:, :], ang.t[:, :], rc.t[:, 0:1], None, ALU.mult), reads=[ang.b, rc.b], writes=[ang.b])
            S.op("dve", lambda e: e.tensor_scalar(ki.t[:, :], ang.t[:, :], 1.0 / TWO_PI, None, ALU.mult), reads=[ang.b], writes=[ki.b])
            S.op("dve", lambda e: e.tensor_copy(kf.t[:, :], ki.t[:, :]), reads=[ki.b], writes=[kf.b])
            S.op("dve", lambda e: e.scalar_tensor_tensor(yy.t[:, :], kf.t[:, :], -C1_2PI, ang.t[:, :], ALU.mult, ALU.add), reads=[kf.b, ang.b], writes=[yy.b])
            S.op("dve", lambda e: e.scalar_tensor_tensor(yy.t[:, :], kf.t[:, :], -C2_2PI, yy.t[:, :], ALU.mult, ALU.add), reads=[kf.b, yy.b], writes=[yy.b])
            S.op("dve", lambda e: e.tensor_scalar(yy.t[:, :], yy.t[:, :], -math.pi, math.pi, ALU.max, ALU.min), reads=[yy.b], writes=[yy.b])
            S.op("dve", lambda e: e.scalar_tensor_tensor(ya.t[:, :], yy.t[:, :], -1.0, yy.t[:, :], ALU.mult, ALU.max), reads=[yy.b], writes=[ya.b])
            S.op("act", lambda e: e.activation(St.t[:, :], yy.t[:, :], AF.Sin), reads=[yy.b], writes=[St.b])
            S.op("act", lambda e: e.activation(Ct.t[:, :], ya.t[:, :], AF.Sin, bias=math.pi / 2, scale=-1.0), reads=[ya.b], writes=[Ct.b])
            S.op("dve", lambda e: e.tensor_scalar(St.t[:, :], St.t[:, :], rc.t[:, 1:2], None, ALU.mult), reads=[St.b, rc.b], writes=[St.b])
            yield
            psq = ps_p.next()
            for c in range(6):
                S.op("pe", lambda e, c=c: e.matmul(psq.t[:, :], ones.t[:, :], sqq.t[:, c, :], start=(c == 0), stop=(c == 5)),
                     reads=[ones.b, sqq.b], writes=[psq.b], inc=(c == 5))
            psk = ps_p.next()
            for c in range(4):
                S.op("pe", lambda e, c=c: e.matmul(psk.t[:, :], ones.t[:, :], sqk.t[:, c, :], start=(c == 0), stop=(c == 3)),
                     reads=[ones.b, sqk.b], writes=[psk.b], inc=(c == 3))
            yield
            for (rb, pq, inv_n) in ((rbq, psq, 1.0 / 768), (rbk, psk, 1.0 / 512)):
                S.op("act", lambda e, rb=rb, pq=pq, inv_n=inv_n: e.activation(rb.t[:, :], pq.t[:, :], AF.Ln, bias=epsb.t[:, 0:1], scale=inv_n), reads=[pq.b, epsb.b], writes=[rb.b])
                S.op("act", lambda e, rb=rb: e.activation(rb.t[:, :], rb.t[:, :], AF.Exp, scale=-0.5), reads=[rb.b], writes=[rb.b])
            for c in range(4):
                S.op("dve", lambda e, c=c: e.scalar_tensor_tensor(ckvn.t[:, c, :], ckv.t[:, c, :], gkv.t[:, c:c + 1], rbk.t[:, :], ALU.mult, ALU.mult),
                     reads=[ckv.b, gkv.b, rbk.b], writes=[ckvn.b])
            for c in range(6):
                S.op("dve", lambda e, c=c: e.scalar_tensor_tensor(cqn.t[:, c, :], cq.t[:, c, :], gq.t[:, c:c + 1], rbq.t[:, :], ALU.mult, ALU.mult),
                     reads=[cq.b, gq.b, rbq.b], writes=[cqn.b])
            rope(kr.t[:, :], krp.t[:, :], Kpe.t[:, c0:c0 + TT], Kpeb[i], [kr.b, krp.b])
            yield
            for h in range(2):
                ps = ps_p.next()
                for c in range(4):
                    S.op("pe", lambda e, c=c, ps=ps, h=h: e.matmul(ps.t[:, :], wkv.t[:, c, h * 128:(h + 1) * 128], ckvn.t[:, c, :], start=(c == 0), stop=(c == 3)),
                         reads=[wkv.b, ckvn.b], writes=[ps.b], inc=(c == 3))
                evac_dve(S, KT[h].t[:, c0:c0 + TT], ps.t[:, :], [ps.b], [KTb[h][i]])
            for s in range(4):
                ps = ps_p.next()
                for c in range(4):
                    S.op("pe", lambda e, c=c, ps=ps, s=s: e.matmul(ps.t[:, 0:256], ckvn.t[:, c, s * 128:(s + 1) * 128], wkv.t[:, c, 256:512], start=(c == 0), stop=(c == 3)),
                         reads=[wkv.b, ckvn.b], writes=[ps.b], inc=(c == 3))
                evac_dve(S, V.t[:, i * 4 + s, :, 0:128], ps.t[:, 0:256].rearrange("p (h v) -> p h v", h=2), [ps.b], [Vb[i]])
                if s == 1:
                    yield
            yield
            for h in range(2):
                ps = ps_p.next()
                for c in range(6):
                    S.op("pe", lambda e, c=c, ps=ps, h=h: e.matmul(ps.t[:, :], wq.t[:, c, h * 256:h * 256 + 128], cqn.t[:, c, :], start=(c == 0), stop=(c == 5)),
                         reads=[wq.b, cqn.b], writes=[ps.b], inc=(c == 5))
                evac_dve(S, qn[par][h].t[:, :], ps.t[:, :], [ps.b], [qn[par][h].b])
                yield
                psa = ps_p.next()
                for c in range(6):
                    S.op("pe", lambda e, c=c, ps=psa, h=h: e.matmul(ps.t[0:64, :], wq.t[:, c, h * 256 + 128:h * 256 + 192], cqn.t[:, c, :], start=(c == 0), stop=(c == 5)),
                         reads=[wq.b, cqn.b], writes=[psa.b], inc=(c == 5))
                psb = ps_p.next()
                for c in range(6):
                    S.op("pe", lambda e, c=c, ps=psb, h=h: e.matmul(ps.t[0:64, :], wq.t[:, c, h * 256 + 192:h * 256 + 256], cqn.t[:, c, :], start=(c == 0), stop=(c == 5)),
                         reads=[wq.b, cqn.b], writes=[psb.b], inc=(c == 5))
                rope(psa.t[0:64, :], psb.t[0:64, :], qpe[par][h].t[:, :], qpe[par][h].b, [psa.b, psb.b])
                yield

        def attention(b, i, gen):
            t0 = b * SL + i * TT
            par = i % 2
            nkb = 4 * i + 4
            total = 2 * nkb
            NSTAGE = 12
            every = max(1, total // NSTAGE)
            delay = min(total // 4, 12)
            stepno = 0
            if gen is not None:
                next(gen, None)
            for h in range(2):
                def qk(kb, h=h):
                    off = kb - 4 * i
                    qs = max(0, off) * 128
                    N = TT - qs
                    ps = ps_s.next()
                    S.op("pe", lambda e, ps=ps, kb=kb, qs=qs, N=N: e.matmul(ps.t[:, 0:N], KT[h].t[:, kb * 128:(kb + 1) * 128], qn[par][h].t[:, qs:TT], start=True, stop=False),
                         reads=[KTb[h][kb // 4], qn[par][h].b], writes=[ps.b], inc=False)
                    S.op("pe", lambda e, ps=ps, kb=kb, qs=qs, N=N: e.matmul(ps.t[:, 0:N], Kpe.t[:, kb * 128:(kb + 1) * 128], qpe[par][h].t[:, qs:TT], start=False, stop=True),
                         reads=[Kpeb[kb // 4], qpe[par][h].b], writes=[ps.b])
                    return ps, off, qs, N

                cur = qk(0)
                for kb in range(nkb):
                    nxt = qk(kb + 1) if kb + 1 < nkb else None
                    ps, off, qs, N = cur
                    p = pT.next()
                    S.op("act", lambda e, ps=ps, p=p, N=N: e.activation(p.t[:, 0:N], ps.t[:, 0:N], AF.Exp, scale=MLA_SCALE), reads=[ps.b], writes=[p.b])
                    if off >= 0:
                        S.op("pool", lambda e, p=p: e.memset(p.t[64:128, 0:64], 0.0), writes=[p.b])
                    js = list(range(max(0, off), 4))
                    for j in js:
                        po = ps_o[j]
                        lastj = (j == js[-1])
                        S.op("pe", lambda e, p=p, po=po, j=j, qs=qs, kb=kb, h=h: e.matmul(po.t[:, 0:129], p.t[:, j * 128 - qs:j * 128 - qs + 128], V.t[:, kb, h, 0:129], start=(kb == 0), stop=(kb == 4 * i + j)),
                             reads=[p.b, Vb[kb // 4]], writes=([ps_o[jj].b for jj in js] if (lastj or j == js[0]) else []), inc=lastj)
                    cur = nxt
                    stepno += 1
                    if gen is not None and stepno >= delay and stepno % every == 0:
                        next(gen, None)
                for j in range(4):
                    po = ps_o[j]
                    S.op("dve", lambda e, po=po: e.reciprocal(rec.t[:, 0:1], po.t[:, 128:129]), reads=[po.b], writes=[rec.b])
                    ob = osb.next()
                    S.op("dve", lambda e, po=po, ob=ob: e.tensor_scalar(ob.t[:, :], po.t[:, 0:128], rec.t[:, 0:1], None, ALU.mult), reads=[po.b, rec.b], writes=[ob.b])
                    S.dma("sp", out[t0 + j * 128:t0 + (j + 1) * 128, ocol0 + h * 128:ocol0 + (h + 1) * 128], ob.t[:, :], reads=[ob.b])
            if gen is not None:
                for _ in gen:
                    pass

        for b in range(nb):
            for _ in prologue(b, 0):
                pass
            for i in range(NT):
                gen = prologue(b, i + 1) if i + 1 < NT else None
                attention(b, i, gen)
        S.barrier()


def rope_consts():
    inv = np.exp(-math.log(10000.0) * 2.0 * np.arange(32, dtype=np.float32) / 64).astype(np.float32)
    rc = np.zeros((64, 2), np.float32)
    rc[:, 0] = np.concatenate([inv, inv])
    rc[:, 1] = np.concatenate([-np.ones(32, np.float32), np.ones(32, np.float32)])
    return rc


def mla_weights_for_core(w_uq_l, w_ukv_l, heads):
    qcols, kcols, vcols = [], [], []
    for h in heads:
        base = h * 192
        rope = base + 128 + np.arange(64)
        qcols.append(np.concatenate([base + np.arange(128), rope, rope[ROPE_PERM]]))
        kcols.append(h * 256 + np.arange(128))
        vcols.append(h * 256 + 128 + np.arange(128))
    wuq = np.ascontiguousarray(w_uq_l[:, np.concatenate(qcols)])
    wukv = np.ascontiguousarray(w_ukv_l[:, np.concatenate(kcols + vcols)])
    return wuq, wukv


def gain_pc(g):
    return np.ascontiguousarray(np.asarray(g, np.float32).reshape(-1, 128).T)


def phase_B_ca(nc, S, nb, SL, aqT, akT, av, biasT_d, out, ocol):
    NKB = SL // 128
    with ExitStack() as st:
        cx = Ctx(nc, st)
        EB = cx.sb("EB", [128, 640], F32)
        S.dma("sp", EB.t[:, :], biasT_d, writes=[EB.b])
        S.op("act", lambda e: e.activation(EB.t[:, :], EB.t[:, :], AF.Exp), reads=[EB.b], writes=[EB.b])
        S.op("pool", lambda e: e.memset(EB.t[0:64, 576:640], 0.0), writes=[EB.b])
        S.op("pool", lambda e: e.memset(EB.t[64:128, 0:64], 0.0), writes=[EB.b])
        QT = cx.sb("QT", [128, SL], BF16)
        KT = cx.sb("KT", [128, SL], BF16)
        V = cx.sb("V", [128, NKB, VW], BF16)
        e32 = RR([cx.sb("e32", [128, 128], F32) for _ in range(2)])
        pb = RR([cx.sb("pb", [128, 128], BF16) for _ in range(2)])
        osb = RR([cx.sb("osb", [128, 128], F32) for _ in range(2)])
        rec = cx.sb("rec", [128, 1], F32)
        ps_s = RR([cx.ps("ps_s", [128, 512]) for _ in range(2)])
        ps_o = RR([cx.ps("ps_o", [128, 512]) for _ in range(2)])
        for b in range(nb):
            tb = b * SL
            S.op("pool", lambda e: e.memset(V.t[:, :, :], 1.0), writes=[V.b])
            S.dma("pool", QT.t[:, :], aqT[:, tb:tb + SL], writes=[QT.b])
            S.dma("pool", KT.t[:, :], akT[:, tb:tb + SL], writes=[KT.b])
            for g in range(0, NKB, 16):
                n = min(16, NKB - g)
                S.dma("pool", V.t[:, g:g + n, 0:128], av[tb + g * 128:tb + (g + n) * 128, :].rearrange("(n p) d -> p n d", p=128), writes=[V.b])
            steps = [(m, kb) for m in range(NKB) for kb in range(max(0, m - 4), m + 1)]

            def qk(m, kb):
                ps = ps_s.next()
                S.op("pe", lambda e, ps=ps, kb=kb, m=m: e.matmul(ps.t[:, 0:128], KT.t[:, kb * 128:(kb + 1) * 128], QT.t[:, m * 128:(m + 1) * 128], start=True, stop=True),
                     reads=[KT.b, QT.b], writes=[ps.b])
                return ps

            cur = qk(*steps[0])
            acc = None
            for si, (m, kb) in enumerate(steps):
                nxt = qk(*steps[si + 1]) if si + 1 < len(steps) else None
                first = max(0, m - 4)
                if kb == first:
                    acc = ps_o.next()
                ps = cur
                ee = e32.next()
                S.op("act", lambda e, ps=ps, ee=ee: e.activation(ee.t[:, :], ps.t[:, 0:128], AF.Exp, scale=CA_SCALE), reads=[ps.b], writes=[ee.b])
                p = pb.next()
                S.op("dve", lambda e, ee=ee, p=p, kb=kb, m=m: e.tensor_tensor(p.t[:, :], ee.t[:, :], EB.t[:, (m - kb) * 128:(m - kb + 1) * 128], ALU.mult),
                     reads=[ee.b, EB.b], writes=[p.b])
                S.op("pe", lambda e, p=p, acc=acc, kb=kb, m=m, first=first: e.matmul(acc.t[:, 0:129], p.t[:, :], V.t[:, kb, 0:129], start=(kb == first), stop=(kb == m)),
                     reads=[p.b, V.b], writes=[acc.b])
                if kb == m:
                    S.op("dve", lambda e, acc=acc: e.reciprocal(rec.t[:, 0:1], acc.t[:, 128:129]), reads=[acc.b], writes=[rec.b])
                    ob = osb.next()
                    S.op("dve", lambda e, acc=acc, ob=ob: e.tensor_scalar(ob.t[:, :], acc.t[:, 0:128], rec.t[:, 0:1], None, ALU.mult), reads=[acc.b, rec.b], writes=[ob.b])
                    S.dma("sp", out[tb + m * 128:tb + (m + 1) * 128, ocol:ocol + 128], ob.t[:, :], reads=[ob.b])
                cur = nxt
        S.barrier()


HG_BIG = 4.7e18


def phase_B_hg(nc, S, nb, SL, layer, hqT, hfT, hi, hgate, lbraw_d, gout_d, tri_d, ident_d, out, ocol):
    TT = 512
    L = 64
    NCH = TT // L
    NT = SL // TT
    with ExitStack() as st:
        cx = Ctx(nc, st)
        tri = cx.sb("tri", [128, 128], F32)
        gout = cx.sb("gout", [128, 128], F32)
        idf = cx.sb("idf", [128, 128], F32)
        idb = cx.sb("idb", [128, 128], BF16)
        lbr = cx.sb("lbr", [128, DEPTH], F32)
        S.dma("sp", tri.t[:, :], tri_d, writes=[tri.b])
        S.dma("sp", gout.t[:, :], gout_d, writes=[gout.b])
        S.dma("sp", idf.t[:, :], ident_d, writes=[idf.b])
        S.dma("sp", lbr.t[:, :], lbraw_d, writes=[lbr.b])
        S.op("dve", lambda e: e.tensor_copy(idb.t[:, :], idf.t[:, :]), reads=[idf.b], writes=[idb.b])
        sm = cx.sb("sm", [128, 8], F32)
        S.op("dve", lambda e: e.tensor_tensor(sm.t[:, 0:1], lbr.t[:, 1:2], lbr.t[:, 0:1], ALU.subtract), reads=[lbr.b], writes=[sm.b])
        S.op("act", lambda e: e.activation(sm.t[:, 1:2], sm.t[:, 0:1], AF.Sigmoid), reads=[sm.b], writes=[sm.b])
        S.op("act", lambda e: e.activation(sm.t[:, 2:3], sm.t[:, 0:1], AF.Sigmoid, scale=-1.0), reads=[sm.b], writes=[sm.b])
        if layer == 0:
            S.op("dve", lambda e: e.tensor_tensor(sm.t[:, 4:5], sm.t[:, 2:3], sm.t[:, 2:3], ALU.subtract), reads=[sm.b], writes=[sm.b])
        else:
            S.op("dve", lambda e: e.tensor_tensor(sm.t[:, 3:4], sm.t[:, 2:3], sm.t[:, 1:2], ALU.add), reads=[sm.b], writes=[sm.b])
            S.op("dve", lambda e: e.tensor_tensor(sm.t[:, 4:5], sm.t[:, 3:4], sm.t[:, 2:3], ALU.subtract), reads=[sm.b], writes=[sm.b])
        S.op("dve", lambda e: e.tensor_scalar(sm.t[:, 5:6], sm.t[:, 4:5], -1.0, 1.0, ALU.mult, ALU.add), reads=[sm.b], writes=[sm.b])
        lb_ap = sm.t[:, 4:5]
        oml_ap = sm.t[:, 5:6]
        mask = cx.sb("mask", [128, TT], F32)
        S.op("pool", lambda e: e.memset(mask.t[:, :], 1.0), writes=[mask.b])
        for k in range(NCH):
            S.op("pool", lambda e, k=k: e.memset(mask.t[:, k * L:k * L + 1], 0.0), writes=[mask.b])
        class Set:
            pass
        sets = []
        for b in range(nb):
            B = Set()
            for nm in ("hq", "hf", "sg", "sgn", "ff", "bb", "eq", "ek", "sq"):
                setattr(B, nm, cx.sb(nm, [128, TT], F32))
            B.qT = cx.sb("qT", [128, TT], BF16)
            B.kT = cx.sb("kT", [128, TT], BF16)
            B.ktm = cx.sb("ktm", [L, NCH, 128], BF16)
            B.vtm = cx.sb("vtm", [L, NCH, 128], BF16)
            B.gtm = cx.sb("gtm", [L, NCH, 128], F32)
            B.sc = cx.sb("sc", [128, 4 * NCH], F32)
            B.AT = RR([cx.sb("AT", [L, L], BF16) for _ in range(2)])
            B.state = cx.sb("state", [128, 128], F32)
            B.tmp = cx.sb("tmp", [128, 128], F32)
            B.stbf = cx.sb("stbf", [128, 128], BF16)
            B.junk = cx.sb("junk", [L, 128], F32)
            B.ssq = cx.sb("ssq", [L, 1], F32)
            B.rr = cx.sb("rr", [L, 1], F32)
            B.o1 = cx.sb("o1", [L, 128], F32)
            B.osb = RR([cx.sb("osb", [L, 128], F32) for _ in range(2)])
            B.ps_a = cx.ps("ps_a", [128, 512])
            B.ps_o = cx.ps("ps_o", [128, 512])
            B.ps_d = cx.ps("ps_d", [128, 512])
            B.ps_tr = cx.ps("ps_tr", [128, 1024], BF16)
            B.flip = [b]
            sets.append(B)

        def prep(b, i):
            B = sets[b]
            hq, hf, sg, sgn, ff, bb, eq, ek, sq, qT, kT, ktm, vtm, gtm, sc = B.hq, B.hf, B.sg, B.sgn, B.ff, B.bb, B.eq, B.ek, B.sq, B.qT, B.kT, B.ktm, B.vtm, B.gtm, B.sc
            t0 = b * SL + i * TT
            S.dma("sp", hq.t[:, :], hqT[:, t0:t0 + TT], writes=[hq.b])
            S.dma("sp", hf.t[:, :], hfT[:, t0:t0 + TT], writes=[hf.b])
            S.dma("pool", vtm.t[:, :, :], hi[t0:t0 + TT, :].rearrange("(n p) d -> p n d", p=L), writes=[vtm.b])
            S.dma("sp", gtm.t[:, :, :], hgate[t0:t0 + TT, :].rearrange("(n p) d -> p n d", p=L), writes=[gtm.b])
            S.op("act", lambda e: e.activation(sg.t[:, :], hf.t[:, :], AF.Sigmoid), reads=[hf.b], writes=[sg.b])
            S.op("act", lambda e: e.activation(sgn.t[:, :], hf.t[:, :], AF.Sigmoid, scale=-1.0), reads=[hf.b], writes=[sgn.b])
            S.op("act", lambda e: e.activation(sq.t[:, :], hq.t[:, :], AF.Silu), reads=[hq.b], writes=[sq.b])
            S.op("act", lambda e: e.activation(gtm.t[:, :, :], gtm.t[:, :, :], AF.Silu), reads=[gtm.b], writes=[gtm.b])
            S.op("dve", lambda e: e.tensor_scalar(ff.t[:, :], sg.t[:, :], oml_ap, lb_ap, ALU.mult, ALU.add), reads=[sg.b, sm.b], writes=[ff.b])
            S.op("dve", lambda e: e.tensor_scalar_max(ff.t[:, :], ff.t[:, :], TINY), reads=[ff.b], writes=[ff.b])
            S.op("act", lambda e: e.activation(ff.t[:, :], ff.t[:, :], AF.Ln), reads=[ff.b], writes=[ff.b])
            S.op("dve", lambda e: e.tensor_tensor_scan(bb.t[:, :], mask.t[:, :], ff.t[:, :], 0.0, ALU.mult, ALU.add), reads=[mask.b, ff.b], writes=[bb.b])
            S.op("dve", lambda e: e.tensor_scalar(sgn.t[:, :], sgn.t[:, :], oml_ap, None, ALU.mult), reads=[sgn.b, sm.b], writes=[sgn.b])
            for n in range(NCH):
                c0 = n * L
                mid = bb.t[:, c0 + L // 2 - 1:c0 + L // 2]
                last = bb.t[:, c0 + L - 1:c0 + L]
                S.op("dve", lambda e, n=n, mid=mid: e.tensor_scalar(sc.t[:, n:n + 1], mid, -1.0, None, ALU.mult), reads=[bb.b], writes=[sc.b])
                S.op("act", lambda e, n=n, c0=c0: e.activation(eq.t[:, c0:c0 + L], bb.t[:, c0:c0 + L], AF.Exp, bias=sc.t[:, n:n + 1]), reads=[bb.b, sc.b], writes=[eq.b])
                S.op("act", lambda e, n=n, c0=c0, mid=mid: e.activation(ek.t[:, c0:c0 + L], bb.t[:, c0:c0 + L], AF.Exp, bias=mid, scale=-1.0), reads=[bb.b], writes=[ek.b])
                S.op("act", lambda e, n=n, mid=mid: e.activation(sc.t[:, NCH + n:NCH + n + 1], mid, AF.Exp), reads=[bb.b], writes=[sc.b])
                S.op("act", lambda e, n=n, last=last: e.activation(sc.t[:, 2 * NCH + n:2 * NCH + n + 1], last, AF.Exp, bias=sc.t[:, n:n + 1]), reads=[bb.b, sc.b], writes=[sc.b])
                S.op("act", lambda e, n=n, last=last: e.activation(sc.t[:, 3 * NCH + n:3 * NCH + n + 1], last, AF.Exp), reads=[bb.b], writes=[sc.b])
            S.op("dve", lambda e: e.scalar_tensor_tensor(qT.t[:, :], eq.t[:, :], HG_BIG, sq.t[:, :], ALU.min, ALU.mult), reads=[sq.b, eq.b], writes=[qT.b])
            S.op("dve", lambda e: e.scalar_tensor_tensor(kT.t[:, :], ek.t[:, :], HG_BIG, sgn.t[:, :], ALU.min, ALU.mult), reads=[sgn.b, ek.b], writes=[kT.b])
            ps_tr = B.ps_tr
            for n in range(NCH):
                S.op("pe", lambda e, n=n: e.transpose(ps_tr.t[0:L, n * 128:(n + 1) * 128], kT.t[:, n * L:(n + 1) * L], idb.t[:, :]),
                     reads=[kT.b, idb.b], writes=[ps_tr.b], inc=(n == NCH - 1))
            evac(S, B.flip, ktm.t[:, :, :], ps_tr.t[0:L, :].rearrange("p (n k) -> p n k", n=NCH), [ps_tr.b], [ktm.b])

        def chunk(b, i, n):
            B = sets[b]
            qT, kT, ktm, vtm, gtm, sc, state, tmp, stbf, junk, ssq, rr, o1 = B.qT, B.kT, B.ktm, B.vtm, B.gtm, B.sc, B.state, B.tmp, B.stbf, B.junk, B.ssq, B.rr, B.o1
            t0 = b * SL + i * TT
            c0 = n * L
            pa = B.ps_a
            S.op("pe", lambda e, c0=c0: e.matmul(pa.t[0:L, 0:L], kT.t[:, c0:c0 + L], qT.t[:, c0:c0 + L], start=True, stop=True),
                 reads=[kT.b, qT.b], writes=[pa.b])
            at = B.AT.next()
            S.op("dve", lambda e, at=at: e.tensor_tensor(at.t[:, :], pa.t[0:L, 0:L], tri.t[0:L, 0:L], ALU.mult), reads=[pa.b, tri.b], writes=[at.b])
            S.op("dve", lambda e, n=n: e.tensor_scalar(stbf.t[:, :], state.t[:, :], sc.t[:, NCH + n:NCH + n + 1], None, ALU.mult), reads=[state.b, sc.b], writes=[stbf.b])
            po = B.ps_o
            S.op("pe", lambda e, at=at, n=n: e.matmul(po.t[0:L, 0:128], at.t[:, :], vtm.t[:, n, :], start=True, stop=False),
                 reads=[at.b, vtm.b], writes=[po.b], inc=False)
            S.op("pe", lambda e, c0=c0: e.matmul(po.t[0:L, 0:128], qT.t[:, c0:c0 + L], stbf.t[:, :], start=False, stop=True),
                 reads=[qT.b, stbf.b], writes=[po.b])
            ps_d = B.ps_d
            S.op("pe", lambda e, n=n: e.matmul(ps_d.t[:, 0:128], ktm.t[:, n, :], vtm.t[:, n, :], start=True, stop=True),
                 reads=[ktm.b, vtm.b], writes=[ps_d.b])
            S.op("dve", lambda e, n=n: e.tensor_scalar(tmp.t[:, :], state.t[:, :], sc.t[:, 3 * NCH + n:3 * NCH + n + 1], None, ALU.mult), reads=[state.b, sc.b], writes=[tmp.b])
            S.op("dve", lambda e, n=n: e.scalar_tensor_tensor(state.t[:, :], ps_d.t[:, 0:128], sc.t[:, 2 * NCH + n:2 * NCH + n + 1], tmp.t[:, :], ALU.mult, ALU.add),
                 reads=[ps_d.b, sc.b, tmp.b], writes=[state.b])
            S.op("act", lambda e: e.activation(junk.t[:, :], po.t[0:L, 0:128], AF.Square, accum_out=ssq.t[:, 0:1]), reads=[po.b], writes=[junk.b, ssq.b])
            rstd(S, rr.t[:, 0:1], rr.b, ssq.t[:, 0:1], ssq.b, 1.0 / 128)
            S.op("dve", lambda e: e.scalar_tensor_tensor(o1.t[:, :], po.t[0:L, 0:128], rr.t[:, 0:1], gout.t[0:L, :], ALU.mult, ALU.mult),
                 reads=[po.b, rr.b, gout.b], writes=[o1.b])
            ob = B.osb.next()
            S.op("dve", lambda e, ob=ob, n=n: e.tensor_tensor(ob.t[:, :], o1.t[:, :], gtm.t[:, n, :], ALU.mult), reads=[o1.b, gtm.b], writes=[ob.b])
            S.dma("sp", out[t0 + c0:t0 + c0 + L, ocol:ocol + 128], ob.t[:, :], reads=[ob.b])

        for b in range(nb):
            S.op("pool", lambda e, b=b: e.memset(sets[b].state.t[:, :], 0.0), writes=[sets[b].state.b])
        for i in range(NT):
            for b in range(nb):
                prep(b, i)
            for n in range(NCH):
                for b in range(nb):
                    chunk(b, i, n)
        S.barrier()


def ca_bias_tile(rel_bias_h):
    ki = np.arange(128)[:, None]
    qi = np.arange(640)[None, :]
    return np.ascontiguousarray(rel_bias_h[np.clip(qi - ki, -256, 256) + 256].astype(np.float32))


TRI = np.ascontiguousarray((np.arange(128)[:, None] <= np.arange(128)[None, :]).astype(np.float32))


def mm_tm_stream(S, lhsT, nK, w_dram, wch_rr, ps_rr, stg_rr, junk, yscr, row0, ssqp, yb, hook=None):
    G = 8
    ngrp = (nK + G - 1) // G
    for j in range(DEBUG.get("nj", 8)):
        acc = [ps_rr.next() for _ in range(4)]
        for g in range(ngrp):
            k0 = g * G
            kn = min(G, nK - k0)
            w = wch_rr.next()
            wv = w.t[:, :].rearrange("p (c n) -> p c n", n=512)
            S.dma("pool", wv[:, 0:kn, :], w_dram[k0 * 128:(k0 + kn) * 128, j * 512:(j + 1) * 512].rearrange("(c p) n -> p c n", p=128), writes=[w.b])
            for s in range(4):
                for k in range(kn):
                    S.op("pe", lambda e, a=acc[s], s=s, k=k, k0=k0, wv=wv, g=g, kn=kn: e.matmul(a.t[:, :], lhsT.t[:, k0 + k, s * 128:(s + 1) * 128], wv[:, k, :], start=(g == 0 and k == 0), stop=(g == ngrp - 1 and k == kn - 1)),
                         reads=[lhsT.b, w.b], writes=[acc[s].b], inc=(k == kn - 1))
            if hook is not None:
                hook()
        for s in range(4):
            sg = stg_rr.next()
            S.op("dve", lambda e, sg=sg, a=acc[s]: e.tensor_copy(sg.t[:, :], a.t[:, :]), reads=[acc[s].b], writes=[sg.b])
            if not DEBUG.get("nosq"):
                S.op("act", lambda e, sg=sg, s=s, j=j: e.activation(junk.t[:, 0:512], sg.t[:, :], AF.Square, accum_out=ssqp.t[:, s * 8 + j:s * 8 + j + 1]),
                     reads=[sg.b], writes=[junk.b, ssqp.b])
            if not DEBUG.get("nostore"):
                S.dma("sp", yscr[row0 + s * 128:row0 + (s + 1) * 128, j * 512:(j + 1) * 512], sg.t[:, :], reads=[sg.b], writes=[yb[s][j]])


def tail(S, yscr, row0, resid, outp, ssqp, yb, gpost, tl_rr, rt, pe_id):
    for s in range(4):
        r0 = row0 + s * 128
        S.op("dve", lambda e, s=s: e.tensor_reduce(rt.t[:, 0:1], ssqp.t[:, s * 8:(s + 1) * 8], mybir.AxisListType.X, ALU.add), reads=[ssqp.b], writes=[rt.b])
        rstd(S, rt.t[:, 0:1], rt.b, rt.t[:, 0:1], rt.b, 1.0 / D_MODEL)
        for j in range(8):
            ybk = tl_rr.next()
            xbk = tl_rr.next()
            S.dma("sp", ybk.t[:, :], yscr[r0:r0 + 128, j * 512:(j + 1) * 512], reads=[yb[s][j]], writes=[ybk.b])
            S.dma("sp", xbk.t[:, :], resid[r0:r0 + 128, j * 512:(j + 1) * 512], writes=[xbk.b])
            S.op("dve", lambda e, ybk=ybk, j=j: e.scalar_tensor_tensor(ybk.t[:, :], ybk.t[:, :], rt.t[:, 0:1], gpost.t[:, j * 512:(j + 1) * 512], ALU.mult, ALU.mult),
                 reads=[ybk.b, rt.b, gpost.b], writes=[ybk.b])
            S.op("pool", lambda e, ybk=ybk, xbk=xbk: e.tensor_tensor(xbk.t[:, :], ybk.t[:, :], xbk.t[:, :], ALU.add), reads=[ybk.b, xbk.b], writes=[xbk.b])
            S.dma("sp", outp[r0:r0 + 128, j * 512:(j + 1) * 512], xbk.t[:, :], reads=[xbk.b])


def phase_C(nc, S, T, o, x, gpc_d, w_out, gpost_d, ident_d, yscr, x1):
    TT = 512
    NC = D_MODEL // 128
    groups = [(0, 2048, True), (2048, 3072, False), (3072, 4096, True)]
    with ExitStack() as st:
        cx = Ctx(nc, st)
        ident = cx.sb("ident", [128, 128], F32)
        gpc = cx.sb("gpc", [128, NC], F32)
        gpost = cx.sb("gpost", [128, D_MODEL], F32)
        S.dma("sp", ident.t[:, :], ident_d, writes=[ident.b])
        S.dma("sp", gpc.t[:, :], gpc_d, writes=[gpc.b])
        S.dma("sp", gpost.t[:, :], gpost_d, writes=[gpost.b])
        ps_t = RR([cx.ps("ps_t", [128, 512]) for _ in range(2)])
        ps_m = RR([cx.ps("ps_m", [128, 512]) for _ in range(6)])
        fr = Front(S, cx, D_MODEL, ident, gpc, ps_t, nbuf=2)
        oTs = [cx.sb("oT", [128, NC, TT], BF16) for _ in range(2)]
        wch = RR([cx.sb("wch", [128, 4096], BF16) for _ in range(3)])
        stg = RR([cx.sb("stg", [128, 512], F32) for _ in range(2)])
        tl = RR([cx.sb("tl", [128, 512], F32) for _ in range(4)])
        ssqp = cx.sb("ssqp", [128, 32], F32)
        rt = cx.sb("rt", [128, 1], F32)
        NTT = T // TT

        def front_all(tt):
            for s in range(4):
                yield from fr.run_gen(o[tt * TT + s * 128: tt * TT + (s + 1) * 128, :], s, oTs[tt % 2], groups=groups)

        for _ in front_all(0):
            pass
        for tt in range(NTT):
            yb = [[Buf() for _ in range(8)] for _ in range(4)]
            nxt = front_all(tt + 1) if tt + 1 < NTT else None

            def hook(nxt=nxt):
                if nxt is not None:
                    next(nxt, None)
                    next(nxt, None)
            mm_tm_stream(S, oTs[tt % 2], NC, w_out, wch, ps_m, stg, fr.junk, yscr, tt * TT, ssqp, yb, hook=hook)
            if nxt is not None:
                for _ in nxt:
                    pass
            tail(S, yscr, tt * TT, x, x1, ssqp, yb, gpost, tl, rt, None)
        S.barrier()


def phase_D(nc, S, T, x1, gpc_d, wg, wu, wd, gpost_d, ident_d, yscr, x2, dff=D_FF):
    TT = 512
    NC = D_MODEL // 128
    NF = dff // 128
    with ExitStack() as st:
        cx = Ctx(nc, st)
        ident = cx.sb("ident", [128, 128], F32)
        gpc = cx.sb("gpc", [128, NC], F32)
        gpost = cx.sb("gpost", [128, D_MODEL], F32)
        S.dma("sp", ident.t[:, :], ident_d, writes=[ident.b])
        S.dma("sp", gpc.t[:, :], gpc_d, writes=[gpc.b])
        S.dma("sp", gpost.t[:, :], gpost_d, writes=[gpost.b])
        ps_t = RR([cx.ps("ps_t", [128, 512]) for _ in range(2)])
        ps_m = RR([cx.ps("ps_m", [128, 512]) for _ in range(6)])
        fr = Front(S, cx, D_MODEL, ident, gpc, ps_t, nbuf=1)
        hT = cx.sb("hT", [128, NC, TT], BF16)
        actT = cx.sb("actT", [128, NF, TT], BF16)
        wch = RR([cx.sb("wch", [128, 4096], BF16) for _ in range(3)])
        stg = RR([cx.sb("stg", [128, 512], F32) for _ in range(2)])
        sgl = RR([cx.sb("sgl", [128, 512], F32) for _ in range(2)])
        tl = RR([cx.sb("tl", [128, 512], F32) for _ in range(4)])
        ssqp = cx.sb("ssqp", [128, 32], F32)
        rt = cx.sb("rt", [128, 1], F32)
        NTT = T // TT

        def front_all(tt):
            for s in range(4):
                yield from fr.run_gen(x1[tt * TT + s * 128: tt * TT + (s + 1) * 128, :], s, hT)

        for _ in front_all(0):
            pass
        for tt in range(NTT):
            yb = [[Buf() for _ in range(8)] for _ in range(4)]
            for f in range(NF):
                pp = []
                for wmat in (wg, wu):
                    w = wch.next()
                    wv = w.t[:, :].rearrange("p (c n) -> p c n", n=128)
                    S.dma("pool", wv[:, :, :], wmat[:, f * 128:(f + 1) * 128].rearrange("(c p) n -> p c n", p=128), writes=[w.b])
                    pt = ps_m.next()
                    for c in range(NC):
                        S.op("pe", lambda e, c=c, pt=pt, wv=wv: e.matmul(pt.t[:, :], wv[:, c, :], hT.t[:, c, :], start=(c == 0), stop=(c == NC - 1)),
                             reads=[w.b, hT.b], writes=[pt.b], inc=(c == NC - 1))
                    pp.append(pt)
                sg = sgl.next()
                S.op("act", lambda e, sg=sg, pg=pp[0]: e.activation(sg.t[:, :], pg.t[:, :], AF.Silu), reads=[pp[0].b], writes=[sg.b])
                S.op("dve", lambda e, sg=sg, pu=pp[1], f=f: e.tensor_tensor(actT.t[:, f, :], sg.t[:, :], pu.t[:, :], ALU.mult), reads=[sg.b, pp[1].b], writes=[actT.b])
            nxt = front_all(tt + 1) if tt + 1 < NTT else None

            def hook(nxt=nxt):
                if nxt is not None:
                    next(nxt, None)
            mm_tm_stream(S, actT, NF, wd, wch, ps_m, stg, fr.junk, yscr, tt * TT, ssqp, yb, hook=hook)
            if nxt is not None:
                for _ in nxt:
                    pass
            tail(S, yscr, tt * TT, x1, x2, ssqp, yb, gpost, tl, rt, None)
        S.barrier()


def build_B(layer, nb=BATCH, SL=SEQ):
    NTOK = nb * SL
    nc = bass.Bass("TRN2", target_bir_lowering=False)
    d = lambda n, s, t=F32: nc.dram_tensor(n, s, t, kind="ExternalInput").ap()
    lat = d("lat", [1408, NTOK])
    pos64 = d("pos64", [64, NTOK], I32)
    wuq = d("wuq", [768, 512])
    wukv = d("wukv", [512, 512])
    gq = d("gq", [128, 6])
    gkv = d("gkv", [128, 4])
    rc = d("rc", [64, 2])
    hqf = d("hqf", [256, NTOK])
    hig = d("hig", [NTOK, 256])
    lbr = d("lbr", [128, DEPTH])
    gout = d("gout", [128, 128])
    tri = d("tri", [128, 128])
    ident = d("ident", [128, 128])
    aqk = d("aqk", [256, NTOK])
    av = d("av", [NTOK, 128])
    biasT = d("biasT", [128, 640])
    out = nc.dram_tensor("out", [NTOK, 512], F32, kind="ExternalOutput").ap()
    with ExitStack() as st:
        S = Sched(nc, st)
        only = DEBUG.get("only")
        if only in (None, "mla"):
            phase_B_mla(nc, S, nb, SL, lat[0:768, :], lat[768:1280, :], lat[1280:1408, :], pos64, wuq, wukv, gq, gkv, rc, out, 0)
        if only in (None, "hg"):
            phase_B_hg(nc, S, nb, SL, layer, hqf[0:128, :], hqf[128:256, :], hig[:, 0:128], hig[:, 128:256], lbr, gout, tri, ident, out, 256)
        if only in (None, "ca"):
            phase_B_ca(nc, S, nb, SL, aqk[0:128, :], aqk[128:256, :], av, biasT, out, 384)
        S.finish()
        S.emit()
    return nc


def build_CD(T, with_A):
    nc = bass.Bass("TRN2", target_bir_lowering=False)
    d = lambda n, s, t=F32: nc.dram_tensor(n, s, t, kind="ExternalInput").ap()
    o = d("o", [T, D_MODEL])
    x = d("x", [T, D_MODEL])
    gpc_o = d("gpc_o", [128, 32])
    w_out = d("w_out", [D_MODEL, D_MODEL])
    gpost_a = d("gpost_a", [128, D_MODEL])
    gpc_f = d("gpc_f", [128, 32])
    wg = d("wg", [D_MODEL, D_FF])
    wu = d("wu", [D_MODEL, D_FF])
    wd = d("wd", [D_FF, D_MODEL])
    gpost_f = d("gpost_f", [128, D_MODEL])
    ident = d("ident", [128, 128])
    yscr = nc.dram_tensor("yscr", [T, D_MODEL], F32, kind="Internal").ap()
    x1 = nc.dram_tensor("x1", [T, D_MODEL], F32, kind="Internal").ap()
    x2 = nc.dram_tensor("x2", [T, D_MODEL], F32, kind="ExternalOutput").ap()
    if with_A:
        gpc_n = d("gpc_n", [128, 32])
        wfm = d("wfm", [D_MODEL, NFM])
        wtm = d("wtm", [D_MODEL, NTM])
        hT = nc.dram_tensor("hT", [NFM, T], F32, kind="ExternalOutput").ap()
        htm = nc.dram_tensor("htm", [T, NTM], F32, kind="ExternalOutput").ap()
    with ExitStack() as st:
        S = Sched(nc, st)
        phase_C(nc, S, T, o, x, gpc_o, w_out, gpost_a, ident, yscr, x1)
        phase_D(nc, S, T, x1, gpc_f, wg, wu, wd, gpost_f, ident, yscr, x2)
        if with_A:
            phase_A(nc, S, T, x2, gpc_n, wfm, wtm, ident, hT, htm)
        S.finish()
        S.emit()
    return nc


def _run(nc, in_maps):
    res = run_bass_kernel_spmd(nc, in_maps, core_ids=list(range(NCORES)))
    return res.results


def kernel_unfused(x, positions, attn_pre_norm, attn_post_norm, w_in, mla_q_norm, mla_kv_norm, w_uq, w_ukv,
           mla_out_norm, hg_lower_bounds, hg_out_norm, ca_rel_bias, ca_out_norm, w_out, ffn_pre_norm,
           ffn_post_norm, w_gate, w_up, w_down):
    f32 = lambda a: np.ascontiguousarray(np.asarray(a, dtype=np.float32))
    x = f32(x)
    NTOK = BATCH * SEQ
    T = NTOK // NCORES
    xs = x.reshape(NTOK, D_MODEL)
    pos64 = np.ascontiguousarray(np.broadcast_to(np.asarray(positions, np.int32).reshape(1, NTOK), (64, NTOK)))
    rc = rope_consts()
    ones1024 = np.ones(1024, np.float32)
    wsplit = [split_w_in(f32(w_in[l])) for l in range(DEPTH)]
    xcur = [np.ascontiguousarray(xs[c * T:(c + 1) * T]) for c in range(NCORES)]

    ncA = build_A(T)
    resA = _run(ncA, [{"x": xcur[c], "gpc": gain_pc(attn_pre_norm[0]), "wfm": wsplit[0][0], "wtm": wsplit[0][1], "ident": IDENT}
                      for c in range(NCORES)])
    hT_parts = [r["hT"] for r in resA]
    htm_parts = [r["htm"] for r in resA]
    for l in range(DEPTH):
        hT_all = np.concatenate(hT_parts, axis=1)
        htm_all = np.concatenate(htm_parts, axis=0)
        del hT_parts, htm_parts
        lat = np.ascontiguousarray(hT_all[0:1408])
        ncB = build_B(l)
        in_maps = []
        for c in range(NCORES):
            wuq_c, wukv_c = mla_weights_for_core(f32(w_uq[l]), f32(w_ukv[l]), [2 * c, 2 * c + 1])
            hs = slice(c * 128, (c + 1) * 128)
            in_maps.append({
                "lat": lat, "pos64": pos64, "wuq": wuq_c, "wukv": wukv_c,
                "gq": gain_pc(mla_q_norm[l]), "gkv": gain_pc(mla_kv_norm[l]), "rc": rc,
                "hqf": np.ascontiguousarray(np.concatenate([hT_all[1408 + c * 128:1408 + (c + 1) * 128], hT_all[2432 + c * 128:2432 + (c + 1) * 128]], 0)),
                "hig": np.ascontiguousarray(np.concatenate([htm_all[:, hs], htm_all[:, 1024 + c * 128:1024 + (c + 1) * 128]], 1)),
                "lbr": np.ascontiguousarray(f32(hg_lower_bounds)[:, hs].T),
                "gout": bcast128(f32(hg_out_norm[l])[hs]), "tri": TRI, "ident": IDENT,
                "aqk": np.ascontiguousarray(np.concatenate([hT_all[3456 + c * 128:3456 + (c + 1) * 128], hT_all[4480 + c * 128:4480 + (c + 1) * 128]], 0)),
                "av": np.ascontiguousarray(htm_all[:, 2048 + c * 128:2048 + (c + 1) * 128]),
                "biasT": ca_bias_tile(f32(ca_rel_bias[l])[c]),
            })
        del hT_all, htm_all
        resB = _run(ncB, in_maps)
        del in_maps, lat
        o_all = np.empty((NTOK, D_MODEL), np.float32)
        for c in range(NCORES):
            oc = resB[c]["out"]
            o_all[:, 2 * c * 128:(2 * c + 2) * 128] = oc[:, 0:256]
            o_all[:, 2048 + c * 128:2048 + (c + 1) * 128] = oc[:, 256:384]
            o_all[:, 3072 + c * 128:3072 + (c + 1) * 128] = oc[:, 384:512]
        del resB
        last = (l == DEPTH - 1)
        ncCD = build_CD(T, with_A=not last)
        gpc_o = gain_pc(np.concatenate([f32(mla_out_norm[l]), ones1024, f32(ca_out_norm[l])]))
        base = {"gpc_o": gpc_o, "w_out": f32(w_out[l]), "gpost_a": bcast128(f32(attn_post_norm[l])),
                "gpc_f": gain_pc(ffn_pre_norm[l]), "wg": f32(w_gate[l]), "wu": f32(w_up[l]), "wd": f32(w_down[l]),
                "gpost_f": bcast128(f32(ffn_post_norm[l])), "ident": IDENT}
        if not last:
            base.update({"gpc_n": gain_pc(attn_pre_norm[l + 1]), "wfm": wsplit[l + 1][0], "wtm": wsplit[l + 1][1]})
        in_maps = []
        for c in range(NCORES):
            m = dict(base)
            m["o"] = np.ascontiguousarray(o_all[c * T:(c + 1) * T])
            m["x"] = xcur[c]
            in_maps.append(m)
        del o_all
        resC = _run(ncCD, in_maps)
        del in_maps
        xcur = [r["x2"] for r in resC]
        if not last:
            hT_parts = [r["hT"] for r in resC]
            htm_parts = [r["htm"] for r in resC]
        del resC
    out = np.concatenate(xcur, axis=0).reshape(BATCH, SEQ, D_MODEL).astype(np.float32)
    return out


def build_fused(SL=SEQ):
    T = SL
    nc = bass.Bass("TRN2", target_bir_lowering=False)
    d = lambda n, s, t=F32: nc.dram_tensor(n, s, t, kind="ExternalInput").ap()
    x = d("x", [T, D_MODEL])
    pos64 = d("pos64", [64, T], I32)
    rc = d("rc", [64, 2])
    tri = d("tri", [128, 128])
    ident = d("ident", [128, 128])
    L = []
    for l in range(DEPTH):
        p = "l%d_" % l
        L.append(dict(
            gpc_pre=d(p + "gpc_pre", [128, 32]), wfm=d(p + "wfm", [D_MODEL, NFM]), wtm=d(p + "wtm", [D_MODEL, NTM]),
            wuq=d(p + "wuq", [8 * 768, 512]), wukv=d(p + "wukv", [8 * 512, 512]), gq=d(p + "gq", [128, 6]), gkv=d(p + "gkv", [128, 4]),
            lbr=d(p + "lbr", [8 * 128, DEPTH]), gout=d(p + "gout", [8 * 128, 128]), biasT=d(p + "biasT", [8 * 128, 640]),
            gpc_o=d(p + "gpc_o", [128, 32]), w_out=d(p + "w_out", [D_MODEL, D_MODEL]), gpost_a=d(p + "gpost_a", [128, D_MODEL]),
            gpc_f=d(p + "gpc_f", [128, 32]), wg=d(p + "wg", [D_MODEL, D_FF]), wu=d(p + "wu", [D_MODEL, D_FF]), wd=d(p + "wd", [D_FF, D_MODEL]),
            gpost_f=d(p + "gpost_f", [128, D_MODEL])))
    scr = lambda n, s: nc.dram_tensor(n, s, F32, kind="Internal").ap()
    hT = scr("hT", [NFM, T])
    htm = scr("htm", [T, NTM])
    o = scr("o_scr", [T, D_MODEL])
    yscr = scr("yscr", [T, D_MODEL])
    x1 = scr("x1", [T, D_MODEL])
    xmid = scr("xmid", [T, D_MODEL])
    xout = nc.dram_tensor("xout", [T, D_MODEL], F32, kind="ExternalOutput").ap()
    with ExitStack() as st:
        S = Sched(nc, st)
        xin = x
        for l in range(DEPTH):
            W = L[l]
            phase_A(nc, S, T, xin, W["gpc_pre"], W["wfm"], W["wtm"], ident, hT, htm)
            for hp in range(8):
                phase_B_mla(nc, S, 1, SL, hT[0:768, :], hT[768:1280, :], hT[1280:1408, :], pos64,
                            W["wuq"][hp * 768:(hp + 1) * 768, :], W["wukv"][hp * 512:(hp + 1) * 512, :], W["gq"], W["gkv"], rc, o, hp * 256)
            for h in range(8):
                hs = slice(h * 128, (h + 1) * 128)
                phase_B_hg(nc, S, 1, SL, l, hT[1408 + h * 128:1408 + (h + 1) * 128, :], hT[2432 + h * 128:2432 + (h + 1) * 128, :],
                           htm[:, hs], htm[:, 1024 + h * 128:1024 + (h + 1) * 128], W["lbr"][hs, :], W["gout"][hs, :], tri, ident, o, 2048 + h * 128)
            for h in range(8):
                hs = slice(h * 128, (h + 1) * 128)
                phase_B_ca(nc, S, 1, SL, hT[3456 + h * 128:3456 + (h + 1) * 128, :], hT[4480 + h * 128:4480 + (h + 1) * 128, :],
                           htm[:, 2048 + h * 128:2048 + (h + 1) * 128], W["biasT"][hs, :], o, 3072 + h * 128)
            phase_C(nc, S, T, o, xin, W["gpc_o"], W["w_out"], W["gpost_a"], ident, yscr, x1)
            xnext = xout if l == DEPTH - 1 else xmid
            phase_D(nc, S, T, x1, W["gpc_f"], W["wg"], W["wu"], W["wd"], W["gpost_f"], ident, yscr, xnext)
            xin = xnext
        S.finish()
        S.emit()
        print("fused program: %d scheduled ops" % S.ninst, flush=True)
    return nc


def fused_inputs(b, x, positions, attn_pre_norm, attn_post_norm, w_in, mla_q_norm, mla_kv_norm, w_uq, w_ukv,
                 mla_out_norm, hg_lower_bounds, hg_out_norm, ca_rel_bias, ca_out_norm, w_out, ffn_pre_norm,
                 ffn_post_norm, w_gate, w_up, w_down, cache):
    f32 = lambda a: np.ascontiguousarray(np.asarray(a, dtype=np.float32))
    m = {"x": f32(x[b]), "pos64": np.ascontiguousarray(np.broadcast_to(np.asarray(positions[b], np.int32)[None, :], (64, SEQ))),
         "rc": rope_consts(), "tri": TRI, "ident": IDENT}
    if "w" not in cache:
        w = {}
        ones1024 = np.ones(1024, np.float32)
        for l in range(DEPTH):
            p = "l%d_" % l
            wfm, wtm = split_w_in(f32(w_in[l]))
            wq, wk = [], []
            for hp in range(8):
                a, bb = mla_weights_for_core(f32(w_uq[l]), f32(w_ukv[l]), [2 * hp, 2 * hp + 1])
                wq.append(a)
                wk.append(bb)
            w.update({
                p + "gpc_pre": gain_pc(attn_pre_norm[l]), p + "wfm": wfm, p + "wtm": wtm,
                p + "wuq": np.ascontiguousarray(np.concatenate(wq, 0)), p + "wukv": np.ascontiguousarray(np.concatenate(wk, 0)),
                p + "gq": gain_pc(mla_q_norm[l]), p + "gkv": gain_pc(mla_kv_norm[l]),
                p + "lbr": np.ascontiguousarray(f32(hg_lower_bounds).T),
                p + "gout": np.ascontiguousarray(np.concatenate([bcast128(f32(hg_out_norm[l])[h * 128:(h + 1) * 128]) for h in range(8)], 0)),
                p + "biasT": np.ascontiguousarray(np.concatenate([ca_bias_tile(f32(ca_rel_bias[l])[h]) for h in range(8)], 0)),
                p + "gpc_o": gain_pc(np.concatenate([f32(mla_out_norm[l]), ones1024, f32(ca_out_norm[l])])),
                p + "w_out": f32(w_out[l]), p + "gpost_a": bcast128(f32(attn_post_norm[l])),
                p + "gpc_f": gain_pc(ffn_pre_norm[l]), p + "wg": f32(w_gate[l]), p + "wu": f32(w_up[l]), p + "wd": f32(w_down[l]),
                p + "gpost_f": bcast128(f32(ffn_post_norm[l]))})
        cache["w"] = w
    m.update(cache["w"])
    return m


def kernel_fused(**inputs):
    nc = build_fused()
    cache = {}
    maps = [fused_inputs(c // 4, cache=cache, **inputs) for c in range(NCORES)]
    res = run_bass_kernel_spmd(nc, maps, core_ids=list(range(NCORES)))
    out = np.stack([res.results[0]["xout"], res.results[4]["xout"]], 0).reshape(BATCH, SEQ, D_MODEL)
    return out.astype(np.float32)


def kernel(**inputs):
    return kernel_unfused(**inputs)
```

```python
import math
import numpy as np
import concourse.bass as bass
import concourse.mybir as mybir
from concourse.bass_utils import run_bass_kernel_spmd
from contextlib import ExitStack

F32 = mybir.dt.float32
BF16 = mybir.dt.bfloat16
I32 = mybir.dt.int32
AF = mybir.ActivationFunctionType
ALU = mybir.AluOpType

NCORES = 8
DEBUG = {}
D_MODEL = 4096
BATCH = 2
SEQ = 8192
DEPTH = 2
EPS = 1e-6
TINY = 1e-30
MLA_HEADS = 16
QR = 768
KVR = 512
ROPE = 64
D_FF = 11008
D_IN = 8512
NFM = 768 + 512 + 128 + 1024 + 1024 + 1024 + 1024
NTM = 3072
TWO_PI = 2.0 * math.pi


class Buf:
    __slots__ = ("w", "r", "name")

    def __init__(self, name=""):
        self.w = None
        self.r = []
        self.name = name


class TileH:
    def __init__(self, t, name):
        self.t = t
        self.b = Buf(name)


class Sched:
    ENG = ("pe", "act", "dve", "pool", "sp")

    def __init__(self, nc, stack, n_dma_sems=48):
        self.nc = nc
        self.prog = {e: [] for e in self.ENG}
        self.sem = {e: stack.enter_context(nc.semaphore("s_" + e)) for e in self.ENG}
        self.cnt = {e: 0 for e in self.ENG}
        self.seen = {e: {} for e in self.ENG}
        self.dsem = [stack.enter_context(nc.semaphore("d%d" % i)) for i in range(n_dma_sems)]
        self.dtot = [0] * n_dma_sems
        nsp = (2 * n_dma_sems) // 3
        self.dpool = {"sp": list(range(0, nsp)), "pool": list(range(nsp, n_dma_sems))}
        self.dnext = {"sp": 0, "pool": 0}
        self.ninst = 0

    def _waits(self, e, reads, writes, extra=()):
        d = {}

        def add(ev):
            if ev is None:
                return
            k = ev[0]
            if k not in d or d[k][1] < ev[1]:
                d[k] = ev
        for b in reads:
            add(b.w)
        for b in writes:
            add(b.w)
            for ev in b.r:
                add(ev)
        for ev in extra:
            add(ev)
        waits = []
        for k, ev in d.items():
            if e == "pe" and k is self.sem["pe"]:
                continue
            if self.seen[e].get(k, 0) < ev[1]:
                self.seen[e][k] = ev[1]
                waits.append(ev)
        return waits

    def _commit(self, ev, reads, writes):
        for b in writes:
            b.w = ev
            b.r = []
        for b in reads:
            b.r.append(ev)
            if len(b.r) > 64:
                m = {}
                for x in b.r:
                    if x[0] not in m or m[x[0]][1] < x[1]:
                        m[x[0]] = x
                b.r = list(m.values())

    def op(self, e, fn, reads=(), writes=(), inc=True):
        waits = self._waits(e, reads, writes)
        if inc:
            self.cnt[e] += 1
            ev = (self.sem[e], self.cnt[e])
            self.prog[e].append((waits, fn, self.sem[e], 1))
            self._commit(ev, reads, writes)
        else:
            self.prog[e].append((waits, fn, None, 0))
        self.ninst += 1

    def dma(self, q, out, in_, reads=(), writes=()):
        lst = self.dpool[q]
        k = lst[self.dnext[q] % len(lst)]
        self.dnext[q] += 1
        extra = []
        if self.dtot[k] > 0:
            extra.append((self.dsem[k], self.dtot[k]))
        waits = self._waits(q, reads, writes, extra)
        self.dtot[k] += 16
        ev = (self.dsem[k], self.dtot[k])
        self.prog[q].append((waits, lambda eng, o=out, i=in_: eng.dma_start(out=o, in_=i), self.dsem[k], 16))
        self._commit(ev, reads, writes)
        self.ninst += 1

    def barrier(self):
        evs = [(self.sem[e], self.cnt[e]) for e in self.ENG if self.cnt[e] > 0]
        evs += [(self.dsem[k], t) for k, t in enumerate(self.dtot) if t > 0]
        for e in self.ENG:
            waits = []
            for ev in evs:
                if self.seen[e].get(ev[0], 0) < ev[1]:
                    self.seen[e][ev[0]] = ev[1]
                    waits.append(ev)
            if waits:
                self.prog[e].append((waits, None, None, 0))

    def finish(self):
        self.barrier()

    def emit(self):
        progs = self.prog

        def run(eng, lst):
            for waits, fn, sem, inc in lst:
                for (s, v) in waits:
                    eng.wait_ge(s, v)
                if fn is not None:
                    ins = fn(eng)
                    if sem is not None:
                        ins.then_inc(sem, inc)

        with self.nc.Block() as block:
            @block.sync
            def _(eng):
                run(eng, progs["sp"])

            @block.tensor
            def _(eng):
                run(eng, progs["pe"])

            @block.scalar
            def _(eng):
                run(eng, progs["act"])

            @block.vector
            def _(eng):
                run(eng, progs["dve"])

            @block.gpsimd
            def _(eng):
                run(eng, progs["pool"])


_UID = [0]


class Ctx:
    def __init__(self, nc, st):
        self.nc = nc
        self.st = st

    def sb(self, name, shape, dt):
        _UID[0] += 1
        return TileH(self.st.enter_context(self.nc.sbuf_tensor("%s_%d" % (name, _UID[0]), list(shape), dt)), name)

    def ps(self, name, shape, dt=F32):
        _UID[0] += 1
        return TileH(self.st.enter_context(self.nc.psum_tensor("%s_%d" % (name, _UID[0]), list(shape), dt)), name)


class RR:
    def __init__(self, tiles):
        self.tiles = tiles
        self.i = 0

    def next(self):
        t = self.tiles[self.i % len(self.tiles)]
        self.i += 1
        return t


def rstd(S, out_ap, out_b, in_ap, in_b, inv_n):
    S.op("act", lambda e: e.activation(out_ap, in_ap, AF.Sqrt, bias=EPS, scale=inv_n), reads=[in_b], writes=[out_b])
    S.op("dve", lambda e: e.reciprocal(out_ap, out_ap), reads=[out_b], writes=[out_b])


def evac_dve(S, out_ap, in_ap, reads, writes):
    S.op("dve", lambda e: e.tensor_copy(out_ap, in_ap), reads=reads, writes=writes)


def evac(S, flip, out_ap, in_ap, reads, writes):
    flip[0] ^= 1
    if flip[0]:
        S.op("act", lambda e: e.activation(out_ap, in_ap, AF.Copy), reads=reads, writes=writes)
    else:
        S.op("dve", lambda e: e.tensor_copy(out_ap, in_ap), reads=reads, writes=writes)


class Front:
    def __init__(self, S, cx, D, ident, gpc, ps_t, nbuf=2):
        self.S, self.D = S, D
        self.xs = RR([cx.sb("xs", [128, D], F32) for _ in range(nbuf)])
        self.junk = cx.sb("junk", [128, 1024], BF16)
        self.ssq = cx.sb("ssq", [128, 4], F32)
        self.r = cx.sb("r", [128, 1], F32)
        self.diag = cx.sb("diag", [128, 128], F32)
        self.ident, self.gpc, self.ps_t = ident, gpc, ps_t
        self.flip = [0]

    def run(self, x_rows, s, xnT, groups=None):
        for _ in self.run_gen(x_rows, s, xnT, groups):
            pass

    def run_gen(self, x_rows, s, xnT, groups=None):
        S, D = self.S, self.D
        xs, junk, ssq, r, diag = self.xs.next(), self.junk, self.ssq, self.r, self.diag
        ident, gpc = self.ident, self.gpc
        S.dma("sp", xs.t[:, :], x_rows, writes=[xs.b])
        yield
        if groups is None:
            groups = [(0, D, True)]
        for (lo, hi, do_norm) in groups:
            w = hi - lo
            if do_norm:
                npc = w // 1024
                for q in range(npc):
                    S.op("act", lambda e, a=lo + q * 1024, q=q: e.activation(junk.t[:, :], xs.t[:, a:a + 1024], AF.Square, accum_out=ssq.t[:, q:q + 1]),
                         reads=[xs.b], writes=[junk.b, ssq.b])
                if npc > 1:
                    S.op("dve", lambda e, npc=npc: e.tensor_reduce(ssq.t[:, 0:1], ssq.t[:, 0:npc], mybir.AxisListType.X, ALU.add), reads=[ssq.b], writes=[ssq.b])
                rstd(S, r.t[:, 0:1], r.b, ssq.t[:, 0:1], ssq.b, 1.0 / w)
                S.op("dve", lambda e: e.tensor_scalar(diag.t[:, :], ident.t[:, :], r.t[:, 0:1], None, ALU.mult),
                     reads=[ident.b, r.b], writes=[diag.b])
                dg = diag
            else:
                dg = ident
            for cg in range(lo // 512, hi // 512):
                pt = self.ps_t.next()
                for k in range(4):
                    c = cg * 4 + k
                    S.op("pe", lambda e, c=c, k=k, pt=pt, dg=dg: e.matmul(pt.t[:, k * 128:(k + 1) * 128], xs.t[:, c * 128:(c + 1) * 128], dg.t[:, :], start=True, stop=True),
                         reads=[xs.b, dg.b], writes=[pt.b], inc=(k == 3))
                self.flip[0] ^= 1
                for k in range(4):
                    c = cg * 4 + k
                    if self.flip[0]:
                        S.op("act", lambda e, c=c, k=k, pt=pt: e.activation(xnT.t[:, c, s * 128:(s + 1) * 128], pt.t[:, k * 128:(k + 1) * 128], AF.Copy, scale=gpc.t[:, c:c + 1]),
                             reads=[pt.b, gpc.b], writes=[xnT.b])
                    else:
                        S.op("dve", lambda e, c=c, k=k, pt=pt: e.tensor_scalar(xnT.t[:, c, s * 128:(s + 1) * 128], pt.t[:, k * 128:(k + 1) * 128], gpc.t[:, c:c + 1], None, ALU.mult),
                             reads=[pt.b, gpc.b], writes=[xnT.b])
                yield


def mm_fm(S, wq, nchunk, wch_rr, w_dram, col, xnT, TT, ps_rr, ncols=128):
    wch = wch_rr.next()
    S.dma(wq, wch.t[:, 0:nchunk, 0:ncols], w_dram[:, col:col + ncols].rearrange("(c p) n -> p c n", p=128), writes=[wch.b])
    pt = ps_rr.next()
    for c in range(nchunk):
        S.op("pe", lambda e, c=c, pt=pt, wch=wch: e.matmul(pt.t[0:ncols, 0:TT], wch.t[:, c, 0:ncols], xnT.t[:, c, 0:TT], start=(c == 0), stop=(c == nchunk - 1)),
             reads=[wch.b, xnT.b], writes=[pt.b], inc=(c == nchunk - 1))
    return pt


def phase_A(nc, S, T, x, gbc_d, wfm, wtm, ident_d, hT, htm, nfm=NFM, ntm=NTM, D=D_MODEL):
    TT = 1024 if T % 1024 == 0 else 512
    NH = TT // 512
    NC = D // 128
    with ExitStack() as st:
        cx = Ctx(nc, st)
        ident = cx.sb("ident", [128, 128], F32)
        gbc = cx.sb("gpc", [128, NC], F32)
        S.dma("sp", ident.t[:, :], ident_d, writes=[ident.b])
        S.dma("sp", gbc.t[:, :], gbc_d, writes=[gbc.b])
        ps_t = RR([cx.ps("ps_t", [128, 512]) for _ in range(2)])
        ps_m = RR([cx.ps("ps_m", [128, 512]) for _ in range(6)])
        fr = Front(S, cx, D, ident, gbc, ps_t)
        xnT = cx.sb("xnT", [128, NC, TT], BF16)
        wch = RR([cx.sb("wch", [128, NC, 256], BF16) for _ in range(3)])
        stg = RR([cx.sb("stg", [128, 512], F32) for _ in range(4)])
        flip = [0]
        for tt in range(T // TT):
            for s in range(TT // 128):
                fr.run(x[tt * TT + s * 128: tt * TT + (s + 1) * 128, :], s, xnT)
            for j in range(nfm // 128):
                w = wch.next()
                S.dma("pool", w.t[:, :, 0:128], wfm[:, j * 128:(j + 1) * 128].rearrange("(c p) n -> p c n", p=128), writes=[w.b])
                for hh in range(NH):
                    pt = ps_m.next()
                    for c in range(NC):
                        S.op("pe", lambda e, c=c, pt=pt, w=w, hh=hh: e.matmul(pt.t[:, :], w.t[:, c, 0:128], xnT.t[:, c, hh * 512:(hh + 1) * 512], start=(c == 0), stop=(c == NC - 1)),
                             reads=[w.b, xnT.b], writes=[pt.b], inc=(c == NC - 1))
                    sg = stg.next()
                    evac(S, flip, sg.t[:, :], pt.t[:, :], [pt.b], [sg.b])
                    S.dma("sp", hT[j * 128:(j + 1) * 128, tt * TT + hh * 512:tt * TT + (hh + 1) * 512], sg.t[:, :], reads=[sg.b])
            for j in range(ntm // 256):
                w = wch.next()
                S.dma("pool", w.t[:, :, :], wtm[:, j * 256:(j + 1) * 256].rearrange("(c p) n -> p c n", p=128), writes=[w.b])
                for s in range(TT // 128):
                    pt = ps_m.next()
                    for c in range(NC):
                        S.op("pe", lambda e, c=c, pt=pt, w=w, s=s: e.matmul(pt.t[:, 0:256], xnT.t[:, c, s * 128:(s + 1) * 128], w.t[:, c, :], start=(c == 0), stop=(c == NC - 1)),
                             reads=[w.b, xnT.b], writes=[pt.b], inc=(c == NC - 1))
                    sg = stg.next()
                    evac(S, flip, sg.t[:, 0:256], pt.t[:, 0:256], [pt.b], [sg.b])
                    S.dma("sp", htm[tt * TT + s * 128: tt * TT + (s + 1) * 128, j * 256:(j + 1) * 256], sg.t[:, 0:256], reads=[sg.b])
        S.barrier()


def build_A(T):
    nc = bass.Bass("TRN2", target_bir_lowering=False)
    x = nc.dram_tensor("x", [T, D_MODEL], F32, kind="ExternalInput").ap()
    gbc = nc.dram_tensor("gpc", [128, D_MODEL // 128], F32, kind="ExternalInput").ap()
    wfm = nc.dram_tensor("wfm", [D_MODEL, NFM], F32, kind="ExternalInput").ap()
    wtm = nc.dram_tensor("wtm", [D_MODEL, NTM], F32, kind="ExternalInput").ap()
    ident = nc.dram_tensor("ident", [128, 128], F32, kind="ExternalInput").ap()
    hT = nc.dram_tensor("hT", [NFM, T], F32, kind="ExternalOutput").ap()
    htm = nc.dram_tensor("htm", [T, NTM], F32, kind="ExternalOutput").ap()
    with ExitStack() as st:
        S = Sched(nc, st)
        phase_A(nc, S, T, x, gbc, wfm, wtm, ident, hT, htm)
        S.finish()
        S.emit()
    return nc


ROPE_PERM = np.concatenate([np.arange(32, 64), np.arange(0, 32)])


def split_w_in(w_in_l):
    o = np.cumsum([0, 768, 512, 64, 1024, 1024, 1024, 1024, 1024, 1024, 1024])
    cq, ckv, kr, hq, hf, hi, hg, aq, ak, av = [np.arange(o[i], o[i + 1]) for i in range(10)]
    fm_cols = np.concatenate([cq, ckv, kr, kr[ROPE_PERM], hq, hf, aq, ak])
    tm_cols = np.concatenate([hi, hg, av])
    return np.ascontiguousarray(w_in_l[:, fm_cols]), np.ascontiguousarray(w_in_l[:, tm_cols])


def bcast128(v):
    return np.ascontiguousarray(np.broadcast_to(np.asarray(v, np.float32)[None, :], (128, v.shape[0])))


IDENT = np.eye(128, dtype=np.float32)


C1_2PI = 6.28125
C2_2PI = TWO_PI - 6.28125
MLA_SCALE = 192.0 ** -0.5
CA_SCALE = 128.0 ** -0.5
VW = 132


def phase_B_mla(nc, S, nb, SL, cqT, ckvT, krT, pos64, wuq, wukv, gq_d, gkv_d, rc_d, out, ocol0):
    TT = 512
    NT = SL // TT
    NKB = SL // 128
    with ExitStack() as st:
        cx = Ctx(nc, st)
        ones = cx.sb("ones", [128, 128], BF16)
        S.op("pool", lambda e: e.memset(ones.t[:, :], 1.0), writes=[ones.b])
        gq = cx.sb("gq", [128, 6], F32)
        gkv = cx.sb("gkv", [128, 4], F32)
        rc = cx.sb("rc", [64, 2], F32)
        S.dma("sp", gq.t[:, :], gq_d, writes=[gq.b])
        S.dma("sp", gkv.t[:, :], gkv_d, writes=[gkv.b])
        S.dma("sp", rc.t[:, :], rc_d, writes=[rc.b])
        wq = cx.sb("wq", [128, 6, 512], BF16)
        wkv = cx.sb("wkv", [128, 4, 512], BF16)
        S.dma("pool", wq.t[:, :, :], wuq.rearrange("(c p) n -> p c n", p=128), writes=[wq.b])
        S.dma("pool", wkv.t[:, :, :], wukv.rearrange("(c p) n -> p c n", p=128), writes=[wkv.b])
        KT = [cx.sb("KT%d" % h, [128, SL], BF16) for h in range(2)]
        Kpe = cx.sb("Kpe", [64, SL], BF16)
        V = cx.sb("V", [128, NKB, 2, VW], BF16)
        KTb = [[Buf() for _ in range(NT)] for _ in range(2)]
        Kpeb = [Buf() for _ in range(NT)]
        Vb = [Buf() for _ in range(NT)]
        S.op("pool", lambda e: e.memset(V.t[:, :, :, :], 1.0), writes=Vb)
        cq = cx.sb("cq", [128, 6, TT], F32)
        ckv = cx.sb("ckv", [128, 4, TT], F32)
        sqq = cx.sb("sqq", [128, 6, TT], BF16)
        sqk = cx.sb("sqk", [128, 4, TT], BF16)
        kr = cx.sb("kr", [64, TT], F32)
        krp = cx.sb("krp", [64, TT], F32)
        posi = cx.sb("posi", [64, TT], I32)
        ang = cx.sb("ang", [64, TT], F32)
        ki = cx.sb("ki", [64, TT], I32)
        kf = cx.sb("kf", [64, TT], F32)
        yy = cx.sb("yy", [64, TT], F32)
        ya = cx.sb("ya", [64, TT], F32)
        Ct = cx.sb("Ct", [64, TT], F32)
        St = cx.sb("St", [64, TT], F32)
        t1 = cx.sb("t1", [64, TT], F32)
        t2 = cx.sb("t2", [64, TT], F32)
        cqn = cx.sb("cqn", [128, 6, TT], BF16)
        ckvn = cx.sb("ckvn", [128, 4, TT], BF16)
        rbq = cx.sb("rbq", [128, TT], F32)
        rbk = cx.sb("rbk", [128, TT], F32)
        epsb = cx.sb("epsb", [128, 1], F32)
        S.op("pool", lambda e: e.memset(epsb.t[:, :], EPS), writes=[epsb.b])
        qn = [[cx.sb("qn%d" % h, [128, TT], BF16) for h in range(2)] for _ in range(2)]
        qpe = [[cx.sb("qpe%d" % h, [64, TT], BF16) for h in range(2)] for _ in range(2)]
        pT = RR([cx.sb("pT", [128, TT], BF16) for _ in range(4)])
        osb = RR([cx.sb("osb", [128, 128], F32) for _ in range(2)])
        rec = cx.sb("rec", [128, 1], F32)
        ps_s = RR([cx.ps("ps_s", [128, 512]) for _ in range(2)])
        ps_o = [cx.ps("ps_o", [128, 512]) for _ in range(4)]
        ps_p = RR([cx.ps("ps_p", [128, 512]) for _ in range(2)])
        flip = [0]

        def rope(src_a, src_b, dst_ap, dst_b, src_reads):
            S.op("dve", lambda e: e.tensor_tensor(t1.t[:, :], src_a, Ct.t[:, :], ALU.mult), reads=src_reads + [Ct.b], writes=[t1.b])
            S.op("dve", lambda e: e.tensor_tensor(t2.t[:, :], src_b, St.t[:, :], ALU.mult), reads=src_reads + [St.b], writes=[t2.b])
            S.op("dve", lambda e: e.tensor_tensor(dst_ap, t1.t[:, :], t2.t[:, :], ALU.add), reads=[t1.b, t2.b], writes=[dst_b])

        def prologue(b, i):
            t0 = b * SL + i * TT
            c0 = i * TT
            par = i % 2
            S.dma("sp", cq.t[:, :, :], cqT[:, t0:t0 + TT].rearrange("(c p) t -> p c t", p=128), writes=[cq.b])
            S.dma("sp", ckv.t[:, :, :], ckvT[:, t0:t0 + TT].rearrange("(c p) t -> p c t", p=128), writes=[ckv.b])
            S.dma("sp", kr.t[:, :], krT[0:64, t0:t0 + TT], writes=[kr.b])
            S.dma("sp", krp.t[:, :], krT[64:128, t0:t0 + TT], writes=[krp.b])
            S.dma("sp", posi.t[:, :], pos64[:, t0:t0 + TT], writes=[posi.b])
            yield
            S.op("pool", lambda e: e.tensor_tensor(sqq.t[:, :, :], cq.t[:, :, :], cq.t[:, :, :], ALU.mult), reads=[cq.b], writes=[sqq.b])
            S.op("pool", lambda e: e.tensor_tensor(sqk.t[:, :, :], ckv.t[:, :, :], ckv.t[:, :, :], ALU.mult), reads=[ckv.b], writes=[sqk.b])
            S.op("dve", lambda e: e.tensor_copy(ang.t[:, :], posi.t[:, :]), reads=[posi.b], writes=[ang.b])
            S.op("dve", lambda e: e.tensor_scalar(ang.t[:, :], ang.t[:, :], rc.t[:, 0:1], None, ALU.mult), reads=[ang.b, rc.b], writes=[ang.b])
            S.op("dve", lambda e: e.tensor_scalar(ki.t[:, :], ang.t[:, :], 1.0 / TWO_PI, None, ALU.mult), reads=[ang.b], writes=[ki.b])
            S.op("dve", lambda e: e.tensor_copy(kf.t[:, :], ki.t[:, :]), reads=[ki.b], writes=[kf.b])
            S.op("dve", lambda e: e.scalar_tensor_tensor(yy.t[:, :], kf.t[:, :], -C1_2PI, ang.t[:, :], ALU.mult, ALU.add), reads=[kf.b, ang.b], writes=[yy.b])
            S.op("dve", lambda e: e.scalar_tensor_tensor(yy.t[:, :], kf.t[:, :], -C2_2PI, yy.t[:, :], ALU.mult, ALU.add), reads=[kf.b, yy.b], writes=[yy.b])
            S.op("dve", lambda e: e.tensor_scalar(yy.t[:, :], yy.t[:, :], -math.pi, math.pi, ALU.max, ALU.min), reads=[yy.b], writes=[yy.b])
            S.op("dve", lambda e: e.scalar_tensor_tensor(ya.t[:, :], yy.t[:, :], -1.0, yy.t[:, :], ALU.mult, ALU.max), reads=[yy.b], writes=[ya.b])
            S.op("act", lambda e: e.activation(St.t[:, :], yy.t[:, :], AF.Sin), reads=[yy.b], writes=[St.b])
            S.op("act", lambda e: e.activation(Ct.t[:, :], ya.t[:, :], AF.Sin, bias=math.pi / 2, scale=-1.0), reads=[ya.b], writes=[Ct.b])
            S.op("dve", lambda e: e.tensor_scalar(St.t[:, :], St.t[:, :], rc.t[:, 1:2], None, ALU.mult), reads=[St.b, rc.b], writes=[St.b])
            yield
            psq = ps_p.next()
            for c in range(6):
                S.op("pe", lambda e, c=c: e.matmul(psq.t[:, :], ones.t[:, :], sqq.t[:, c, :], start=(c == 0), stop=(c == 5)),
                     reads=[ones.b, sqq.b], writes=[psq.b], inc=(c == 5))
            psk = ps_p.next()
            for c in range(4):
                S.op("pe", lambda e, c=c: e.matmul(psk.t[:, :], ones.t[:, :], sqk.t[:, c, :], start=(c == 0), stop=(c == 3)),
                     reads=[ones.b, sqk.b], writes=[psk.b], inc=(c == 3))
            yield
            for (rb, pq, inv_n) in ((rbq, psq, 1.0 / 768), (rbk, psk, 1.0 / 512)):
                S.op("act", lambda e, rb=rb, pq=pq, inv_n=inv_n: e.activation(rb.t[:, :], pq.t[:, :], AF.Ln, bias=epsb.t[:, 0:1], scale=inv_n), reads=[pq.b, epsb.b], writes=[rb.b])
                S.op("act", lambda e, rb=rb: e.activation(rb.t[:, :], rb.t[:, :], AF.Exp, scale=-0.5), reads=[rb.b], writes=[rb.b])
            for c in range(4):
                S.op("dve", lambda e, c=c: e.scalar_tensor_tensor(ckvn.t[:, c, :], ckv.t[:, c, :], gkv.t[:, c:c + 1], rbk.t[:, :], ALU.mult, ALU.mult),
                     reads=[ckv.b, gkv.b, rbk.b], writes=[ckvn.b])
            for c in range(6):
                S.op("dve", lambda e, c=c: e.scalar_tensor_tensor(cqn.t[:, c, :], cq.t[:, c, :], gq.t[:, c:c + 1], rbq.t[:, :], ALU.mult, ALU.mult),
                     reads=[cq.b, gq.b, rbq.b], writes=[cqn.b])
            rope(kr.t[:, :], krp.t[:, :], Kpe.t[:, c0:c0 + TT], Kpeb[i], [kr.b, krp.b])
            yield
            for h in range(2):
                ps = ps_p.next()
                for c in range(4):
                    S.op("pe", lambda e, c=c, ps=ps, h=h: e.matmul(ps.t[:, :], wkv.t[:, c, h * 128:(h + 1) * 128], ckvn.t[:, c, :], start=(c == 0), stop=(c == 3)),
                         reads=[wkv.b, ckvn.b], writes=[ps.b], inc=(c == 3))
                evac_dve(S, KT[h].t[:, c0:c0 + TT], ps.t[:, :], [ps.b], [KTb[h][i]])
            for s in range(4):
                ps = ps_p.next()
                for c in range(4):
                    S.op("pe", lambda e, c=c, ps=ps, s=s: e.matmul(ps.t[:, 0:256], ckvn.t[:, c, s * 128:(s + 1) * 128], wkv.t[:, c, 256:512], start=(c == 0), stop=(c == 3)),
                         reads=[wkv.b, ckvn.b], writes=[ps.b], inc=(c == 3))
                evac_dve(S, V.t[:, i * 4 + s, :, 0:128], ps.t[:, 0:256].rearrange("p (h v) -> p h v", h=2), [ps.b], [Vb[i]])
                if s == 1:
                    yield
            yield
            for h in range(2):
                ps = ps_p.next()
                for c in range(6):
                    S.op("pe", lambda e, c=c, ps=ps, h=h: e.matmul(ps.t[:, :], wq.t[:, c, h * 256:h * 256 + 128], cqn.t[:, c, :], start=(c == 0), stop=(c == 5)),
                         reads=[wq.b, cqn.b], writes=[ps.b], inc=(c == 5))
                evac_dve(S, qn[par][h].t[:, :], ps.t[:, :], [ps.b], [qn[par][h].b])
                yield
                psa = ps_p.next()
                for c in range(6):
                    S.op("pe", lambda e, c=c, ps=psa, h=h: e.matmul(ps.t[0:64, :], wq.t[:, c, h * 256 + 128:h * 256 + 192], cqn.t[:, c, :], start=(c == 0), stop=(c == 5)),
                         reads=[wq.b, cqn.b], writes=[psa.b], inc=(c == 5))
                psb = ps_p.next()
                for c in range(6):
                    S.op("pe", lambda e, c=c, ps=psb, h=h: e.matmul(ps.t[0:64, :], wq.t[:, c, h * 256 + 192:h * 256 + 256], cqn.t[:, c, :], start=(c == 0), stop=(c == 5)),
                         reads=[wq.b, cqn.b], writes=[psb.b], inc=(c == 5))
                rope(psa.t[0:64, :], psb.t[0:64, :], qpe[par][h].t[:, :], qpe[par][h].b, [psa.b, psb.b])
                yield

        def attention(b, i, gen):
            t0 = b * SL + i * TT
            par = i % 2
            nkb = 4 * i + 4
            total = 2 * nkb
            NSTAGE = 12
            every = max(1, total // NSTAGE)
            delay = min(total // 4, 12)
            stepno = 0
            if gen is not None:
                next(gen, None)
            for h in range(2):
                def qk(kb, h=h):
                    off = kb - 4 * i
                    qs = max(0, off) * 128
                    N = TT - qs
                    ps = ps_s.next()
                    S.op("pe", lambda e, ps=ps, kb=kb, qs=qs, N=N: e.matmul(ps.t[:, 0:N], KT[h].t[:, kb * 128:(kb + 1) * 128], qn[par][h].t[:, qs:TT], start=True, stop=False),
                         reads=[KTb[h][kb // 4], qn[par][h].b], writes=[ps.b], inc=False)
                    S.op("pe", lambda e, ps=ps, kb=kb, qs=qs, N=N: e.matmul(ps.t[:, 0:N], Kpe.t[:, kb * 128:(kb + 1) * 128], qpe[par][h].t[:, qs:TT], start=False, stop=True),
                         reads=[Kpeb[kb // 4], qpe[par][h].b], writes=[ps.b])
                    return ps, off, qs, N

                def pv(p, off, qs, kb, h=h):
                    js = list(range(max(0, off), 4))
                    for j in js:
                        po = ps_o[j]
                        lastj = (j == js[-1])
                        S.op("pe", lambda e, p=p, po=po, j=j, qs=qs, kb=kb, h=h: e.matmul(po.t[:, 0:129], p.t[:, j * 128 - qs:j * 128 - qs + 128], V.t[:, kb, h, 0:129], start=(kb == 0), stop=(kb == 4 * i + j)),
                             reads=[p.b, Vb[kb // 4]], writes=([ps_o[jj].b for jj in js] if (lastj or j == js[0]) else []), inc=lastj)

                cur = qk(0)
                pend = None
                for kb in range(nkb):
                    nxt = qk(kb + 1) if kb + 1 < nkb else None
                    ps, off, qs, N = cur
                    p = pT.next()
                    S.op("act", lambda e, ps=ps, p=p, N=N: e.activation(p.t[:, 0:N], ps.t[:, 0:N], AF.Exp, scale=MLA_SCALE), reads=[ps.b], writes=[p.b])
                    if off >= 0:
                        S.op("pool", lambda e, p=p: e.memset(p.t[64:128, 0:64], 0.0), writes=[p.b])
                    if pend is not None:
                        pv(*pend)
                    pend = (p, off, qs, kb)
                    cur = nxt
                    stepno += 1
                    if gen is not None and stepno >= delay and stepno % every == 0:
                        next(gen, None)
                pv(*pend)
                for j in range(4):
                    po = ps_o[j]
                    S.op("dve", lambda e, po=po: e.reciprocal(rec.t[:, 0:1], po.t[:, 128:129]), reads=[po.b], writes=[rec.b])
                    ob = osb.next()
                    S.op("dve", lambda e, po=po, ob=ob: e.tensor_scalar(ob.t[:, :], po.t[:, 0:128], rec.t[:, 0:1], None, ALU.mult), reads=[po.b, rec.b], writes=[ob.b])
                    S.dma("sp", out[t0 + j * 128:t0 + (j + 1) * 128, ocol0 + h * 128:ocol0 + (h + 1) * 128], ob.t[:, :], reads=[ob.b])
            if gen is not None:
                for _ in gen:
                    pass

        for b in range(nb):
            for _ in prologue(b, 0):
                pass
            for i in range(NT):
                gen = prologue(b, i + 1) if i + 1 < NT else None
                attention(b, i, gen)
        S.barrier()


def rope_consts():
    inv = np.exp(-math.log(10000.0) * 2.0 * np.arange(32, dtype=np.float32) / 64).astype(np.float32)
    rc = np.zeros((64, 2), np.float32)
    rc[:, 0] = np.concatenate([inv, inv])
    rc[:, 1] = np.concatenate([-np.ones(32, np.float32), np.ones(32, np.float32)])
    return rc


def mla_weights_for_core(w_uq_l, w_ukv_l, heads):
    qcols, kcols, vcols = [], [], []
    for h in heads:
        base = h * 192
        rope = base + 128 + np.arange(64)
        qcols.append(np.concatenate([base + np.arange(128), rope, rope[ROPE_PERM]]))
        kcols.append(h * 256 + np.arange(128))
        vcols.append(h * 256 + 128 + np.arange(128))
    wuq = np.ascontiguousarray(w_uq_l[:, np.concatenate(qcols)])
    wukv = np.ascontiguousarray(w_ukv_l[:, np.concatenate(kcols + vcols)])
    return wuq, wukv


def gain_pc(g):
    return np.ascontiguousarray(np.asarray(g, np.float32).reshape(-1, 128).T)


def phase_B_ca(nc, S, nb, SL, aqT, akT, av, biasT_d, out, ocol):
    NKB = SL // 128
    with ExitStack() as st:
        cx = Ctx(nc, st)
        EB = cx.sb("EB", [128, 640], F32)
        S.dma("sp", EB.t[:, :], biasT_d, writes=[EB.b])
        S.op("act", lambda e: e.activation(EB.t[:, :], EB.t[:, :], AF.Exp), reads=[EB.b], writes=[EB.b])
        S.op("pool", lambda e: e.memset(EB.t[0:64, 576:640], 0.0), writes=[EB.b])
        S.op("pool", lambda e: e.memset(EB.t[64:128, 0:64], 0.0), writes=[EB.b])
        QT = cx.sb("QT", [128, SL], BF16)
        KT = cx.sb("KT", [128, SL], BF16)
        V = cx.sb("V", [128, NKB, VW], BF16)
        e32 = RR([cx.sb("e32", [128, 128], F32) for _ in range(3)])
        pb = RR([cx.sb("pb", [128, 128], BF16) for _ in range(3)])
        osb = RR([cx.sb("osb", [128, 128], F32) for _ in range(2)])
        rec = cx.sb("rec", [128, 1], F32)
        ps_s = RR([cx.ps("ps_s", [128, 512]) for _ in range(5)])
        ps_o = RR([cx.ps("ps_o", [128, 512]) for _ in range(3)])
        for b in range(nb):
            tb = b * SL
            S.op("pool", lambda e: e.memset(V.t[:, :, :], 1.0), writes=[V.b])
            S.dma("pool", QT.t[:, :], aqT[:, tb:tb + SL], writes=[QT.b])
            S.dma("pool", KT.t[:, :], akT[:, tb:tb + SL], writes=[KT.b])
            for g in range(0, NKB, 16):
                n = min(16, NKB - g)
                S.dma("pool", V.t[:, g:g + n, 0:128], av[tb + g * 128:tb + (g + n) * 128, :].rearrange("(n p) d -> p n d", p=128), writes=[V.b])
            steps = [(m, kb) for m in range(NKB) for kb in range(max(0, m - 4), m + 1)]

            def qk(m, kb):
                ps = ps_s.next()
                S.op("pe", lambda e, ps=ps, kb=kb, m=m: e.matmul(ps.t[:, 0:128], KT.t[:, kb * 128:(kb + 1) * 128], QT.t[:, m * 128:(m + 1) * 128], start=True, stop=True),
                     reads=[KT.b, QT.b], writes=[ps.b])
                return ps

            LOOK = 3
            queue = [qk(*steps[k]) for k in range(min(LOOK, len(steps)))]
            acc = None
            for si, (m, kb) in enumerate(steps):
                if si + LOOK < len(steps):
                    queue.append(qk(*steps[si + LOOK]))
                first = max(0, m - 4)
                if kb == first:
                    acc = ps_o.next()
                ps = queue.pop(0)
                ee = e32.next()
                S.op("act", lambda e, ps=ps, ee=ee: e.activation(ee.t[:, :], ps.t[:, 0:128], AF.Exp, scale=CA_SCALE), reads=[ps.b], writes=[ee.b])
                p = pb.next()
                S.op("dve", lambda e, ee=ee, p=p, kb=kb, m=m: e.tensor_tensor(p.t[:, :], ee.t[:, :], EB.t[:, (m - kb) * 128:(m - kb + 1) * 128], ALU.mult),
                     reads=[ee.b, EB.b], writes=[p.b])
                S.op("pe", lambda e, p=p, acc=acc, kb=kb, m=m, first=first: e.matmul(acc.t[:, 0:129], p.t[:, :], V.t[:, kb, 0:129], start=(kb == first), stop=(kb == m)),
                     reads=[p.b, V.b], writes=[acc.b])
                if kb == m:
                    S.op("dve", lambda e, acc=acc: e.reciprocal(rec.t[:, 0:1], acc.t[:, 128:129]), reads=[acc.b], writes=[rec.b])
                    ob = osb.next()
                    S.op("dve", lambda e, acc=acc, ob=ob: e.tensor_scalar(ob.t[:, :], acc.t[:, 0:128], rec.t[:, 0:1], None, ALU.mult), reads=[acc.b, rec.b], writes=[ob.b])
                    S.dma("sp", out[tb + m * 128:tb + (m + 1) * 128, ocol:ocol + 128], ob.t[:, :], reads=[ob.b])
        S.barrier()


HG_BIG = 4.7e18


def phase_B_hg(nc, S, nb, SL, layer, hqT, hfT, hi, hgate, lbraw_d, gout_d, tri_d, ident_d, out, ocol):
    TT = 512
    L = 64
    NCH = TT // L
    NT = SL // TT
    with ExitStack() as st:
        cx = Ctx(nc, st)
        tri = cx.sb("tri", [128, 128], F32)
        gout = cx.sb("gout", [128, 128], F32)
        idf = cx.sb("idf", [128, 128], F32)
        idb = cx.sb("idb", [128, 128], BF16)
        lbr = cx.sb("lbr", [128, DEPTH], F32)
        S.dma("sp", tri.t[:, :], tri_d, writes=[tri.b])
        S.dma("sp", gout.t[:, :], gout_d, writes=[gout.b])
        S.dma("sp", idf.t[:, :], ident_d, writes=[idf.b])
        S.dma("sp", lbr.t[:, :], lbraw_d, writes=[lbr.b])
        S.op("dve", lambda e: e.tensor_copy(idb.t[:, :], idf.t[:, :]), reads=[idf.b], writes=[idb.b])
        sm = cx.sb("sm", [128, 8], F32)
        S.op("dve", lambda e: e.tensor_tensor(sm.t[:, 0:1], lbr.t[:, 1:2], lbr.t[:, 0:1], ALU.subtract), reads=[lbr.b], writes=[sm.b])
        S.op("act", lambda e: e.activation(sm.t[:, 1:2], sm.t[:, 0:1], AF.Sigmoid), reads=[sm.b], writes=[sm.b])
        S.op("act", lambda e: e.activation(sm.t[:, 2:3], sm.t[:, 0:1], AF.Sigmoid, scale=-1.0), reads=[sm.b], writes=[sm.b])
        if layer == 0:
            S.op("dve", lambda e: e.tensor_tensor(sm.t[:, 4:5], sm.t[:, 2:3], sm.t[:, 2:3], ALU.subtract), reads=[sm.b], writes=[sm.b])
        else:
            S.op("dve", lambda e: e.tensor_tensor(sm.t[:, 3:4], sm.t[:, 2:3], sm.t[:, 1:2], ALU.add), reads=[sm.b], writes=[sm.b])
            S.op("dve", lambda e: e.tensor_tensor(sm.t[:, 4:5], sm.t[:, 3:4], sm.t[:, 2:3], ALU.subtract), reads=[sm.b], writes=[sm.b])
        S.op("dve", lambda e: e.tensor_scalar(sm.t[:, 5:6], sm.t[:, 4:5], -1.0, 1.0, ALU.mult, ALU.add), reads=[sm.b], writes=[sm.b])
        lb_ap = sm.t[:, 4:5]
        oml_ap = sm.t[:, 5:6]
        mask = cx.sb("mask", [128, TT], F32)
        S.op("pool", lambda e: e.memset(mask.t[:, :], 1.0), writes=[mask.b])
        for k in range(NCH):
            S.op("pool", lambda e, k=k: e.memset(mask.t[:, k * L:k * L + 1], 0.0), writes=[mask.b])
        class Set:
            pass
        sets = []
        for b in range(nb):
            B = Set()
            for nm in ("hq", "hf", "sg", "sgn", "ff", "bb", "eq", "ek", "sq"):
                setattr(B, nm, cx.sb(nm, [128, TT], F32))
            B.qT = cx.sb("qT", [128, TT], BF16)
            B.kT = cx.sb("kT", [128, TT], BF16)
            B.ktm = cx.sb("ktm", [L, NCH, 128], BF16)
            B.vtm = cx.sb("vtm", [L, NCH, 128], BF16)
            B.gtm = cx.sb("gtm", [L, NCH, 128], F32)
            B.sc = cx.sb("sc", [128, 4 * NCH], F32)
            B.AT = RR([cx.sb("AT", [L, L], BF16) for _ in range(2)])
            B.state = cx.sb("state", [128, 128], F32)
            B.tmp = cx.sb("tmp", [128, 128], F32)
            B.stbf = cx.sb("stbf", [128, 128], BF16)
            B.junk = cx.sb("junk", [L, 128], F32)
            B.ssq = cx.sb("ssq", [L, 1], F32)
            B.rr = cx.sb("rr", [L, 1], F32)
            B.o1 = cx.sb("o1", [L, 128], F32)
            B.osb = RR([cx.sb("osb", [L, 128], F32) for _ in range(2)])
            B.ps_a = cx.ps("ps_a", [128, 512])
            B.ps_o = cx.ps("ps_o", [128, 512])
            B.ps_d = cx.ps("ps_d", [128, 512])
            B.ps_tr = cx.ps("ps_tr", [128, 1024], BF16)
            B.flip = [b]
            sets.append(B)

        def prep(b, i):
            B = sets[b]
            hq, hf, sg, sgn, ff, bb, eq, ek, sq, qT, kT, ktm, vtm, gtm, sc = B.hq, B.hf, B.sg, B.sgn, B.ff, B.bb, B.eq, B.ek, B.sq, B.qT, B.kT, B.ktm, B.vtm, B.gtm, B.sc
            t0 = b * SL + i * TT
            S.dma("sp", hq.t[:, :], hqT[:, t0:t0 + TT], writes=[hq.b])
            S.dma("sp", hf.t[:, :], hfT[:, t0:t0 + TT], writes=[hf.b])
            S.dma("pool", vtm.t[:, :, :], hi[t0:t0 + TT, :].rearrange("(n p) d -> p n d", p=L), writes=[vtm.b])
            S.dma("sp", gtm.t[:, :, :], hgate[t0:t0 + TT, :].rearrange("(n p) d -> p n d", p=L), writes=[gtm.b])
            S.op("act", lambda e: e.activation(sg.t[:, :], hf.t[:, :], AF.Sigmoid), reads=[hf.b], writes=[sg.b])
            S.op("act", lambda e: e.activation(sgn.t[:, :], hf.t[:, :], AF.Sigmoid, scale=-1.0), reads=[hf.b], writes=[sgn.b])
            S.op("act", lambda e: e.activation(sq.t[:, :], hq.t[:, :], AF.Silu), reads=[hq.b], writes=[sq.b])
            S.op("act", lambda e: e.activation(gtm.t[:, :, :], gtm.t[:, :, :], AF.Silu), reads=[gtm.b], writes=[gtm.b])
            S.op("dve", lambda e: e.tensor_scalar(ff.t[:, :], sg.t[:, :], oml_ap, lb_ap, ALU.mult, ALU.add), reads=[sg.b, sm.b], writes=[ff.b])
            S.op("dve", lambda e: e.tensor_scalar_max(ff.t[:, :], ff.t[:, :], TINY), reads=[ff.b], writes=[ff.b])
            S.op("act", lambda e: e.activation(ff.t[:, :], ff.t[:, :], AF.Ln), reads=[ff.b], writes=[ff.b])
            S.op("dve", lambda e: e.tensor_tensor_scan(bb.t[:, :], mask.t[:, :], ff.t[:, :], 0.0, ALU.mult, ALU.add), reads=[mask.b, ff.b], writes=[bb.b])
            S.op("dve", lambda e: e.tensor_scalar(sgn.t[:, :], sgn.t[:, :], oml_ap, None, ALU.mult), reads=[sgn.b, sm.b], writes=[sgn.b])
            for n in range(NCH):
                c0 = n * L
                mid = bb.t[:, c0 + L // 2 - 1:c0 + L // 2]
                last = bb.t[:, c0 + L - 1:c0 + L]
                S.op("dve", lambda e, n=n, mid=mid: e.tensor_scalar(sc.t[:, n:n + 1], mid, -1.0, None, ALU.mult), reads=[bb.b], writes=[sc.b])
                S.op("act", lambda e, n=n, c0=c0: e.activation(eq.t[:, c0:c0 + L], bb.t[:, c0:c0 + L], AF.Exp, bias=sc.t[:, n:n + 1]), reads=[bb.b, sc.b], writes=[eq.b])
                S.op("act", lambda e, n=n, c0=c0, mid=mid: e.activation(ek.t[:, c0:c0 + L], bb.t[:, c0:c0 + L], AF.Exp, bias=mid, scale=-1.0), reads=[bb.b], writes=[ek.b])
                S.op("act", lambda e, n=n, mid=mid: e.activation(sc.t[:, NCH + n:NCH + n + 1], mid, AF.Exp), reads=[bb.b], writes=[sc.b])
                S.op("act", lambda e, n=n, last=last: e.activation(sc.t[:, 2 * NCH + n:2 * NCH + n + 1], last, AF.Exp, bias=sc.t[:, n:n + 1]), reads=[bb.b, sc.b], writes=[sc.b])
                S.op("act", lambda e, n=n, last=last: e.activation(sc.t[:, 3 * NCH + n:3 * NCH + n + 1], last, AF.Exp), reads=[bb.b], writes=[sc.b])
            S.op("dve", lambda e: e.scalar_tensor_tensor(qT.t[:, :], eq.t[:, :], HG_BIG, sq.t[:, :], ALU.min, ALU.mult), reads=[sq.b, eq.b], writes=[qT.b])
            S.op("dve", lambda e: e.scalar_tensor_tensor(kT.t[:, :], ek.t[:, :], HG_BIG, sgn.t[:, :], ALU.min, ALU.mult), reads=[sgn.b, ek.b], writes=[kT.b])
            ps_tr = B.ps_tr
            for n in range(NCH):
                S.op("pe", lambda e, n=n: e.transpose(ps_tr.t[0:L, n * 128:(n + 1) * 128], kT.t[:, n * L:(n + 1) * L], idb.t[:, :]),
                     reads=[kT.b, idb.b], writes=[ps_tr.b], inc=(n == NCH - 1))
            evac(S, B.flip, ktm.t[:, :, :], ps_tr.t[0:L, :].rearrange("p (n k) -> p n k", n=NCH), [ps_tr.b], [ktm.b])

        def chunk(b, i, n):
            B = sets[b]
            qT, kT, ktm, vtm, gtm, sc, state, tmp, stbf, junk, ssq, rr, o1 = B.qT, B.kT, B.ktm, B.vtm, B.gtm, B.sc, B.state, B.tmp, B.stbf, B.junk, B.ssq, B.rr, B.o1
            t0 = b * SL + i * TT
            c0 = n * L
            pa = B.ps_a
            S.op("pe", lambda e, c0=c0: e.matmul(pa.t[0:L, 0:L], kT.t[:, c0:c0 + L], qT.t[:, c0:c0 + L], start=True, stop=True),
                 reads=[kT.b, qT.b], writes=[pa.b])
            at = B.AT.next()
            S.op("dve", lambda e, at=at: e.tensor_tensor(at.t[:, :], pa.t[0:L, 0:L], tri.t[0:L, 0:L], ALU.mult), reads=[pa.b, tri.b], writes=[at.b])
            S.op("dve", lambda e, n=n: e.tensor_scalar(stbf.t[:, :], state.t[:, :], sc.t[:, NCH + n:NCH + n + 1], None, ALU.mult), reads=[state.b, sc.b], writes=[stbf.b])
            po = B.ps_o
            S.op("pe", lambda e, at=at, n=n: e.matmul(po.t[0:L, 0:128], at.t[:, :], vtm.t[:, n, :], start=True, stop=False),
                 reads=[at.b, vtm.b], writes=[po.b], inc=False)
            S.op("pe", lambda e, c0=c0: e.matmul(po.t[0:L, 0:128], qT.t[:, c0:c0 + L], stbf.t[:, :], start=False, stop=True),
                 reads=[qT.b, stbf.b], writes=[po.b])
            ps_d = B.ps_d
            S.op("pe", lambda e, n=n: e.matmul(ps_d.t[:, 0:128], ktm.t[:, n, :], vtm.t[:, n, :], start=True, stop=True),
                 reads=[ktm.b, vtm.b], writes=[ps_d.b])
            S.op("dve", lambda e, n=n: e.tensor_scalar(tmp.t[:, :], state.t[:, :], sc.t[:, 3 * NCH + n:3 * NCH + n + 1], None, ALU.mult), reads=[state.b, sc.b], writes=[tmp.b])
            S.op("dve", lambda e, n=n: e.scalar_tensor_tensor(state.t[:, :], ps_d.t[:, 0:128], sc.t[:, 2 * NCH + n:2 * NCH + n + 1], tmp.t[:, :], ALU.mult, ALU.add),
                 reads=[ps_d.b, sc.b, tmp.b], writes=[state.b])
            S.op("act", lambda e: e.activation(junk.t[:, :], po.t[0:L, 0:128], AF.Square, accum_out=ssq.t[:, 0:1]), reads=[po.b], writes=[junk.b, ssq.b])
            rstd(S, rr.t[:, 0:1], rr.b, ssq.t[:, 0:1], ssq.b, 1.0 / 128)
            S.op("dve", lambda e: e.scalar_tensor_tensor(o1.t[:, :], po.t[0:L, 0:128], rr.t[:, 0:1], gout.t[0:L, :], ALU.mult, ALU.mult),
                 reads=[po.b, rr.b, gout.b], writes=[o1.b])
            ob = B.osb.next()
            S.op("dve", lambda e, ob=ob, n=n: e.tensor_tensor(ob.t[:, :], o1.t[:, :], gtm.t[:, n, :], ALU.mult), reads=[o1.b, gtm.b], writes=[ob.b])
            S.dma("sp", out[t0 + c0:t0 + c0 + L, ocol:ocol + 128], ob.t[:, :], reads=[ob.b])

        for b in range(nb):
            S.op("pool", lambda e, b=b: e.memset(sets[b].state.t[:, :], 0.0), writes=[sets[b].state.b])
        for i in range(NT):
            for b in range(nb):
                prep(b, i)
            for n in range(NCH):
                for b in range(nb):
                    chunk(b, i, n)
        S.barrier()


def ca_bias_tile(rel_bias_h):
    ki = np.arange(128)[:, None]
    qi = np.arange(640)[None, :]
    return np.ascontiguousarray(rel_bias_h[np.clip(qi - ki, -256, 256) + 256].astype(np.float32))


TRI = np.ascontiguousarray((np.arange(128)[:, None] <= np.arange(128)[None, :]).astype(np.float32))


def mm_tm_stream(S, lhsT, nK, w_dram, wch_rr, ps_rr, stg_rr, junk, yscr, row0, ssqp, yb, hook=None):
    G = 8
    ngrp = (nK + G - 1) // G
    for j in range(DEBUG.get("nj", 8)):
        acc = [ps_rr.next() for _ in range(4)]
        for g in range(ngrp):
            k0 = g * G
            kn = min(G, nK - k0)
            w = wch_rr.next()
            wv = w.t[:, :].rearrange("p (c n) -> p c n", n=512)
            S.dma("pool", wv[:, 0:kn, :], w_dram[k0 * 128:(k0 + kn) * 128, j * 512:(j + 1) * 512].rearrange("(c p) n -> p c n", p=128), writes=[w.b])
            for s in range(4):
                for k in range(kn):
                    S.op("pe", lambda e, a=acc[s], s=s, k=k, k0=k0, wv=wv, g=g, kn=kn: e.matmul(a.t[:, :], lhsT.t[:, k0 + k, s * 128:(s + 1) * 128], wv[:, k, :], start=(g == 0 and k == 0), stop=(g == ngrp - 1 and k == kn - 1)),
                         reads=[lhsT.b, w.b], writes=[acc[s].b], inc=(k == kn - 1))
            if hook is not None:
                hook()
        for s in range(4):
            sg = stg_rr.next()
            S.op("dve", lambda e, sg=sg, a=acc[s]: e.tensor_copy(sg.t[:, :], a.t[:, :]), reads=[acc[s].b], writes=[sg.b])
            if not DEBUG.get("nosq"):
                S.op("act", lambda e, sg=sg, s=s, j=j: e.activation(junk.t[:, 0:512], sg.t[:, :], AF.Square, accum_out=ssqp.t[:, s * 8 + j:s * 8 + j + 1]),
                     reads=[sg.b], writes=[junk.b, ssqp.b])
            if not DEBUG.get("nostore"):
                S.dma("sp", yscr[row0 + s * 128:row0 + (s + 1) * 128, j * 512:(j + 1) * 512], sg.t[:, :], reads=[sg.b], writes=[yb[s][j]])


def tail(S, yscr, row0, resid, outp, ssqp, yb, gpost, tl_rr, rt, pe_id):
    for s in range(4):
        r0 = row0 + s * 128
        S.op("dve", lambda e, s=s: e.tensor_reduce(rt.t[:, 0:1], ssqp.t[:, s * 8:(s + 1) * 8], mybir.AxisListType.X, ALU.add), reads=[ssqp.b], writes=[rt.b])
        rstd(S, rt.t[:, 0:1], rt.b, rt.t[:, 0:1], rt.b, 1.0 / D_MODEL)
        for j in range(8):
            ybk = tl_rr.next()
            xbk = tl_rr.next()
            S.dma("sp", ybk.t[:, :], yscr[r0:r0 + 128, j * 512:(j + 1) * 512], reads=[yb[s][j]], writes=[ybk.b])
            S.dma("sp", xbk.t[:, :], resid[r0:r0 + 128, j * 512:(j + 1) * 512], writes=[xbk.b])
            S.op("dve", lambda e, ybk=ybk, j=j: e.scalar_tensor_tensor(ybk.t[:, :], ybk.t[:, :], rt.t[:, 0:1], gpost.t[:, j * 512:(j + 1) * 512], ALU.mult, ALU.mult),
                 reads=[ybk.b, rt.b, gpost.b], writes=[ybk.b])
            S.op("pool", lambda e, ybk=ybk, xbk=xbk: e.tensor_tensor(xbk.t[:, :], ybk.t[:, :], xbk.t[:, :], ALU.add), reads=[ybk.b, xbk.b], writes=[xbk.b])
            S.dma("sp", outp[r0:r0 + 128, j * 512:(j + 1) * 512], xbk.t[:, :], reads=[xbk.b])


def phase_C(nc, S, T, o, x, gpc_d, w_out, gpost_d, ident_d, yscr, x1):
    TT = 512
    NC = D_MODEL // 128
    groups = [(0, 2048, True), (2048, 3072, False), (3072, 4096, True)]
    with ExitStack() as st:
        cx = Ctx(nc, st)
        ident = cx.sb("ident", [128, 128], F32)
        gpc = cx.sb("gpc", [128, NC], F32)
        gpost = cx.sb("gpost", [128, D_MODEL], F32)
        S.dma("sp", ident.t[:, :], ident_d, writes=[ident.b])
        S.dma("sp", gpc.t[:, :], gpc_d, writes=[gpc.b])
        S.dma("sp", gpost.t[:, :], gpost_d, writes=[gpost.b])
        ps_t = RR([cx.ps("ps_t", [128, 512]) for _ in range(2)])
        ps_m = RR([cx.ps("ps_m", [128, 512]) for _ in range(6)])
        fr = Front(S, cx, D_MODEL, ident, gpc, ps_t, nbuf=2)
        oTs = [cx.sb("oT", [128, NC, TT], BF16) for _ in range(2)]
        wch = RR([cx.sb("wch", [128, 4096], BF16) for _ in range(3)])
        stg = RR([cx.sb("stg", [128, 512], F32) for _ in range(2)])
        tl = RR([cx.sb("tl", [128, 512], F32) for _ in range(4)])
        ssqp = cx.sb("ssqp", [128, 32], F32)
        rt = cx.sb("rt", [128, 1], F32)
        NTT = T // TT

        def front_all(tt):
            for s in range(4):
                yield from fr.run_gen(o[tt * TT + s * 128: tt * TT + (s + 1) * 128, :], s, oTs[tt % 2], groups=groups)

        for _ in front_all(0):
            pass
        for tt in range(NTT):
            yb = [[Buf() for _ in range(8)] for _ in range(4)]
            nxt = front_all(tt + 1) if tt + 1 < NTT else None

            def hook(nxt=nxt):
                if nxt is not None:
                    next(nxt, None)
                    next(nxt, None)
            mm_tm_stream(S, oTs[tt % 2], NC, w_out, wch, ps_m, stg, fr.junk, yscr, tt * TT, ssqp, yb, hook=hook)
            if nxt is not None:
                for _ in nxt:
                    pass
            tail(S, yscr, tt * TT, x, x1, ssqp, yb, gpost, tl, rt, None)
        S.barrier()


def phase_D(nc, S, T, x1, gpc_d, wg, wu, wd, gpost_d, ident_d, yscr, x2, dff=D_FF):
    TT = 512
    NC = D_MODEL // 128
    NF = dff // 128
    with ExitStack() as st:
        cx = Ctx(nc, st)
        ident = cx.sb("ident", [128, 128], F32)
        gpc = cx.sb("gpc", [128, NC], F32)
        gpost = cx.sb("gpost", [128, D_MODEL], F32)
        S.dma("sp", ident.t[:, :], ident_d, writes=[ident.b])
        S.dma("sp", gpc.t[:, :], gpc_d, writes=[gpc.b])
        S.dma("sp", gpost.t[:, :], gpost_d, writes=[gpost.b])
        ps_t = RR([cx.ps("ps_t", [128, 512]) for _ in range(2)])
        ps_m = RR([cx.ps("ps_m", [128, 512]) for _ in range(6)])
        fr = Front(S, cx, D_MODEL, ident, gpc, ps_t, nbuf=1)
        hT = cx.sb("hT", [128, NC, TT], BF16)
        actT = cx.sb("actT", [128, NF, TT], BF16)
        wch = RR([cx.sb("wch", [128, 4096], BF16) for _ in range(3)])
        stg = RR([cx.sb("stg", [128, 512], F32) for _ in range(2)])
        sgl = RR([cx.sb("sgl", [128, 512], F32) for _ in range(2)])
        tl = RR([cx.sb("tl", [128, 512], F32) for _ in range(4)])
        ssqp = cx.sb("ssqp", [128, 32], F32)
        rt = cx.sb("rt", [128, 1], F32)
        NTT = T // TT

        def front_all(tt):
            for s in range(4):
                yield from fr.run_gen(x1[tt * TT + s * 128: tt * TT + (s + 1) * 128, :], s, hT)

        for _ in front_all(0):
            pass
        for tt in range(NTT):
            yb = [[Buf() for _ in range(8)] for _ in range(4)]
            for f in range(NF):
                pp = []
                for wmat in (wg, wu):
                    w = wch.next()
                    wv = w.t[:, :].rearrange("p (c n) -> p c n", n=128)
                    S.dma("pool", wv[:, :, :], wmat[:, f * 128:(f + 1) * 128].rearrange("(c p) n -> p c n", p=128), writes=[w.b])
                    pt = ps_m.next()
                    for c in range(NC):
                        S.op("pe", lambda e, c=c, pt=pt, wv=wv: e.matmul(pt.t[:, :], wv[:, c, :], hT.t[:, c, :], start=(c == 0), stop=(c == NC - 1)),
                             reads=[w.b, hT.b], writes=[pt.b], inc=(c == NC - 1))
                    pp.append(pt)
                sg = sgl.next()
                S.op("act", lambda e, sg=sg, pg=pp[0]: e.activation(sg.t[:, :], pg.t[:, :], AF.Silu), reads=[pp[0].b], writes=[sg.b])
                S.op("dve", lambda e, sg=sg, pu=pp[1], f=f: e.tensor_tensor(actT.t[:, f, :], sg.t[:, :], pu.t[:, :], ALU.mult), reads=[sg.b, pp[1].b], writes=[actT.b])
            nxt = front_all(tt + 1) if tt + 1 < NTT else None

            def hook(nxt=nxt):
                if nxt is not None:
                    next(nxt, None)
            mm_tm_stream(S, actT, NF, wd, wch, ps_m, stg, fr.junk, yscr, tt * TT, ssqp, yb, hook=hook)
            if nxt is not None:
                for _ in nxt:
                    pass
            tail(S, yscr, tt * TT, x1, x2, ssqp, yb, gpost, tl, rt, None)
        S.barrier()


def build_B(layer, nb=BATCH, SL=SEQ):
    NTOK = nb * SL
    nc = bass.Bass("TRN2", target_bir_lowering=False)
    d = lambda n, s, t=F32: nc.dram_tensor(n, s, t, kind="ExternalInput").ap()
    lat = d("lat", [1408, NTOK])
    pos64 = d("pos64", [64, NTOK], I32)
    wuq = d("wuq", [768, 512])
    wukv = d("wukv", [512, 512])
    gq = d("gq", [128, 6])
    gkv = d("gkv", [128, 4])
    rc = d("rc", [64, 2])
    hqf = d("hqf", [256, NTOK])
    hig = d("hig", [NTOK, 256])
    lbr = d("lbr", [128, DEPTH])
    gout = d("gout", [128, 128])
    tri = d("tri", [128, 128])
    ident = d("ident", [128, 128])
    aqk = d("aqk", [256, NTOK])
    av = d("av", [NTOK, 128])
    biasT = d("biasT", [128, 640])
    out = nc.dram_tensor("out", [NTOK, 512], F32, kind="ExternalOutput").ap()
    with ExitStack() as st:
        S = Sched(nc, st)
        only = DEBUG.get("only")
        if only in (None, "mla"):
            phase_B_mla(nc, S, nb, SL, lat[0:768, :], lat[768:1280, :], lat[1280:1408, :], pos64, wuq, wukv, gq, gkv, rc, out, 0)
        if only in (None, "hg"):
            phase_B_hg(nc, S, nb, SL, layer, hqf[0:128, :], hqf[128:256, :], hig[:, 0:128], hig[:, 128:256], lbr, gout, tri, ident, out, 256)
        if only in (None, "ca"):
            phase_B_ca(nc, S, nb, SL, aqk[0:128, :], aqk[128:256, :], av, biasT, out, 384)
        S.finish()
        S.emit()
    return nc


def build_CD(T, with_A):
    nc = bass.Bass("TRN2", target_bir_lowering=False)
    d = lambda n, s, t=F32: nc.dram_tensor(n, s, t, kind="ExternalInput").ap()
    o = d("o", [T, D_MODEL])
    x = d("x", [T, D_MODEL])
    gpc_o = d("gpc_o", [128, 32])
    w_out = d("w_out", [D_MODEL, D_MODEL])
    gpost_a = d("gpost_a", [128, D_MODEL])
    gpc_f = d("gpc_f", [128, 32])
    wg = d("wg", [D_MODEL, D_FF])
    wu = d("wu", [D_MODEL, D_FF])
    wd = d("wd", [D_FF, D_MODEL])
    gpost_f = d("gpost_f", [128, D_MODEL])
    ident = d("ident", [128, 128])
    yscr = nc.dram_tensor("yscr", [T, D_MODEL], F32, kind="Internal").ap()
    x1 = nc.dram_tensor("x1", [T, D_MODEL], F32, kind="Internal").ap()
    x2 = nc.dram_tensor("x2", [T, D_MODEL], F32, kind="ExternalOutput").ap()
    if with_A:
        gpc_n = d("gpc_n", [128, 32])
        wfm = d("wfm", [D_MODEL, NFM])
        wtm = d("wtm", [D_MODEL, NTM])
        hT = nc.dram_tensor("hT", [NFM, T], F32, kind="ExternalOutput").ap()
        htm = nc.dram_tensor("htm", [T, NTM], F32, kind="ExternalOutput").ap()
    with ExitStack() as st:
        S = Sched(nc, st)
        phase_C(nc, S, T, o, x, gpc_o, w_out, gpost_a, ident, yscr, x1)
        phase_D(nc, S, T, x1, gpc_f, wg, wu, wd, gpost_f, ident, yscr, x2)
        if with_A:
            phase_A(nc, S, T, x2, gpc_n, wfm, wtm, ident, hT, htm)
        S.finish()
        S.emit()
    return nc


def _run(nc, in_maps):
    res = run_bass_kernel_spmd(nc, in_maps, core_ids=list(range(NCORES)))
    return res.results


def kernel_unfused(x, positions, attn_pre_norm, attn_post_norm, w_in, mla_q_norm, mla_kv_norm, w_uq, w_ukv,
           mla_out_norm, hg_lower_bounds, hg_out_norm, ca_rel_bias, ca_out_norm, w_out, ffn_pre_norm,
           ffn_post_norm, w_gate, w_up, w_down):
    f32 = lambda a: np.ascontiguousarray(np.asarray(a, dtype=np.float32))
    x = f32(x)
    NTOK = BATCH * SEQ
    T = NTOK // NCORES
    xs = x.reshape(NTOK, D_MODEL)
    pos64 = np.ascontiguousarray(np.broadcast_to(np.asarray(positions, np.int32).reshape(1, NTOK), (64, NTOK)))
    rc = rope_consts()
    ones1024 = np.ones(1024, np.float32)
    wsplit = [split_w_in(f32(w_in[l])) for l in range(DEPTH)]
    xcur = [np.ascontiguousarray(xs[c * T:(c + 1) * T]) for c in range(NCORES)]

    ncA = build_A(T)
    resA = _run(ncA, [{"x": xcur[c], "gpc": gain_pc(attn_pre_norm[0]), "wfm": wsplit[0][0], "wtm": wsplit[0][1], "ident": IDENT}
                      for c in range(NCORES)])
    hT_parts = [r["hT"] for r in resA]
    htm_parts = [r["htm"] for r in resA]
    for l in range(DEPTH):
        hT_all = np.concatenate(hT_parts, axis=1)
        htm_all = np.concatenate(htm_parts, axis=0)
        del hT_parts, htm_parts
        lat = np.ascontiguousarray(hT_all[0:1408])
        ncB = build_B(l)
        in_maps = []
        for c in range(NCORES):
            wuq_c, wukv_c = mla_weights_for_core(f32(w_uq[l]), f32(w_ukv[l]), [2 * c, 2 * c + 1])
            hs = slice(c * 128, (c + 1) * 128)
            in_maps.append({
                "lat": lat, "pos64": pos64, "wuq": wuq_c, "wukv": wukv_c,
                "gq": gain_pc(mla_q_norm[l]), "gkv": gain_pc(mla_kv_norm[l]), "rc": rc,
                "hqf": np.ascontiguousarray(np.concatenate([hT_all[1408 + c * 128:1408 + (c + 1) * 128], hT_all[2432 + c * 128:2432 + (c + 1) * 128]], 0)),
                "hig": np.ascontiguousarray(np.concatenate([htm_all[:, hs], htm_all[:, 1024 + c * 128:1024 + (c + 1) * 128]], 1)),
                "lbr": np.ascontiguousarray(f32(hg_lower_bounds)[:, hs].T),
                "gout": bcast128(f32(hg_out_norm[l])[hs]), "tri": TRI, "ident": IDENT,
                "aqk": np.ascontiguousarray(np.concatenate([hT_all[3456 + c * 128:3456 + (c + 1) * 128], hT_all[4480 + c * 128:4480 + (c + 1) * 128]], 0)),
                "av": np.ascontiguousarray(htm_all[:, 2048 + c * 128:2048 + (c + 1) * 128]),
                "biasT": ca_bias_tile(f32(ca_rel_bias[l])[c]),
            })
        del hT_all, htm_all
        resB = _run(ncB, in_maps)
        del in_maps, lat
        o_all = np.empty((NTOK, D_MODEL), np.float32)
        for c in range(NCORES):
            oc = resB[c]["out"]
            o_all[:, 2 * c * 128:(2 * c + 2) * 128] = oc[:, 0:256]
            o_all[:, 2048 + c * 128:2048 + (c + 1) * 128] = oc[:, 256:384]
            o_all[:, 3072 + c * 128:3072 + (c + 1) * 128] = oc[:, 384:512]
        del resB
        last = (l == DEPTH - 1)
        ncCD = build_CD(T, with_A=not last)
        gpc_o = gain_pc(np.concatenate([f32(mla_out_norm[l]), ones1024, f32(ca_out_norm[l])]))
        base = {"gpc_o": gpc_o, "w_out": f32(w_out[l]), "gpost_a": bcast128(f32(attn_post_norm[l])),
                "gpc_f": gain_pc(ffn_pre_norm[l]), "wg": f32(w_gate[l]), "wu": f32(w_up[l]), "wd": f32(w_down[l]),
                "gpost_f": bcast128(f32(ffn_post_norm[l])), "ident": IDENT}
        if not last:
            base.update({"gpc_n": gain_pc(attn_pre_norm[l + 1]), "wfm": wsplit[l + 1][0], "wtm": wsplit[l + 1][1]})
        in_maps = []
        for c in range(NCORES):
            m = dict(base)
            m["o"] = np.ascontiguousarray(o_all[c * T:(c + 1) * T])
            m["x"] = xcur[c]
            in_maps.append(m)
        del o_all
        resC = _run(ncCD, in_maps)
        del in_maps
        xcur = [r["x2"] for r in resC]
        if not last:
            hT_parts = [r["hT"] for r in resC]
            htm_parts = [r["htm"] for r in resC]
        del resC
    out = np.concatenate(xcur, axis=0).reshape(BATCH, SEQ, D_MODEL).astype(np.float32)
    return out


def build_fused(SL=SEQ):
    T = SL
    nc = bass.Bass("TRN2", target_bir_lowering=False)
    d = lambda n, s, t=F32: nc.dram_tensor(n, s, t, kind="ExternalInput").ap()
    x = d("x", [T, D_MODEL])
    pos64 = d("pos64", [64, T], I32)
    rc = d("rc", [64, 2])
    tri = d("tri", [128, 128])
    ident = d("ident", [128, 128])
    L = []
    for l in range(DEPTH):
        p = "l%d_" % l
        L.append(dict(
            gpc_pre=d(p + "gpc_pre", [128, 32]), wfm=d(p + "wfm", [D_MODEL, NFM]), wtm=d(p + "wtm", [D_MODEL, NTM]),
            wuq=d(p + "wuq", [8 * 768, 512]), wukv=d(p + "wukv", [8 * 512, 512]), gq=d(p + "gq", [128, 6]), gkv=d(p + "gkv", [128, 4]),
            lbr=d(p + "lbr", [8 * 128, DEPTH]), gout=d(p + "gout", [8 * 128, 128]), biasT=d(p + "biasT", [8 * 128, 640]),
            gpc_o=d(p + "gpc_o", [128, 32]), w_out=d(p + "w_out", [D_MODEL, D_MODEL]), gpost_a=d(p + "gpost_a", [128, D_MODEL]),
            gpc_f=d(p + "gpc_f", [128, 32]), wg=d(p + "wg", [D_MODEL, D_FF]), wu=d(p + "wu", [D_MODEL, D_FF]), wd=d(p + "wd", [D_FF, D_MODEL]),
            gpost_f=d(p + "gpost_f", [128, D_MODEL])))
    scr = lambda n, s: nc.dram_tensor(n, s, F32, kind="Internal").ap()
    hT = scr("hT", [NFM, T])
    htm = scr("htm", [T, NTM])
    o = scr("o_scr", [T, D_MODEL])
    yscr = scr("yscr", [T, D_MODEL])
    x1 = scr("x1", [T, D_MODEL])
    xmid = scr("xmid", [T, D_MODEL])
    xout = nc.dram_tensor("xout", [T, D_MODEL], F32, kind="ExternalOutput").ap()
    with ExitStack() as st:
        S = Sched(nc, st)
        xin = x
        for l in range(DEPTH):
            W = L[l]
            phase_A(nc, S, T, xin, W["gpc_pre"], W["wfm"], W["wtm"], ident, hT, htm)
            for hp in range(8):
                phase_B_mla(nc, S, 1, SL, hT[0:768, :], hT[768:1280, :], hT[1280:1408, :], pos64,
                            W["wuq"][hp * 768:(hp + 1) * 768, :], W["wukv"][hp * 512:(hp + 1) * 512, :], W["gq"], W["gkv"], rc, o, hp * 256)
            for h in range(8):
                hs = slice(h * 128, (h + 1) * 128)
                phase_B_hg(nc, S, 1, SL, l, hT[1408 + h * 128:1408 + (h + 1) * 128, :], hT[2432 + h * 128:2432 + (h + 1) * 128, :],
                           htm[:, hs], htm[:, 1024 + h * 128:1024 + (h + 1) * 128], W["lbr"][hs, :], W["gout"][hs, :], tri, ident, o, 2048 + h * 128)
            for h in range(8):
                hs = slice(h * 128, (h + 1) * 128)
                phase_B_ca(nc, S, 1, SL, hT[3456 + h * 128:3456 + (h + 1) * 128, :], hT[4480 + h * 128:4480 + (h + 1) * 128, :],
                           htm[:, 2048 + h * 128:2048 + (h + 1) * 128], W["biasT"][hs, :], o, 3072 + h * 128)
            phase_C(nc, S, T, o, xin, W["gpc_o"], W["w_out"], W["gpost_a"], ident, yscr, x1)
            xnext = xout if l == DEPTH - 1 else xmid
            phase_D(nc, S, T, x1, W["gpc_f"], W["wg"], W["wu"], W["wd"], W["gpost_f"], ident, yscr, xnext)
            xin = xnext
        S.finish()
        S.emit()
        print("fused program: %d scheduled ops" % S.ninst, flush=True)
    return nc


def fused_inputs(b, x, positions, attn_pre_norm, attn_post_norm, w_in, mla_q_norm, mla_kv_norm, w_uq, w_ukv,
                 mla_out_norm, hg_lower_bounds, hg_out_norm, ca_rel_bias, ca_out_norm, w_out, ffn_pre_norm,
                 ffn_post_norm, w_gate, w_up, w_down, cache):
    f32 = lambda a: np.ascontiguousarray(np.asarray(a, dtype=np.float32))
    m = {"x": f32(x[b]), "pos64": np.ascontiguousarray(np.broadcast_to(np.asarray(positions[b], np.int32)[None, :], (64, SEQ))),
         "rc": rope_consts(), "tri": TRI, "ident": IDENT}
    if "w" not in cache:
        w = {}
        ones1024 = np.ones(1024, np.float32)
        for l in range(DEPTH):
            p = "l%d_" % l
            wfm, wtm = split_w_in(f32(w_in[l]))
            wq, wk = [], []
            for hp in range(8):
                a, bb = mla_weights_for_core(f32(w_uq[l]), f32(w_ukv[l]), [2 * hp, 2 * hp + 1])
                wq.append(a)
                wk.append(bb)
            w.update({
                p + "gpc_pre": gain_pc(attn_pre_norm[l]), p + "wfm": wfm, p + "wtm": wtm,
                p + "wuq": np.ascontiguousarray(np.concatenate(wq, 0)), p + "wukv": np.ascontiguousarray(np.concatenate(wk, 0)),
                p + "gq": gain_pc(mla_q_norm[l]), p + "gkv": gain_pc(mla_kv_norm[l]),
                p + "lbr": np.ascontiguousarray(f32(hg_lower_bounds).T),
                p + "gout": np.ascontiguousarray(np.concatenate([bcast128(f32(hg_out_norm[l])[h * 128:(h + 1) * 128]) for h in range(8)], 0)),
                p + "biasT": np.ascontiguousarray(np.concatenate([ca_bias_tile(f32(ca_rel_bias[l])[h]) for h in range(8)], 0)),
                p + "gpc_o": gain_pc(np.concatenate([f32(mla_out_norm[l]), ones1024, f32(ca_out_norm[l])])),
                p + "w_out": f32(w_out[l]), p + "gpost_a": bcast128(f32(attn_post_norm[l])),
                p + "gpc_f": gain_pc(ffn_pre_norm[l]), p + "wg": f32(w_gate[l]), p + "wu": f32(w_up[l]), p + "wd": f32(w_down[l]),
                p + "gpost_f": bcast128(f32(ffn_post_norm[l]))})
        cache["w"] = w
    m.update(cache["w"])
    return m


def kernel_fused(**inputs):
    nc = build_fused()
    cache = {}
    maps = [fused_inputs(c // 4, cache=cache, **inputs) for c in range(NCORES)]
    res = run_bass_kernel_spmd(nc, maps, core_ids=list(range(NCORES)))
    out = np.stack([res.results[0]["xout"], res.results[4]["xout"]], 0).reshape(BATCH, SEQ, D_MODEL)
    return out.astype(np.float32)


def kernel(**inputs):
    return kernel_unfused(**inputs)
```

```python
import math
import numpy as np
import concourse.bass as bass
import concourse.mybir as mybir
from concourse.bass_utils import run_bass_kernel_spmd
from contextlib import ExitStack

F32 = mybir.dt.float32
BF16 = mybir.dt.bfloat16
I32 = mybir.dt.int32
AF = mybir.ActivationFunctionType
ALU = mybir.AluOpType

NCORES = 8
DEBUG = {}
D_MODEL = 4096
BATCH = 2
SEQ = 8192
DEPTH = 2
EPS = 1e-6
TINY = 1e-30
MLA_HEADS = 16
QR = 768
KVR = 512
ROPE = 64
D_FF = 11008
D_IN = 8512
NFM = 768 + 512 + 128 + 1024 + 1024 + 1024 + 1024
NTM = 3072
TWO_PI = 2.0 * math.pi


class Buf:
    __slots__ = ("w", "r", "name")

    def __init__(self, name=""):
        self.w = None
        self.r = []
        self.name = name


class TileH:
    def __init__(self, t, name):
        self.t = t
        self.b = Buf(name)


class Sched:
    ENG = ("pe", "act", "dve", "pool", "sp")

    def __init__(self, nc, stack, n_dma_sems=48):
        self.nc = nc
        self.prog = {e: [] for e in self.ENG}
        self.sem = {e: stack.enter_context(nc.semaphore("s_" + e)) for e in self.ENG}
        self.cnt = {e: 0 for e in self.ENG}
        self.seen = {e: {} for e in self.ENG}
        self.pend = {}
        self.dsem = [stack.enter_context(nc.semaphore("d%d" % i)) for i in range(n_dma_sems)]
        self.dtot = [0] * n_dma_sems
        nsp = (2 * n_dma_sems) // 3
        self.dpool = {"sp": list(range(0, nsp)), "pool": list(range(nsp, n_dma_sems))}
        self.dnext = {"sp": 0, "pool": 0}
        self.ninst = 0

    def _waits(self, e, reads, writes, extra=()):
        d = {}

        def add(ev):
            if ev is None:
                return
            k = ev[0]
            if k not in d or d[k][1] < ev[1]:
                d[k] = ev
        for b in reads:
            add(b.w)
        for b in writes:
            add(b.w)
            for ev in b.r:
                add(ev)
        for ev in extra:
            add(ev)
        waits = []
        for k, ev in d.items():
            if e == "pe" and k is self.sem["pe"]:
                continue
            if self.seen[e].get(k, 0) < ev[1]:
                self.seen[e][k] = ev[1]
                waits.append(ev)
        return waits

    def _commit(self, ev, reads, writes):
        for b in writes:
            b.w = ev
            b.r = []
        for b in reads:
            b.r.append(ev)
            if len(b.r) > 64:
                m = {}
                for x in b.r:
                    if x[0] not in m or m[x[0]][1] < x[1]:
                        m[x[0]] = x
                b.r = list(m.values())

    def op(self, e, fn, reads=(), writes=(), inc=True):
        waits = self._waits(e, reads, writes)
        if inc:
            self.cnt[e] += 1
            ev = (self.sem[e], self.cnt[e])
            self.prog[e].append((waits, fn, self.sem[e], 1))
            pr, pw = self.pend.get(e, ([], []))
            self._commit(ev, list(reads) + pr, list(writes) + pw)
            self.pend[e] = ([], [])
        else:
            self.prog[e].append((waits, fn, None, 0))
            pr, pw = self.pend.setdefault(e, ([], []))
            pr.extend(reads)
            pw.extend(writes)
        self.ninst += 1

    def dma(self, q, out, in_, reads=(), writes=()):
        lst = self.dpool[q]
        k = lst[self.dnext[q] % len(lst)]
        self.dnext[q] += 1
        extra = []
        if self.dtot[k] > 0:
            extra.append((self.dsem[k], self.dtot[k]))
        waits = self._waits(q, reads, writes, extra)
        self.dtot[k] += 16
        ev = (self.dsem[k], self.dtot[k])
        self.prog[q].append((waits, lambda eng, o=out, i=in_: eng.dma_start(out=o, in_=i), self.dsem[k], 16))
        self._commit(ev, reads, writes)
        self.ninst += 1

    def barrier(self):
        evs = [(self.sem[e], self.cnt[e]) for e in self.ENG if self.cnt[e] > 0]
        evs += [(self.dsem[k], t) for k, t in enumerate(self.dtot) if t > 0]
        for e in self.ENG:
            waits = []
            for ev in evs:
                if self.seen[e].get(ev[0], 0) < ev[1]:
                    self.seen[e][ev[0]] = ev[1]
                    waits.append(ev)
            if waits:
                self.prog[e].append((waits, None, None, 0))

    def finish(self):
        self.barrier()

    def emit(self):
        progs = self.prog

        def run(eng, lst):
            for waits, fn, sem, inc in lst:
                for (s, v) in waits:
                    eng.wait_ge(s, v)
                if fn is not None:
                    ins = fn(eng)
                    if sem is not None:
                        ins.then_inc(sem, inc)

        with self.nc.Block() as block:
            @block.sync
            def _(eng):
                run(eng, progs["sp"])

            @block.tensor
            def _(eng):
                run(eng, progs["pe"])

            @block.scalar
            def _(eng):
                run(eng, progs["act"])

            @block.vector
            def _(eng):
                run(eng, progs["dve"])

            @block.gpsimd
            def _(eng):
                run(eng, progs["pool"])


_UID = [0]


class Ctx:
    def __init__(self, nc, st):
        self.nc = nc
        self.st = st

    def sb(self, name, shape, dt):
        _UID[0] += 1
        return TileH(self.st.enter_context(self.nc.sbuf_tensor("%s_%d" % (name, _UID[0]), list(shape), dt)), name)

    def ps(self, name, shape, dt=F32):
        _UID[0] += 1
        return TileH(self.st.enter_context(self.nc.psum_tensor("%s_%d" % (name, _UID[0]), list(shape), dt)), name)


class RR:
    def __init__(self, tiles):
        self.tiles = tiles
        self.i = 0

    def next(self):
        t = self.tiles[self.i % len(self.tiles)]
        self.i += 1
        return t


def rstd(S, out_ap, out_b, in_ap, in_b, inv_n):
    S.op("act", lambda e: e.activation(out_ap, in_ap, AF.Sqrt, bias=EPS, scale=inv_n), reads=[in_b], writes=[out_b])
    S.op("dve", lambda e: e.reciprocal(out_ap, out_ap), reads=[out_b], writes=[out_b])


def evac_dve(S, out_ap, in_ap, reads, writes):
    S.op("dve", lambda e: e.tensor_copy(out_ap, in_ap), reads=reads, writes=writes)


def evac(S, flip, out_ap, in_ap, reads, writes):
    flip[0] ^= 1
    if flip[0]:
        S.op("act", lambda e: e.activation(out_ap, in_ap, AF.Copy), reads=reads, writes=writes)
    else:
        S.op("dve", lambda e: e.tensor_copy(out_ap, in_ap), reads=reads, writes=writes)


class Front:
    def __init__(self, S, cx, D, ident, gpc, ps_t, nbuf=2):
        self.S, self.D = S, D
        self.xs = RR([cx.sb("xs", [128, D], F32) for _ in range(nbuf)])
        self.junk = cx.sb("junk", [128, 1024], BF16)
        self.ssq = cx.sb("ssq", [128, 4], F32)
        self.r = cx.sb("r", [128, 1], F32)
        self.diag = cx.sb("diag", [128, 128], F32)
        self.ident, self.gpc, self.ps_t = ident, gpc, ps_t
        self.flip = [0]

    def run(self, x_rows, s, xnT, groups=None):
        for _ in self.run_gen(x_rows, s, xnT, groups):
            pass

    def run_gen(self, x_rows, s, xnT, groups=None):
        S, D = self.S, self.D
        xs, junk, ssq, r, diag = self.xs.next(), self.junk, self.ssq, self.r, self.diag
        ident, gpc = self.ident, self.gpc
        S.dma("sp", xs.t[:, :], x_rows, writes=[xs.b])
        yield
        if groups is None:
            groups = [(0, D, True)]
        for (lo, hi, do_norm) in groups:
            w = hi - lo
            if do_norm:
                npc = w // 1024
                for q in range(npc):
                    S.op("act", lambda e, a=lo + q * 1024, q=q: e.activation(junk.t[:, :], xs.t[:, a:a + 1024], AF.Square, accum_out=ssq.t[:, q:q + 1]),
                         reads=[xs.b], writes=[junk.b, ssq.b])
                if npc > 1:
                    S.op("dve", lambda e, npc=npc: e.tensor_reduce(ssq.t[:, 0:1], ssq.t[:, 0:npc], mybir.AxisListType.X, ALU.add), reads=[ssq.b], writes=[ssq.b])
                rstd(S, r.t[:, 0:1], r.b, ssq.t[:, 0:1], ssq.b, 1.0 / w)
                S.op("dve", lambda e: e.tensor_scalar(diag.t[:, :], ident.t[:, :], r.t[:, 0:1], None, ALU.mult),
                     reads=[ident.b, r.b], writes=[diag.b])
                dg = diag
            else:
                dg = ident
            for cg in range(lo // 512, hi // 512):
                pt = self.ps_t.next()
                for k in range(4):
                    c = cg * 4 + k
                    S.op("pe", lambda e, c=c, k=k, pt=pt, dg=dg: e.matmul(pt.t[:, k * 128:(k + 1) * 128], xs.t[:, c * 128:(c + 1) * 128], dg.t[:, :], start=True, stop=True),
                         reads=[xs.b, dg.b], writes=[pt.b], inc=(k == 3))
                self.flip[0] ^= 1
                for k in range(4):
                    c = cg * 4 + k
                    if self.flip[0]:
                        S.op("act", lambda e, c=c, k=k, pt=pt: e.activation(xnT.t[:, c, s * 128:(s + 1) * 128], pt.t[:, k * 128:(k + 1) * 128], AF.Copy, scale=gpc.t[:, c:c + 1]),
                             reads=[pt.b, gpc.b], writes=[xnT.b])
                    else:
                        S.op("dve", lambda e, c=c, k=k, pt=pt: e.tensor_scalar(xnT.t[:, c, s * 128:(s + 1) * 128], pt.t[:, k * 128:(k + 1) * 128], gpc.t[:, c:c + 1], None, ALU.mult),
                             reads=[pt.b, gpc.b], writes=[xnT.b])
                yield


def mm_fm(S, wq, nchunk, wch_rr, w_dram, col, xnT, TT, ps_rr, ncols=128):
    wch = wch_rr.next()
    S.dma(wq, wch.t[:, 0:nchunk, 0:ncols], w_dram[:, col:col + ncols].rearrange("(c p) n -> p c n", p=128), writes=[wch.b])
    pt = ps_rr.next()
    for c in range(nchunk):
        S.op("pe", lambda e, c=c, pt=pt, wch=wch: e.matmul(pt.t[0:ncols, 0:TT], wch.t[:, c, 0:ncols], xnT.t[:, c, 0:TT], start=(c == 0), stop=(c == nchunk - 1)),
             reads=[wch.b, xnT.b], writes=[pt.b], inc=(c == nchunk - 1))
    return pt


def phase_A(nc, S, T, x, gbc_d, wfm, wtm, ident_d, hT, htm, nfm=NFM, ntm=NTM, D=D_MODEL):
    TT = 1024 if T % 1024 == 0 else 512
    NH = TT // 512
    NC = D // 128
    with ExitStack() as st:
        cx = Ctx(nc, st)
        ident = cx.sb("ident", [128, 128], F32)
        gbc = cx.sb("gpc", [128, NC], F32)
        S.dma("sp", ident.t[:, :], ident_d, writes=[ident.b])
        S.dma("sp", gbc.t[:, :], gbc_d, writes=[gbc.b])
        ps_t = RR([cx.ps("ps_t", [128, 512]) for _ in range(2)])
        ps_m = RR([cx.ps("ps_m", [128, 512]) for _ in range(6)])
        fr = Front(S, cx, D, ident, gbc, ps_t)
        xnT = cx.sb("xnT", [128, NC, TT], BF16)
        wch = RR([cx.sb("wch", [128, NC, 256], BF16) for _ in range(3)])
        stg = RR([cx.sb("stg", [128, 512], F32) for _ in range(4)])
        flip = [0]
        for tt in range(T // TT):
            for s in range(TT // 128):
                fr.run(x[tt * TT + s * 128: tt * TT + (s + 1) * 128, :], s, xnT)
            for j in range(nfm // 128):
                w = wch.next()
                S.dma("pool", w.t[:, :, 0:128], wfm[:, j * 128:(j + 1) * 128].rearrange("(c p) n -> p c n", p=128), writes=[w.b])
                for hh in range(NH):
                    pt = ps_m.next()
                    for c in range(NC):
                        S.op("pe", lambda e, c=c, pt=pt, w=w, hh=hh: e.matmul(pt.t[:, :], w.t[:, c, 0:128], xnT.t[:, c, hh * 512:(hh + 1) * 512], start=(c == 0), stop=(c == NC - 1)),
                             reads=[w.b, xnT.b], writes=[pt.b], inc=(c == NC - 1))
                    sg = stg.next()
                    evac(S, flip, sg.t[:, :], pt.t[:, :], [pt.b], [sg.b])
                    S.dma("sp", hT[j * 128:(j + 1) * 128, tt * TT + hh * 512:tt * TT + (hh + 1) * 512], sg.t[:, :], reads=[sg.b])
            for j in range(ntm // 256):
                w = wch.next()
                S.dma("pool", w.t[:, :, :], wtm[:, j * 256:(j + 1) * 256].rearrange("(c p) n -> p c n", p=128), writes=[w.b])
                for s in range(TT // 128):
                    pt = ps_m.next()
                    for c in range(NC):
                        S.op("pe", lambda e, c=c, pt=pt, w=w, s=s: e.matmul(pt.t[:, 0:256], xnT.t[:, c, s * 128:(s + 1) * 128], w.t[:, c, :], start=(c == 0), stop=(c == NC - 1)),
                             reads=[w.b, xnT.b], writes=[pt.b], inc=(c == NC - 1))
                    sg = stg.next()
                    evac(S, flip, sg.t[:, 0:256], pt.t[:, 0:256], [pt.b], [sg.b])
                    S.dma("sp", htm[tt * TT + s * 128: tt * TT + (s + 1) * 128, j * 256:(j + 1) * 256], sg.t[:, 0:256], reads=[sg.b])
        S.barrier()


def build_A(T):
    nc = bass.Bass("TRN2", target_bir_lowering=False)
    x = nc.dram_tensor("x", [T, D_MODEL], F32, kind="ExternalInput").ap()
    gbc = nc.dram_tensor("gpc", [128, D_MODEL // 128], F32, kind="ExternalInput").ap()
    wfm = nc.dram_tensor("wfm", [D_MODEL, NFM], F32, kind="ExternalInput").ap()
    wtm = nc.dram_tensor("wtm", [D_MODEL, NTM], F32, kind="ExternalInput").ap()
    ident = nc.dram_tensor("ident", [128, 128], F32, kind="ExternalInput").ap()
    hT = nc.dram_tensor("hT", [NFM, T], F32, kind="ExternalOutput").ap()
    htm = nc.dram_tensor("htm", [T, NTM], F32, kind="ExternalOutput").ap()
    with ExitStack() as st:
        S = Sched(nc, st)
        phase_A(nc, S, T, x, gbc, wfm, wtm, ident, hT, htm)
        S.finish()
        S.emit()
    return nc


ROPE_PERM = np.concatenate([np.arange(32, 64), np.arange(0, 32)])


def split_w_in(w_in_l):
    o = np.cumsum([0, 768, 512, 64, 1024, 1024, 1024, 1024, 1024, 1024, 1024])
    cq, ckv, kr, hq, hf, hi, hg, aq, ak, av = [np.arange(o[i], o[i + 1]) for i in range(10)]
    fm_cols = np.concatenate([cq, ckv, kr, kr[ROPE_PERM], hq, hf, aq, ak])
    tm_cols = np.concatenate([hi, hg, av])
    return np.ascontiguousarray(w_in_l[:, fm_cols]), np.ascontiguousarray(w_in_l[:, tm_cols])


def bcast128(v):
    return np.ascontiguousarray(np.broadcast_to(np.asarray(v, np.float32)[None, :], (128, v.shape[0])))


IDENT = np.eye(128, dtype=np.float32)


C1_2PI = 6.28125
C2_2PI = TWO_PI - 6.28125
MLA_SCALE = 192.0 ** -0.5
CA_SCALE = 128.0 ** -0.5
VW = 132


def phase_B_mla(nc, S, nb, SL, cqT, ckvT, krT, pos64, wuq, wukv, gq_d, gkv_d, rc_d, out, ocol0):
    TT = 512
    NT = SL // TT
    NKB = SL // 128
    with ExitStack() as st:
        cx = Ctx(nc, st)
        ones = cx.sb("ones", [128, 128], BF16)
        S.op("pool", lambda e: e.memset(ones.t[:, :], 1.0), writes=[ones.b])
        gq = cx.sb("gq", [128, 6], F32)
        gkv = cx.sb("gkv", [128, 4], F32)
        rc = cx.sb("rc", [64, 2], F32)
        S.dma("sp", gq.t[:, :], gq_d, writes=[gq.b])
        S.dma("sp", gkv.t[:, :], gkv_d, writes=[gkv.b])
        S.dma("sp", rc.t[:, :], rc_d, writes=[rc.b])
        wq = cx.sb("wq", [128, 6, 512], BF16)
        wkv = cx.sb("wkv", [128, 4, 512], BF16)
        S.dma("pool", wq.t[:, :, :], wuq.rearrange("(c p) n -> p c n", p=128), writes=[wq.b])
        S.dma("pool", wkv.t[:, :, :], wukv.rearrange("(c p) n -> p c n", p=128), writes=[wkv.b])
        KT = [cx.sb("KT%d" % h, [128, SL], BF16) for h in range(2)]
        Kpe = cx.sb("Kpe", [64, SL], BF16)
        V = cx.sb("V", [128, NKB, 2, VW], BF16)
        KTb = [[Buf() for _ in range(NT)] for _ in range(2)]
        Kpeb = [Buf() for _ in range(NT)]
        Vb = [Buf() for _ in range(NT)]
        S.op("pool", lambda e: e.memset(V.t[:, :, :, :], 1.0), writes=Vb)
        cq = cx.sb("cq", [128, 6, TT], F32)
        ckv = cx.sb("ckv", [128, 4, TT], F32)
        sqq = cx.sb("sqq", [128, 6, TT], BF16)
        sqk = cx.sb("sqk", [128, 4, TT], BF16)
        kr = cx.sb("kr", [64, TT], F32)
        krp = cx.sb("krp", [64, TT], F32)
        posi = cx.sb("posi", [64, TT], I32)
        ang = cx.sb("ang", [64, TT], F32)
        ki = cx.sb("ki", [64, TT], I32)
        kf = cx.sb("kf", [64, TT], F32)
        yy = cx.sb("yy", [64, TT], F32)
        ya = cx.sb("ya", [64, TT], F32)
        Ct = cx.sb("Ct", [64, TT], F32)
        St = cx.sb("St", [64, TT], F32)
        t1 = cx.sb("t1", [64, TT], F32)
        t2 = cx.sb("t2", [64, TT], F32)
        cqn = cx.sb("cqn", [128, 6, TT], BF16)
        ckvn = cx.sb("ckvn", [128, 4, TT], BF16)
        rbq = cx.sb("rbq", [128, TT], F32)
        rbk = cx.sb("rbk", [128, TT], F32)
        epsb = cx.sb("epsb", [128, 1], F32)
        S.op("pool", lambda e: e.memset(epsb.t[:, :], EPS), writes=[epsb.b])
        qn = [[cx.sb("qn%d" % h, [128, TT], BF16) for h in range(2)] for _ in range(2)]
        qpe = [[cx.sb("qpe%d" % h, [64, TT], BF16) for h in range(2)] for _ in range(2)]
        pT = RR([cx.sb("pT", [128, TT], BF16) for _ in range(4)])
        osb = RR([cx.sb("osb", [128, 128], F32) for _ in range(2)])
        rec = cx.sb("rec", [128, 1], F32)
        ps_s = RR([cx.ps("ps_s", [128, 512]) for _ in range(2)])
        ps_o = [cx.ps("ps_o", [128, 512]) for _ in range(4)]
        ps_p = RR([cx.ps("ps_p", [128, 512]) for _ in range(2)])
        flip = [0]

        def rope(src_a, src_b, dst_ap, dst_b, src_reads):
            S.op("dve", lambda e: e.tensor_tensor(t1.t[:, :], src_a, Ct.t[:, :], ALU.mult), reads=src_reads + [Ct.b], writes=[t1.b])
            S.op("dve", lambda e: e.tensor_tensor(t2.t[:, :], src_b, St.t[:, :], ALU.mult), reads=src_reads + [St.b], writes=[t2.b])
            S.op("dve", lambda e: e.tensor_tensor(dst_ap, t1.t[:, :], t2.t[:, :], ALU.add), reads=[t1.b, t2.b], writes=[dst_b])

        def prologue(b, i):
            t0 = b * SL + i * TT
            c0 = i * TT
            par = i % 2
            S.dma("sp", cq.t[:, :, :], cqT[:, t0:t0 + TT].rearrange("(c p) t -> p c t", p=128), writes=[cq.b])
            S.dma("sp", ckv.t[:, :, :], ckvT[:, t0:t0 + TT].rearrange("(c p) t -> p c t", p=128), writes=[ckv.b])
            S.dma("sp", kr.t[:, :], krT[0:64, t0:t0 + TT], writes=[kr.b])
            S.dma("sp", krp.t[:, :], krT[64:128, t0:t0 + TT], writes=[krp.b])
            S.dma("sp", posi.t[:, :], pos64[:, t0:t0 + TT], writes=[posi.b])
            yield
            S.op("pool", lambda e: e.tensor_tensor(sqq.t[:, :, :], cq.t[:, :, :], cq.t[:, :, :], ALU.mult), reads=[cq.b], writes=[sqq.b])
            S.op("pool", lambda e: e.tensor_tensor(sqk.t[:, :, :], ckv.t[:, :, :], ckv.t[:, :, :], ALU.mult), reads=[ckv.b], writes=[sqk.b])
            S.op("dve", lambda e: e.tensor_copy(ang.t[:, :], posi.t[:, :]), reads=[posi.b], writes=[ang.b])
            S.op("dve", lambda e: e.tensor_scalar(ang.t[:, :], ang.t[:, :], rc.t[:, 0:1], None, ALU.mult), reads=[ang.b, rc.b], writes=[ang.b])
            S.op("dve", lambda e: e.tensor_scalar(ki.t[:, :], ang.t[:, :], 1.0 / TWO_PI, None, ALU.mult), reads=[ang.b], writes=[ki.b])
            S.op("dve", lambda e: e.tensor_copy(kf.t[:, :], ki.t[:, :]), reads=[ki.b], writes=[kf.b])
            S.op("dve", lambda e: e.scalar_tensor_tensor(yy.t[:, :], kf.t[:, :], -C1_2PI, ang.t[:, :], ALU.mult, ALU.add), reads=[kf.b, ang.b], writes=[yy.b])
            S.op("dve", lambda e: e.scalar_tensor_tensor(yy.t[:, :], kf.t[:, :], -C2_2PI, yy.t[:, :], ALU.mult, ALU.add), reads=[kf.b, yy.b], writes=[yy.b])
            S.op("dve", lambda e: e.tensor_scalar(yy.t[:, :], yy.t[:, :], -math.pi, math.pi, ALU.max, ALU.min), reads=[yy.b], writes=[yy.b])
            S.op("dve", lambda e: e.scalar_tensor_tensor(ya.t[:, :], yy.t[:, :], -1.0, yy.t[:, :], ALU.mult, ALU.max), reads=[yy.b], writes=[ya.b])
            S.op("act", lambda e: e.activation(St.t[:, :], yy.t[:, :], AF.Sin), reads=[yy.b], writes=[St.b])
            S.op("act", lambda e: e.activation(Ct.t[:, :], ya.t[:, :], AF.Sin, bias=math.pi / 2, scale=-1.0), reads=[ya.b], writes=[Ct.b])
            S.op("dve", lambda e: e.tensor_scalar(St.t[:, :], St.t[:, :], rc.t[:, 1:2], None, ALU.mult), reads=[St.b, rc.b], writes=[St.b])
            yield
            psq = ps_p.next()
            for c in range(6):
                S.op("pe", lambda e, c=c: e.matmul(psq.t[:, :], ones.t[:, :], sqq.t[:, c, :], start=(c == 0), stop=(c == 5)),
                     reads=[ones.b, sqq.b], writes=[psq.b], inc=(c == 5))
            psk = ps_p.next()
            for c in range(4):
                S.op("pe", lambda e, c=c: e.matmul(psk.t[:, :], ones.t[:, :], sqk.t[:, c, :], start=(c == 0), stop=(c == 3)),
                     reads=[ones.b, sqk.b], writes=[psk.b], inc=(c == 3))
            yield
            for (rb, pq, inv_n) in ((rbq, psq, 1.0 / 768), (rbk, psk, 1.0 / 512)):
                S.op("act", lambda e, rb=rb, pq=pq, inv_n=inv_n: e.activation(rb.t[:, :], pq.t[:, :], AF.Ln, bias=epsb.t[:, 0:1], scale=inv_n), reads=[pq.b, epsb.b], writes=[rb.b])
                S.op("act", lambda e, rb=rb: e.activation(rb.t[:, :], rb.t[:, :], AF.Exp, scale=-0.5), reads=[rb.b], writes=[rb.b])
            for c in range(4):
                S.op("dve", lambda e, c=c: e.scalar_tensor_tensor(ckvn.t[:, c, :], ckv.t[:, c, :], gkv.t[:, c:c + 1], rbk.t[:, :], ALU.mult, ALU.mult),
                     reads=[ckv.b, gkv.b, rbk.b], writes=[ckvn.b])
            for c in range(6):
                S.op("dve", lambda e, c=c: e.scalar_tensor_tensor(cqn.t[:, c, :], cq.t[:, c, :], gq.t[:, c:c + 1], rbq.t[:, :], ALU.mult, ALU.mult),
                     reads=[cq.b, gq.b, rbq.b], writes=[cqn.b])
            rope(kr.t[:, :], krp.t[:, :], Kpe.t[:, c0:c0 + TT], Kpeb[i], [kr.b, krp.b])
            yield
            for h in range(2):
                ps = ps_p.next()
                for c in range(4):
                    S.op("pe", lambda e, c=c, ps=ps, h=h: e.matmul(ps.t[:, :], wkv.t[:, c, h * 128:(h + 1) * 128], ckvn.t[:, c, :], start=(c == 0), stop=(c == 3)),
                         reads=[wkv.b, ckvn.b], writes=[ps.b], inc=(c == 3))
                evac_dve(S, KT[h].t[:, c0:c0 + TT], ps.t[:, :], [ps.b], [KTb[h][i]])
            for s in range(4):
                ps = ps_p.next()
                for c in range(4):
                    S.op("pe", lambda e, c=c, ps=ps, s=s: e.matmul(ps.t[:, 0:256], ckvn.t[:, c, s * 128:(s + 1) * 128], wkv.t[:, c, 256:512], start=(c == 0), stop=(c == 3)),
                         reads=[wkv.b, ckvn.b], writes=[ps.b], inc=(c == 3))
                evac_dve(S, V.t[:, i * 4 + s, :, 0:128], ps.t[:, 0:256].rearrange("p (h v) -> p h v", h=2), [ps.b], [Vb[i]])
                if s == 1:
                    yield
            yield
            for h in range(2):
                ps = ps_p.next()
                for c in range(6):
                    S.op("pe", lambda e, c=c, ps=ps, h=h: e.matmul(ps.t[:, :], wq.t[:, c, h * 256:h * 256 + 128], cqn.t[:, c, :], start=(c == 0), stop=(c == 5)),
                         reads=[wq.b, cqn.b], writes=[ps.b], inc=(c == 5))
                evac_dve(S, qn[par][h].t[:, :], ps.t[:, :], [ps.b], [qn[par][h].b])
                yield
                psa = ps_p.next()
                for c in range(6):
                    S.op("pe", lambda e, c=c, ps=psa, h=h: e.matmul(ps.t[0:64, :], wq.t[:, c, h * 256 + 128:h * 256 + 192], cqn.t[:, c, :], start=(c == 0), stop=(c == 5)),
                         reads=[wq.b, cqn.b], writes=[psa.b], inc=(c == 5))
                psb = ps_p.next()
                for c in range(6):
                    S.op("pe", lambda e, c=c, ps=psb, h=h: e.matmul(ps.t[0:64, :], wq.t[:, c, h * 256 + 192:h * 256 + 256], cqn.t[:, c, :], start=(c == 0), stop=(c == 5)),
                         reads=[wq.b, cqn.b], writes=[psb.b], inc=(c == 5))
                rope(psa.t[0:64, :], psb.t[0:64, :], qpe[par][h].t[:, :], qpe[par][h].b, [psa.b, psb.b])
                yield

        def attention(b, i, gen):
            t0 = b * SL + i * TT
            par = i % 2
            nkb = 4 * i + 4
            total = 2 * nkb
            NSTAGE = 12
            every = max(1, total // NSTAGE)
            delay = min(total // 4, 12)
            stepno = 0
            if gen is not None:
                next(gen, None)
            for h in range(2):
                def qk(kb, h=h):
                    off = kb - 4 * i
                    qs = max(0, off) * 128
                    N = TT - qs
                    ps = ps_s.next()
                    S.op("pe", lambda e, ps=ps, kb=kb, qs=qs, N=N: e.matmul(ps.t[:, 0:N], KT[h].t[:, kb * 128:(kb + 1) * 128], qn[par][h].t[:, qs:TT], start=True, stop=False),
                         reads=[KTb[h][kb // 4], qn[par][h].b], writes=[ps.b], inc=False)
                    S.op("pe", lambda e, ps=ps, kb=kb, qs=qs, N=N: e.matmul(ps.t[:, 0:N], Kpe.t[:, kb * 128:(kb + 1) * 128], qpe[par][h].t[:, qs:TT], start=False, stop=True),
                         reads=[Kpeb[kb // 4], qpe[par][h].b], writes=[ps.b])
                    return ps, off, qs, N

                def pv(p, off, qs, kb, h=h):
                    js = list(range(max(0, off), 4))
                    for j in js:
                        po = ps_o[j]
                        lastj = (j == js[-1])
                        S.op("pe", lambda e, p=p, po=po, j=j, qs=qs, kb=kb, h=h: e.matmul(po.t[:, 0:129], p.t[:, j * 128 - qs:j * 128 - qs + 128], V.t[:, kb, h, 0:129], start=(kb == 0), stop=(kb == 4 * i + j)),
                             reads=[p.b, Vb[kb // 4]], writes=([ps_o[jj].b for jj in js] if (lastj or j == js[0]) else []), inc=lastj)

                cur = qk(0)
                pend = None
                for kb in range(nkb):
                    nxt = qk(kb + 1) if kb + 1 < nkb else None
                    ps, off, qs, N = cur
                    p = pT.next()
                    S.op("act", lambda e, ps=ps, p=p, N=N: e.activation(p.t[:, 0:N], ps.t[:, 0:N], AF.Exp, scale=MLA_SCALE), reads=[ps.b], writes=[p.b])
                    if off >= 0:
                        S.op("pool", lambda e, p=p: e.memset(p.t[64:128, 0:64], 0.0), writes=[p.b])
                    if pend is not None:
                        pv(*pend)
                    pend = (p, off, qs, kb)
                    cur = nxt
                    stepno += 1
                    if gen is not None and stepno >= delay and stepno % every == 0:
                        next(gen, None)
                pv(*pend)
                for j in range(4):
                    po = ps_o[j]
                    S.op("dve", lambda e, po=po: e.reciprocal(rec.t[:, 0:1], po.t[:, 128:129]), reads=[po.b], writes=[rec.b])
                    ob = osb.next()
                    S.op("dve", lambda e, po=po, ob=ob: e.tensor_scalar(ob.t[:, :], po.t[:, 0:128], rec.t[:, 0:1], None, ALU.mult), reads=[po.b, rec.b], writes=[ob.b])
                    S.dma("sp", out[t0 + j * 128:t0 + (j + 1) * 128, ocol0 + h * 128:ocol0 + (h + 1) * 128], ob.t[:, :], reads=[ob.b])
            if gen is not None:
                for _ in gen:
                    pass

        for b in range(nb):
            for _ in prologue(b, 0):
                pass
            for i in range(NT):
                gen = prologue(b, i + 1) if i + 1 < NT else None
                attention(b, i, gen)
        S.barrier()


def rope_consts():
    inv = np.exp(-math.log(10000.0) * 2.0 * np.arange(32, dtype=np.float32) / 64).astype(np.float32)
    rc = np.zeros((64, 2), np.float32)
    rc[:, 0] = np.concatenate([inv, inv])
    rc[:, 1] = np.concatenate([-np.ones(32, np.float32), np.ones(32, np.float32)])
    return rc


def mla_weights_for_core(w_uq_l, w_ukv_l, heads):
    qcols, kcols, vcols = [], [], []
    for h in heads:
        base = h * 192
        rope = base + 128 + np.arange(64)
        qcols.append(np.concatenate([base + np.arange(128), rope, rope[ROPE_PERM]]))
        kcols.append(h * 256 + np.arange(128))
        vcols.append(h * 256 + 128 + np.arange(128))
    wuq = np.ascontiguousarray(w_uq_l[:, np.concatenate(qcols)])
    wukv = np.ascontiguousarray(w_ukv_l[:, np.concatenate(kcols + vcols)])
    return wuq, wukv


def gain_pc(g):
    return np.ascontiguousarray(np.asarray(g, np.float32).reshape(-1, 128).T)


def phase_B_ca(nc, S, nb, SL, aqT, akT, av, biasT_d, out, ocol):
    NKB = SL // 128
    with ExitStack() as st:
        cx = Ctx(nc, st)
        EB = cx.sb("EB", [128, 640], F32)
        S.dma("sp", EB.t[:, :], biasT_d, writes=[EB.b])
        S.op("act", lambda e: e.activation(EB.t[:, :], EB.t[:, :], AF.Exp), reads=[EB.b], writes=[EB.b])
        S.op("pool", lambda e: e.memset(EB.t[0:64, 576:640], 0.0), writes=[EB.b])
        S.op("pool", lambda e: e.memset(EB.t[64:128, 0:64], 0.0), writes=[EB.b])
        QT = cx.sb("QT", [128, SL], BF16)
        KT = cx.sb("KT", [128, SL], BF16)
        V = cx.sb("V", [128, NKB, VW], BF16)
        e32 = RR([cx.sb("e32", [128, 128], F32) for _ in range(3)])
        pb = RR([cx.sb("pb", [128, 128], BF16) for _ in range(3)])
        osb = RR([cx.sb("osb", [128, 128], F32) for _ in range(2)])
        rec = cx.sb("rec", [128, 1], F32)
        ps_s = RR([cx.ps("ps_s", [128, 512]) for _ in range(5)])
        ps_o = RR([cx.ps("ps_o", [128, 512]) for _ in range(3)])
        for b in range(nb):
            tb = b * SL
            S.op("pool", lambda e: e.memset(V.t[:, :, :], 1.0), writes=[V.b])
            S.dma("pool", QT.t[:, :], aqT[:, tb:tb + SL], writes=[QT.b])
            S.dma("pool", KT.t[:, :], akT[:, tb:tb + SL], writes=[KT.b])
            for g in range(0, NKB, 16):
                n = min(16, NKB - g)
                S.dma("pool", V.t[:, g:g + n, 0:128], av[tb + g * 128:tb + (g + n) * 128, :].rearrange("(n p) d -> p n d", p=128), writes=[V.b])
            steps = [(m, kb) for m in range(NKB) for kb in range(max(0, m - 4), m + 1)]

            def qk(m, kb):
                ps = ps_s.next()
                S.op("pe", lambda e, ps=ps, kb=kb, m=m: e.matmul(ps.t[:, 0:128], KT.t[:, kb * 128:(kb + 1) * 128], QT.t[:, m * 128:(m + 1) * 128], start=True, stop=True),
                     reads=[KT.b, QT.b], writes=[ps.b])
                return ps

            LOOK = 3
            queue = [qk(*steps[k]) for k in range(min(LOOK, len(steps)))]
            acc = None
            for si, (m, kb) in enumerate(steps):
                if si + LOOK < len(steps):
                    queue.append(qk(*steps[si + LOOK]))
                first = max(0, m - 4)
                if kb == first:
                    acc = ps_o.next()
                ps = queue.pop(0)
                ee = e32.next()
                S.op("act", lambda e, ps=ps, ee=ee: e.activation(ee.t[:, :], ps.t[:, 0:128], AF.Exp, scale=CA_SCALE), reads=[ps.b], writes=[ee.b])
                p = pb.next()
                S.op("dve", lambda e, ee=ee, p=p, kb=kb, m=m: e.tensor_tensor(p.t[:, :], ee.t[:, :], EB.t[:, (m - kb) * 128:(m - kb + 1) * 128], ALU.mult),
                     reads=[ee.b, EB.b], writes=[p.b])
                S.op("pe", lambda e, p=p, acc=acc, kb=kb, m=m, first=first: e.matmul(acc.t[:, 0:129], p.t[:, :], V.t[:, kb, 0:129], start=(kb == first), stop=(kb == m)),
                     reads=[p.b, V.b], writes=[acc.b])
                if kb == m:
                    S.op("dve", lambda e, acc=acc: e.reciprocal(rec.t[:, 0:1], acc.t[:, 128:129]), reads=[acc.b], writes=[rec.b])
                    ob = osb.next()
                    S.op("dve", lambda e, acc=acc, ob=ob: e.tensor_scalar(ob.t[:, :], acc.t[:, 0:128], rec.t[:, 0:1], None, ALU.mult), reads=[acc.b, rec.b], writes=[ob.b])
                    S.dma("sp", out[tb + m * 128:tb + (m + 1) * 128, ocol:ocol + 128], ob.t[:, :], reads=[ob.b])
        S.barrier()


HG_BIG = 4.7e18


def phase_B_hg(nc, S, nb, SL, layer, hqT, hfT, hi, hgate, lbraw_d, gout_d, tri_d, ident_d, out, ocol):
    TT = 512
    L = 64
    NCH = TT // L
    NT = SL // TT
    with ExitStack() as st:
        cx = Ctx(nc, st)
        tri = cx.sb("tri", [128, 128], F32)
        gout = cx.sb("gout", [128, 128], F32)
        idf = cx.sb("idf", [128, 128], F32)
        idb = cx.sb("idb", [128, 128], BF16)
        lbr = cx.sb("lbr", [128, DEPTH], F32)
        S.dma("sp", tri.t[:, :], tri_d, writes=[tri.b])
        S.dma("sp", gout.t[:, :], gout_d, writes=[gout.b])
        S.dma("sp", idf.t[:, :], ident_d, writes=[idf.b])
        S.dma("sp", lbr.t[:, :], lbraw_d, writes=[lbr.b])
        S.op("dve", lambda e: e.tensor_copy(idb.t[:, :], idf.t[:, :]), reads=[idf.b], writes=[idb.b])
        sm = cx.sb("sm", [128, 8], F32)
        S.op("dve", lambda e: e.tensor_tensor(sm.t[:, 0:1], lbr.t[:, 1:2], lbr.t[:, 0:1], ALU.subtract), reads=[lbr.b], writes=[sm.b])
        S.op("act", lambda e: e.activation(sm.t[:, 1:2], sm.t[:, 0:1], AF.Sigmoid), reads=[sm.b], writes=[sm.b])
        S.op("act", lambda e: e.activation(sm.t[:, 2:3], sm.t[:, 0:1], AF.Sigmoid, scale=-1.0), reads=[sm.b], writes=[sm.b])
        if layer == 0:
            S.op("dve", lambda e: e.tensor_tensor(sm.t[:, 4:5], sm.t[:, 2:3], sm.t[:, 2:3], ALU.subtract), reads=[sm.b], writes=[sm.b])
        else:
            S.op("dve", lambda e: e.tensor_tensor(sm.t[:, 3:4], sm.t[:, 2:3], sm.t[:, 1:2], ALU.add), reads=[sm.b], writes=[sm.b])
            S.op("dve", lambda e: e.tensor_tensor(sm.t[:, 4:5], sm.t[:, 3:4], sm.t[:, 2:3], ALU.subtract), reads=[sm.b], writes=[sm.b])
        S.op("dve", lambda e: e.tensor_scalar(sm.t[:, 5:6], sm.t[:, 4:5], -1.0, 1.0, ALU.mult, ALU.add), reads=[sm.b], writes=[sm.b])
        lb_ap = sm.t[:, 4:5]
        oml_ap = sm.t[:, 5:6]
        mask = cx.sb("mask", [128, TT], F32)
        S.op("pool", lambda e: e.memset(mask.t[:, :], 1.0), writes=[mask.b])
        for k in range(NCH):
            S.op("pool", lambda e, k=k: e.memset(mask.t[:, k * L:k * L + 1], 0.0), writes=[mask.b])
        class Set:
            pass
        sets = []
        for b in range(nb):
            B = Set()
            for nm in ("hq", "hf", "sg", "sgn", "ff", "bb", "eq", "ek", "sq"):
                setattr(B, nm, cx.sb(nm, [128, TT], F32))
            B.qT = cx.sb("qT", [128, TT], BF16)
            B.kT = cx.sb("kT", [128, TT], BF16)
            B.ktm = cx.sb("ktm", [L, NCH, 128], BF16)
            B.vtm = cx.sb("vtm", [L, NCH, 128], BF16)
            B.gtm = cx.sb("gtm", [L, NCH, 128], F32)
            B.sc = cx.sb("sc", [128, 4 * NCH], F32)
            B.AT = RR([cx.sb("AT", [L, L], BF16) for _ in range(2)])
            B.state = cx.sb("state", [128, 128], F32)
            B.tmp = cx.sb("tmp", [128, 128], F32)
            B.stbf = cx.sb("stbf", [128, 128], BF16)
            B.junk = cx.sb("junk", [L, 128], F32)
            B.ssq = cx.sb("ssq", [L, 1], F32)
            B.rr = cx.sb("rr", [L, 1], F32)
            B.o1 = cx.sb("o1", [L, 128], F32)
            B.osb = RR([cx.sb("osb", [L, 128], F32) for _ in range(2)])
            B.ps_a = cx.ps("ps_a", [128, 512])
            B.ps_o = cx.ps("ps_o", [128, 512])
            B.ps_d = cx.ps("ps_d", [128, 512])
            B.ps_tr = cx.ps("ps_tr", [128, 1024], BF16)
            B.flip = [b]
            sets.append(B)

        def prep(b, i):
            B = sets[b]
            hq, hf, sg, sgn, ff, bb, eq, ek, sq, qT, kT, ktm, vtm, gtm, sc = B.hq, B.hf, B.sg, B.sgn, B.ff, B.bb, B.eq, B.ek, B.sq, B.qT, B.kT, B.ktm, B.vtm, B.gtm, B.sc
            t0 = b * SL + i * TT
            S.dma("sp", hq.t[:, :], hqT[:, t0:t0 + TT], writes=[hq.b])
            S.dma("sp", hf.t[:, :], hfT[:, t0:t0 + TT], writes=[hf.b])
            S.dma("pool", vtm.t[:, :, :], hi[t0:t0 + TT, :].rearrange("(n p) d -> p n d", p=L), writes=[vtm.b])
            S.dma("sp", gtm.t[:, :, :], hgate[t0:t0 + TT, :].rearrange("(n p) d -> p n d", p=L), writes=[gtm.b])
            S.op("act", lambda e: e.activation(sg.t[:, :], hf.t[:, :], AF.Sigmoid), reads=[hf.b], writes=[sg.b])
            S.op("act", lambda e: e.activation(sgn.t[:, :], hf.t[:, :], AF.Sigmoid, scale=-1.0), reads=[hf.b], writes=[sgn.b])
            S.op("act", lambda e: e.activation(sq.t[:, :], hq.t[:, :], AF.Silu), reads=[hq.b], writes=[sq.b])
            S.op("act", lambda e: e.activation(gtm.t[:, :, :], gtm.t[:, :, :], AF.Silu), reads=[gtm.b], writes=[gtm.b])
            S.op("dve", lambda e: e.tensor_scalar(ff.t[:, :], sg.t[:, :], oml_ap, lb_ap, ALU.mult, ALU.add), reads=[sg.b, sm.b], writes=[ff.b])
            S.op("dve", lambda e: e.tensor_scalar_max(ff.t[:, :], ff.t[:, :], TINY), reads=[ff.b], writes=[ff.b])
            S.op("act", lambda e: e.activation(ff.t[:, :], ff.t[:, :], AF.Ln), reads=[ff.b], writes=[ff.b])
            S.op("dve", lambda e: e.tensor_tensor_scan(bb.t[:, :], mask.t[:, :], ff.t[:, :], 0.0, ALU.mult, ALU.add), reads=[mask.b, ff.b], writes=[bb.b])
            S.op("dve", lambda e: e.tensor_scalar(sgn.t[:, :], sgn.t[:, :], oml_ap, None, ALU.mult), reads=[sgn.b, sm.b], writes=[sgn.b])
            for n in range(NCH):
                c0 = n * L
                mid = bb.t[:, c0 + L // 2 - 1:c0 + L // 2]
                last = bb.t[:, c0 + L - 1:c0 + L]
                S.op("dve", lambda e, n=n, mid=mid: e.tensor_scalar(sc.t[:, n:n + 1], mid, -1.0, None, ALU.mult), reads=[bb.b], writes=[sc.b])
                S.op("act", lambda e, n=n, c0=c0: e.activation(eq.t[:, c0:c0 + L], bb.t[:, c0:c0 + L], AF.Exp, bias=sc.t[:, n:n + 1]), reads=[bb.b, sc.b], writes=[eq.b])
                S.op("act", lambda e, n=n, c0=c0, mid=mid: e.activation(ek.t[:, c0:c0 + L], bb.t[:, c0:c0 + L], AF.Exp, bias=mid, scale=-1.0), reads=[bb.b], writes=[ek.b])
                S.op("act", lambda e, n=n, mid=mid: e.activation(sc.t[:, NCH + n:NCH + n + 1], mid, AF.Exp), reads=[bb.b], writes=[sc.b])
                S.op("act", lambda e, n=n, last=last: e.activation(sc.t[:, 2 * NCH + n:2 * NCH + n + 1], last, AF.Exp, bias=sc.t[:, n:n + 1]), reads=[bb.b, sc.b], writes=[sc.b])
                S.op("act", lambda e, n=n, last=last: e.activation(sc.t[:, 3 * NCH + n:3 * NCH + n + 1], last, AF.Exp), reads=[bb.b], writes=[sc.b])
            S.op("dve", lambda e: e.scalar_tensor_tensor(qT.t[:, :], eq.t[:, :], HG_BIG, sq.t[:, :], ALU.min, ALU.mult), reads=[sq.b, eq.b], writes=[qT.b])
            S.op("dve", lambda e: e.scalar_tensor_tensor(kT.t[:, :], ek.t[:, :], HG_BIG, sgn.t[:, :], ALU.min, ALU.mult), reads=[sgn.b, ek.b], writes=[kT.b])
            ps_tr = B.ps_tr
            for n in range(NCH):
                S.op("pe", lambda e, n=n: e.transpose(ps_tr.t[0:L, n * 128:(n + 1) * 128], kT.t[:, n * L:(n + 1) * L], idb.t[:, :]),
                     reads=[kT.b, idb.b], writes=[ps_tr.b], inc=(n == NCH - 1))
            evac(S, B.flip, ktm.t[:, :, :], ps_tr.t[0:L, :].rearrange("p (n k) -> p n k", n=NCH), [ps_tr.b], [ktm.b])

        def chunk_a(b, i, n):
            B = sets[b]
            qT, kT, ktm, vtm = B.qT, B.kT, B.ktm, B.vtm
            c0 = n * L
            pa = B.ps_a
            S.op("pe", lambda e, c0=c0: e.matmul(pa.t[0:L, 0:L], kT.t[:, c0:c0 + L], qT.t[:, c0:c0 + L], start=True, stop=True),
                 reads=[kT.b, qT.b], writes=[pa.b])
            ps_d = B.ps_d
            S.op("pe", lambda e, n=n: e.matmul(ps_d.t[:, 0:128], ktm.t[:, n, :], vtm.t[:, n, :], start=True, stop=True),
                 reads=[ktm.b, vtm.b], writes=[ps_d.b])
            at = B.AT.next()
            B.at_cur = at
            S.op("dve", lambda e, at=at: e.tensor_tensor(at.t[:, :], pa.t[0:L, 0:L], tri.t[0:L, 0:L], ALU.mult), reads=[pa.b, tri.b], writes=[at.b])

        def chunk(b, i, n):
            B = sets[b]
            qT, kT, ktm, vtm, gtm, sc, state, tmp, stbf, junk, ssq, rr, o1 = B.qT, B.kT, B.ktm, B.vtm, B.gtm, B.sc, B.state, B.tmp, B.stbf, B.junk, B.ssq, B.rr, B.o1
            t0 = b * SL + i * TT
            c0 = n * L
            at = B.at_cur
            S.op("dve", lambda e, n=n: e.tensor_scalar(stbf.t[:, :], state.t[:, :], sc.t[:, NCH + n:NCH + n + 1], None, ALU.mult), reads=[state.b, sc.b], writes=[stbf.b])
            po = B.ps_o
            S.op("pe", lambda e, at=at, n=n: e.matmul(po.t[0:L, 0:128], at.t[:, :], vtm.t[:, n, :], start=True, stop=False),
                 reads=[at.b, vtm.b], writes=[po.b], inc=False)
            S.op("pe", lambda e, c0=c0: e.matmul(po.t[0:L, 0:128], qT.t[:, c0:c0 + L], stbf.t[:, :], start=False, stop=True),
                 reads=[qT.b, stbf.b], writes=[po.b])
            ps_d = B.ps_d
            S.op("dve", lambda e, n=n: e.tensor_scalar(tmp.t[:, :], state.t[:, :], sc.t[:, 3 * NCH + n:3 * NCH + n + 1], None, ALU.mult), reads=[state.b, sc.b], writes=[tmp.b])
            S.op("dve", lambda e, n=n: e.scalar_tensor_tensor(state.t[:, :], ps_d.t[:, 0:128], sc.t[:, 2 * NCH + n:2 * NCH + n + 1], tmp.t[:, :], ALU.mult, ALU.add),
                 reads=[ps_d.b, sc.b, tmp.b], writes=[state.b])
            S.op("act", lambda e: e.activation(junk.t[:, :], po.t[0:L, 0:128], AF.Square, accum_out=ssq.t[:, 0:1]), reads=[po.b], writes=[junk.b, ssq.b])
            rstd(S, rr.t[:, 0:1], rr.b, ssq.t[:, 0:1], ssq.b, 1.0 / 128)
            S.op("dve", lambda e: e.scalar_tensor_tensor(o1.t[:, :], po.t[0:L, 0:128], rr.t[:, 0:1], gout.t[0:L, :], ALU.mult, ALU.mult),
                 reads=[po.b, rr.b, gout.b], writes=[o1.b])
            ob = B.osb.next()
            S.op("dve", lambda e, ob=ob, n=n: e.tensor_tensor(ob.t[:, :], o1.t[:, :], gtm.t[:, n, :], ALU.mult), reads=[o1.b, gtm.b], writes=[ob.b])
            S.dma("sp", out[t0 + c0:t0 + c0 + L, ocol:ocol + 128], ob.t[:, :], reads=[ob.b])

        for b in range(nb):
            S.op("pool", lambda e, b=b: e.memset(sets[b].state.t[:, :], 0.0), writes=[sets[b].state.b])
        for i in range(NT):
            for b in range(nb):
                prep(b, i)
            for n in range(NCH):
                for b in range(nb):
                    chunk_a(b, i, n)
                for b in range(nb):
                    chunk(b, i, n)
        S.barrier()


def ca_bias_tile(rel_bias_h):
    ki = np.arange(128)[:, None]
    qi = np.arange(640)[None, :]
    return np.ascontiguousarray(rel_bias_h[np.clip(qi - ki, -256, 256) + 256].astype(np.float32))


TRI = np.ascontiguousarray((np.arange(128)[:, None] <= np.arange(128)[None, :]).astype(np.float32))


def mm_tm_stream(S, lhsT, nK, w_dram, wch_rr, ps_rr, stg_rr, junk, yscr, row0, ssqp, yb, hook=None):
    G = 8
    ngrp = (nK + G - 1) // G
    for j in range(DEBUG.get("nj", 8)):
        acc = [ps_rr.next() for _ in range(4)]
        for g in range(ngrp):
            k0 = g * G
            kn = min(G, nK - k0)
            w = wch_rr.next()
            wv = w.t[:, :].rearrange("p (c n) -> p c n", n=512)
            S.dma("pool", wv[:, 0:kn, :], w_dram[k0 * 128:(k0 + kn) * 128, j * 512:(j + 1) * 512].rearrange("(c p) n -> p c n", p=128), writes=[w.b])
            for s in range(4):
                for k in range(kn):
                    S.op("pe", lambda e, a=acc[s], s=s, k=k, k0=k0, wv=wv, g=g, kn=kn: e.matmul(a.t[:, :], lhsT.t[:, k0 + k, s * 128:(s + 1) * 128], wv[:, k, :], start=(g == 0 and k == 0), stop=(g == ngrp - 1 and k == kn - 1)),
                         reads=[lhsT.b, w.b], writes=[acc[s].b], inc=(k == kn - 1))
            if hook is not None:
                hook()
        for s in range(4):
            sg = stg_rr.next()
            S.op("dve", lambda e, sg=sg, a=acc[s]: e.tensor_copy(sg.t[:, :], a.t[:, :]), reads=[acc[s].b], writes=[sg.b])
            if not DEBUG.get("nosq"):
                S.op("act", lambda e, sg=sg, s=s, j=j: e.activation(junk.t[:, 0:512], sg.t[:, :], AF.Square, accum_out=ssqp.t[:, s * 8 + j:s * 8 + j + 1]),
                     reads=[sg.b], writes=[junk.b, ssqp.b])
            if not DEBUG.get("nostore"):
                S.dma("sp", yscr[row0 + s * 128:row0 + (s + 1) * 128, j * 512:(j + 1) * 512], sg.t[:, :], reads=[sg.b], writes=[yb[s][j]])


def tail(S, yscr, row0, resid, outp, ssqp, yb, gpost, tl_rr, rt, pe_id):
    for s in range(4):
        r0 = row0 + s * 128
        S.op("dve", lambda e, s=s: e.tensor_reduce(rt.t[:, 0:1], ssqp.t[:, s * 8:(s + 1) * 8], mybir.AxisListType.X, ALU.add), reads=[ssqp.b], writes=[rt.b])
        rstd(S, rt.t[:, 0:1], rt.b, rt.t[:, 0:1], rt.b, 1.0 / D_MODEL)
        for j in range(8):
            ybk = tl_rr.next()
            xbk = tl_rr.next()
            S.dma("sp", ybk.t[:, :], yscr[r0:r0 + 128, j * 512:(j + 1) * 512], reads=[yb[s][j]], writes=[ybk.b])
            S.dma("sp", xbk.t[:, :], resid[r0:r0 + 128, j * 512:(j + 1) * 512], writes=[xbk.b])
            S.op("dve", lambda e, ybk=ybk, j=j: e.scalar_tensor_tensor(ybk.t[:, :], ybk.t[:, :], rt.t[:, 0:1], gpost.t[:, j * 512:(j + 1) * 512], ALU.mult, ALU.mult),
                 reads=[ybk.b, rt.b, gpost.b], writes=[ybk.b])
            S.op("pool", lambda e, ybk=ybk, xbk=xbk: e.tensor_tensor(xbk.t[:, :], ybk.t[:, :], xbk.t[:, :], ALU.add), reads=[ybk.b, xbk.b], writes=[xbk.b])
            S.dma("sp", outp[r0:r0 + 128, j * 512:(j + 1) * 512], xbk.t[:, :], reads=[xbk.b])


def phase_C(nc, S, T, o, x, gpc_d, w_out, gpost_d, ident_d, yscr, x1):
    TT = 512
    NC = D_MODEL // 128
    groups = [(0, 2048, True), (2048, 3072, False), (3072, 4096, True)]
    with ExitStack() as st:
        cx = Ctx(nc, st)
        ident = cx.sb("ident", [128, 128], F32)
        gpc = cx.sb("gpc", [128, NC], F32)
        gpost = cx.sb("gpost", [128, D_MODEL], F32)
        S.dma("sp", ident.t[:, :], ident_d, writes=[ident.b])
        S.dma("sp", gpc.t[:, :], gpc_d, writes=[gpc.b])
        S.dma("sp", gpost.t[:, :], gpost_d, writes=[gpost.b])
        ps_t = RR([cx.ps("ps_t", [128, 512]) for _ in range(2)])
        ps_m = RR([cx.ps("ps_m", [128, 512]) for _ in range(6)])
        fr = Front(S, cx, D_MODEL, ident, gpc, ps_t, nbuf=2)
        oTs = [cx.sb("oT", [128, NC, TT], BF16) for _ in range(2)]
        wch = RR([cx.sb("wch", [128, 4096], BF16) for _ in range(3)])
        stg = RR([cx.sb("stg", [128, 512], F32) for _ in range(2)])
        tl = RR([cx.sb("tl", [128, 512], F32) for _ in range(4)])
        ssqp = cx.sb("ssqp", [128, 32], F32)
        rt = cx.sb("rt", [128, 1], F32)
        NTT = T // TT

        def front_all(tt):
            for s in range(4):
                yield from fr.run_gen(o[tt * TT + s * 128: tt * TT + (s + 1) * 128, :], s, oTs[tt % 2], groups=groups)

        for _ in front_all(0):
            pass
        for tt in range(NTT):
            yb = [[Buf() for _ in range(8)] for _ in range(4)]
            nxt = front_all(tt + 1) if tt + 1 < NTT else None

            def hook(nxt=nxt):
                if nxt is not None:
                    next(nxt, None)
                    next(nxt, None)
            mm_tm_stream(S, oTs[tt % 2], NC, w_out, wch, ps_m, stg, fr.junk, yscr, tt * TT, ssqp, yb, hook=hook)
            if nxt is not None:
                for _ in nxt:
                    pass
            tail(S, yscr, tt * TT, x, x1, ssqp, yb, gpost, tl, rt, None)
        S.barrier()


def phase_D(nc, S, T, x1, gpc_d, wg, wu, wd, gpost_d, ident_d, yscr, x2, dff=D_FF):
    TT = 512
    NC = D_MODEL // 128
    NF = dff // 128
    with ExitStack() as st:
        cx = Ctx(nc, st)
        ident = cx.sb("ident", [128, 128], F32)
        gpc = cx.sb("gpc", [128, NC], F32)
        gpost = cx.sb("gpost", [128, D_MODEL], F32)
        S.dma("sp", ident.t[:, :], ident_d, writes=[ident.b])
        S.dma("sp", gpc.t[:, :], gpc_d, writes=[gpc.b])
        S.dma("sp", gpost.t[:, :], gpost_d, writes=[gpost.b])
        ps_t = RR([cx.ps("ps_t", [128, 512]) for _ in range(2)])
        ps_m = RR([cx.ps("ps_m", [128, 512]) for _ in range(6)])
        fr = Front(S, cx, D_MODEL, ident, gpc, ps_t, nbuf=1)
        hT = cx.sb("hT", [128, NC, TT], BF16)
        actT = cx.sb("actT", [128, NF, TT], BF16)
        wch = RR([cx.sb("wch", [128, 4096], BF16) for _ in range(3)])
        stg = RR([cx.sb("stg", [128, 512], F32) for _ in range(2)])
        sgl = RR([cx.sb("sgl", [128, 512], F32) for _ in range(2)])
        tl = RR([cx.sb("tl", [128, 512], F32) for _ in range(4)])
        ssqp = cx.sb("ssqp", [128, 32], F32)
        rt = cx.sb("rt", [128, 1], F32)
        NTT = T // TT

        def front_all(tt):
            for s in range(4):
                yield from fr.run_gen(x1[tt * TT + s * 128: tt * TT + (s + 1) * 128, :], s, hT)

        for _ in front_all(0):
            pass
        for tt in range(NTT):
            yb = [[Buf() for _ in range(8)] for _ in range(4)]
            for f in range(NF):
                pp = []
                for wmat in (wg, wu):
                    w = wch.next()
                    wv = w.t[:, :].rearrange("p (c n) -> p c n", n=128)
                    S.dma("pool", wv[:, :, :], wmat[:, f * 128:(f + 1) * 128].rearrange("(c p) n -> p c n", p=128), writes=[w.b])
                    pt = ps_m.next()
                    for c in range(NC):
                        S.op("pe", lambda e, c=c, pt=pt, wv=wv: e.matmul(pt.t[:, :], wv[:, c, :], hT.t[:, c, :], start=(c == 0), stop=(c == NC - 1)),
                             reads=[w.b, hT.b], writes=[pt.b], inc=(c == NC - 1))
                    pp.append(pt)
                sg = sgl.next()
                S.op("act", lambda e, sg=sg, pg=pp[0]: e.activation(sg.t[:, :], pg.t[:, :], AF.Silu), reads=[pp[0].b], writes=[sg.b])
                S.op("dve", lambda e, sg=sg, pu=pp[1], f=f: e.tensor_tensor(actT.t[:, f, :], sg.t[:, :], pu.t[:, :], ALU.mult), reads=[sg.b, pp[1].b], writes=[actT.b])
            nxt = front_all(tt + 1) if tt + 1 < NTT else None

            def hook(nxt=nxt):
                if nxt is not None:
                    next(nxt, None)
            mm_tm_stream(S, actT, NF, wd, wch, ps_m, stg, fr.junk, yscr, tt * TT, ssqp, yb, hook=hook)
            if nxt is not None:
                for _ in nxt:
                    pass
            tail(S, yscr, tt * TT, x1, x2, ssqp, yb, gpost, tl, rt, None)
        S.barrier()


def build_B(layer, nb=BATCH, SL=SEQ):
    NTOK = nb * SL
    nc = bass.Bass("TRN2", target_bir_lowering=False)
    d = lambda n, s, t=F32: nc.dram_tensor(n, s, t, kind="ExternalInput").ap()
    lat = d("lat", [1408, NTOK])
    pos64 = d("pos64", [64, NTOK], I32)
    wuq = d("wuq", [768, 512])
    wukv = d("wukv", [512, 512])
    gq = d("gq", [128, 6])
    gkv = d("gkv", [128, 4])
    rc = d("rc", [64, 2])
    hqf = d("hqf", [256, NTOK])
    hig = d("hig", [NTOK, 256])
    lbr = d("lbr", [128, DEPTH])
    gout = d("gout", [128, 128])
    tri = d("tri", [128, 128])
    ident = d("ident", [128, 128])
    aqk = d("aqk", [256, NTOK])
    av = d("av", [NTOK, 128])
    biasT = d("biasT", [128, 640])
    out = nc.dram_tensor("out", [NTOK, 512], F32, kind="ExternalOutput").ap()
    with ExitStack() as st:
        S = Sched(nc, st)
        only = DEBUG.get("only")
        if only in (None, "mla"):
            phase_B_mla(nc, S, nb, SL, lat[0:768, :], lat[768:1280, :], lat[1280:1408, :], pos64, wuq, wukv, gq, gkv, rc, out, 0)
        if only in (None, "hg"):
            phase_B_hg(nc, S, nb, SL, layer, hqf[0:128, :], hqf[128:256, :], hig[:, 0:128], hig[:, 128:256], lbr, gout, tri, ident, out, 256)
        if only in (None, "ca"):
            phase_B_ca(nc, S, nb, SL, aqk[0:128, :], aqk[128:256, :], av, biasT, out, 384)
        S.finish()
        S.emit()
    return nc


def build_CD(T, with_A):
    nc = bass.Bass("TRN2", target_bir_lowering=False)
    d = lambda n, s, t=F32: nc.dram_tensor(n, s, t, kind="ExternalInput").ap()
    o = d("o", [T, D_MODEL])
    x = d("x", [T, D_MODEL])
    gpc_o = d("gpc_o", [128, 32])
    w_out = d("w_out", [D_MODEL, D_MODEL])
    gpost_a = d("gpost_a", [128, D_MODEL])
    gpc_f = d("gpc_f", [128, 32])
    wg = d("wg", [D_MODEL, D_FF])
    wu = d("wu", [D_MODEL, D_FF])
    wd = d("wd", [D_FF, D_MODEL])
    gpost_f = d("gpost_f", [128, D_MODEL])
    ident = d("ident", [128, 128])
    yscr = nc.dram_tensor("yscr", [T, D_MODEL], F32, kind="Internal").ap()
    x1 = nc.dram_tensor("x1", [T, D_MODEL], F32, kind="Internal").ap()
    x2 = nc.dram_tensor("x2", [T, D_MODEL], F32, kind="ExternalOutput").ap()
    if with_A:
        gpc_n = d("gpc_n", [128, 32])
        wfm = d("wfm", [D_MODEL, NFM])
        wtm = d("wtm", [D_MODEL, NTM])
        hT = nc.dram_tensor("hT", [NFM, T], F32, kind="ExternalOutput").ap()
        htm = nc.dram_tensor("htm", [T, NTM], F32, kind="ExternalOutput").ap()
    with ExitStack() as st:
        S = Sched(nc, st)
        phase_C(nc, S, T, o, x, gpc_o, w_out, gpost_a, ident, yscr, x1)
        phase_D(nc, S, T, x1, gpc_f, wg, wu, wd, gpost_f, ident, yscr, x2)
        if with_A:
            phase_A(nc, S, T, x2, gpc_n, wfm, wtm, ident, hT, htm)
        S.finish()
        S.emit()
    return nc


def _run(nc, in_maps):
    res = run_bass_kernel_spmd(nc, in_maps, core_ids=list(range(NCORES)))
    return res.results


def kernel_unfused(x, positions, attn_pre_norm, attn_post_norm, w_in, mla_q_norm, mla_kv_norm, w_uq, w_ukv,
           mla_out_norm, hg_lower_bounds, hg_out_norm, ca_rel_bias, ca_out_norm, w_out, ffn_pre_norm,
           ffn_post_norm, w_gate, w_up, w_down):
    f32 = lambda a: np.ascontiguousarray(np.asarray(a, dtype=np.float32))
    x = f32(x)
    NTOK = BATCH * SEQ
    T = NTOK // NCORES
    xs = x.reshape(NTOK, D_MODEL)
    pos64 = np.ascontiguousarray(np.broadcast_to(np.asarray(positions, np.int32).reshape(1, NTOK), (64, NTOK)))
    rc = rope_consts()
    ones1024 = np.ones(1024, np.float32)
    wsplit = [split_w_in(f32(w_in[l])) for l in range(DEPTH)]
    xcur = [np.ascontiguousarray(xs[c * T:(c + 1) * T]) for c in range(NCORES)]

    ncA = build_A(T)
    resA = _run(ncA, [{"x": xcur[c], "gpc": gain_pc(attn_pre_norm[0]), "wfm": wsplit[0][0], "wtm": wsplit[0][1], "ident": IDENT}
                      for c in range(NCORES)])
    hT_parts = [r["hT"] for r in resA]
    htm_parts = [r["htm"] for r in resA]
    for l in range(DEPTH):
        hT_all = np.concatenate(hT_parts, axis=1)
        htm_all = np.concatenate(htm_parts, axis=0)
        del hT_parts, htm_parts
        lat = np.ascontiguousarray(hT_all[0:1408])
        ncB = build_B(l)
        in_maps = []
        for c in range(NCORES):
            wuq_c, wukv_c = mla_weights_for_core(f32(w_uq[l]), f32(w_ukv[l]), [2 * c, 2 * c + 1])
            hs = slice(c * 128, (c + 1) * 128)
            in_maps.append({
                "lat": lat, "pos64": pos64, "wuq": wuq_c, "wukv": wukv_c,
                "gq": gain_pc(mla_q_norm[l]), "gkv": gain_pc(mla_kv_norm[l]), "rc": rc,
                "hqf": np.ascontiguousarray(np.concatenate([hT_all[1408 + c * 128:1408 + (c + 1) * 128], hT_all[2432 + c * 128:2432 + (c + 1) * 128]], 0)),
                "hig": np.ascontiguousarray(np.concatenate([htm_all[:, hs], htm_all[:, 1024 + c * 128:1024 + (c + 1) * 128]], 1)),
                "lbr": np.ascontiguousarray(f32(hg_lower_bounds)[:, hs].T),
                "gout": bcast128(f32(hg_out_norm[l])[hs]), "tri": TRI, "ident": IDENT,
                "aqk": np.ascontiguousarray(np.concatenate([hT_all[3456 + c * 128:3456 + (c + 1) * 128], hT_all[4480 + c * 128:4480 + (c + 1) * 128]], 0)),
                "av": np.ascontiguousarray(htm_all[:, 2048 + c * 128:2048 + (c + 1) * 128]),
                "biasT": ca_bias_tile(f32(ca_rel_bias[l])[c]),
            })
        del hT_all, htm_all
        resB = _run(ncB, in_maps)
        del in_maps, lat
        o_all = np.empty((NTOK, D_MODEL), np.float32)
        for c in range(NCORES):
            oc = resB[c]["out"]
            o_all[:, 2 * c * 128:(2 * c + 2) * 128] = oc[:, 0:256]
            o_all[:, 2048 + c * 128:2048 + (c + 1) * 128] = oc[:, 256:384]
            o_all[:, 3072 + c * 128:3072 + (c + 1) * 128] = oc[:, 384:512]
        del resB
        last = (l == DEPTH - 1)
        ncCD = build_CD(T, with_A=not last)
        gpc_o = gain_pc(np.concatenate([f32(mla_out_norm[l]), ones1024, f32(ca_out_norm[l])]))
        base = {"gpc_o": gpc_o, "w_out": f32(w_out[l]), "gpost_a": bcast128(f32(attn_post_norm[l])),
                "gpc_f": gain_pc(ffn_pre_norm[l]), "wg": f32(w_gate[l]), "wu": f32(w_up[l]), "wd": f32(w_down[l]),
                "gpost_f": bcast128(f32(ffn_post_norm[l])), "ident": IDENT}
        if not last:
            base.update({"gpc_n": gain_pc(attn_pre_norm[l + 1]), "wfm": wsplit[l + 1][0], "wtm": wsplit[l + 1][1]})
        in_maps = []
        for c in range(NCORES):
            m = dict(base)
            m["o"] = np.ascontiguousarray(o_all[c * T:(c + 1) * T])
            m["x"] = xcur[c]
            in_maps.append(m)
        del o_all
        resC = _run(ncCD, in_maps)
        del in_maps
        xcur = [r["x2"] for r in resC]
        if not last:
            hT_parts = [r["hT"] for r in resC]
            htm_parts = [r["htm"] for r in resC]
        del resC
    out = np.concatenate(xcur, axis=0).reshape(BATCH, SEQ, D_MODEL).astype(np.float32)
    return out


def build_fused(SL=SEQ):
    T = SL
    nc = bass.Bass("TRN2", target_bir_lowering=False)
    d = lambda n, s, t=F32: nc.dram_tensor(n, s, t, kind="ExternalInput").ap()
    x = d("x", [T, D_MODEL])
    pos64 = d("pos64", [64, T], I32)
    rc = d("rc", [64, 2])
    tri = d("tri", [128, 128])
    ident = d("ident", [128, 128])
    L = []
    for l in range(DEPTH):
        p = "l%d_" % l
        L.append(dict(
            gpc_pre=d(p + "gpc_pre", [128, 32]), wfm=d(p + "wfm", [D_MODEL, NFM]), wtm=d(p + "wtm", [D_MODEL, NTM]),
            wuq=d(p + "wuq", [8 * 768, 512]), wukv=d(p + "wukv", [8 * 512, 512]), gq=d(p + "gq", [128, 6]), gkv=d(p + "gkv", [128, 4]),
            lbr=d(p + "lbr", [8 * 128, DEPTH]), gout=d(p + "gout", [8 * 128, 128]), biasT=d(p + "biasT", [8 * 128, 640]),
            gpc_o=d(p + "gpc_o", [128, 32]), w_out=d(p + "w_out", [D_MODEL, D_MODEL]), gpost_a=d(p + "gpost_a", [128, D_MODEL]),
            gpc_f=d(p + "gpc_f", [128, 32]), wg=d(p + "wg", [D_MODEL, D_FF]), wu=d(p + "wu", [D_MODEL, D_FF]), wd=d(p + "wd", [D_FF, D_MODEL]),
            gpost_f=d(p + "gpost_f", [128, D_MODEL])))
    scr = lambda n, s: nc.dram_tensor(n, s, F32, kind="Internal").ap()
    hT = scr("hT", [NFM, T])
    htm = scr("htm", [T, NTM])
    o = scr("o_scr", [T, D_MODEL])
    yscr = scr("yscr", [T, D_MODEL])
    x1 = scr("x1", [T, D_MODEL])
    xmid = scr("xmid", [T, D_MODEL])
    xout = nc.dram_tensor("xout", [T, D_MODEL], F32, kind="ExternalOutput").ap()
    with ExitStack() as st:
        S = Sched(nc, st)
        xin = x
        for l in range(DEPTH):
            W = L[l]
            phase_A(nc, S, T, xin, W["gpc_pre"], W["wfm"], W["wtm"], ident, hT, htm)
            for hp in range(8):
                phase_B_mla(nc, S, 1, SL, hT[0:768, :], hT[768:1280, :], hT[1280:1408, :], pos64,
                            W["wuq"][hp * 768:(hp + 1) * 768, :], W["wukv"][hp * 512:(hp + 1) * 512, :], W["gq"], W["gkv"], rc, o, hp * 256)
            for h in range(8):
                hs = slice(h * 128, (h + 1) * 128)
                phase_B_hg(nc, S, 1, SL, l, hT[1408 + h * 128:1408 + (h + 1) * 128, :], hT[2432 + h * 128:2432 + (h + 1) * 128, :],
                           htm[:, hs], htm[:, 1024 + h * 128:1024 + (h + 1) * 128], W["lbr"][hs, :], W["gout"][hs, :], tri, ident, o, 2048 + h * 128)
            for h in range(8):
                hs = slice(h * 128, (h + 1) * 128)
                phase_B_ca(nc, S, 1, SL, hT[3456 + h * 128:3456 + (h + 1) * 128, :], hT[4480 + h * 128:4480 + (h + 1) * 128, :],
                           htm[:, 2048 + h * 128:2048 + (h + 1) * 128], W["biasT"][hs, :], o, 3072 + h * 128)
            phase_C(nc, S, T, o, xin, W["gpc_o"], W["w_out"], W["gpost_a"], ident, yscr, x1)
            xnext = xout if l == DEPTH - 1 else xmid
            phase_D(nc, S, T, x1, W["gpc_f"], W["wg"], W["wu"], W["wd"], W["gpost_f"], ident, yscr, xnext)
            xin = xnext
        S.finish()
        S.emit()
        print("fused program: %d scheduled ops" % S.ninst, flush=True)
    return nc


def fused_inputs(b, x, positions, attn_pre_norm, attn_post_norm, w_in, mla_q_norm, mla_kv_norm, w_uq, w_ukv,
                 mla_out_norm, hg_lower_bounds, hg_out_norm, ca_rel_bias, ca_out_norm, w_out, ffn_pre_norm,
                 ffn_post_norm, w_gate, w_up, w_down, cache):
    f32 = lambda a: np.ascontiguousarray(np.asarray(a, dtype=np.float32))
    m = {"x": f32(x[b]), "pos64": np.ascontiguousarray(np.broadcast_to(np.asarray(positions[b], np.int32)[None, :], (64, SEQ))),
         "rc": rope_consts(), "tri": TRI, "ident": IDENT}
    if "w" not in cache:
        w = {}
        ones1024 = np.ones(1024, np.float32)
        for l in range(DEPTH):
            p = "l%d_" % l
            wfm, wtm = split_w_in(f32(w_in[l]))
            wq, wk = [], []
            for hp in range(8):
                a, bb = mla_weights_for_core(f32(w_uq[l]), f32(w_ukv[l]), [2 * hp, 2 * hp + 1])
                wq.append(a)
                wk.append(bb)
            w.update({
                p + "gpc_pre": gain_pc(attn_pre_norm[l]), p + "wfm": wfm, p + "wtm": wtm,
                p + "wuq": np.ascontiguousarray(np.concatenate(wq, 0)), p + "wukv": np.ascontiguousarray(np.concatenate(wk, 0)),
                p + "gq": gain_pc(mla_q_norm[l]), p + "gkv": gain_pc(mla_kv_norm[l]),
                p + "lbr": np.ascontiguousarray(f32(hg_lower_bounds).T),
                p + "gout": np.ascontiguousarray(np.concatenate([bcast128(f32(hg_out_norm[l])[h * 128:(h + 1) * 128]) for h in range(8)], 0)),
                p + "biasT": np.ascontiguousarray(np.concatenate([ca_bias_tile(f32(ca_rel_bias[l])[h]) for h in range(8)], 0)),
                p + "gpc_o": gain_pc(np.concatenate([f32(mla_out_norm[l]), ones1024, f32(ca_out_norm[l])])),
                p + "w_out": f32(w_out[l]), p + "gpost_a": bcast128(f32(attn_post_norm[l])),
                p + "gpc_f": gain_pc(ffn_pre_norm[l]), p + "wg": f32(w_gate[l]), p + "wu": f32(w_up[l]), p + "wd": f32(w_down[l]),
                p + "gpost_f": bcast128(f32(ffn_post_norm[l]))})
        cache["w"] = w
    m.update(cache["w"])
    return m


def kernel_fused(**inputs):
    nc = build_fused()
    cache = {}
    maps = [fused_inputs(c // 4, cache=cache, **inputs) for c in range(NCORES)]
    res = run_bass_kernel_spmd(nc, maps, core_ids=list(range(NCORES)))
    out = np.stack([res.results[0]["xout"], res.results[4]["xout"]], 0).reshape(BATCH, SEQ, D_MODEL)
    return out.astype(np.float32)


def kernel(**inputs):
    return kernel_unfused(**inputs)
```
